# Optimizing a Trainium2 kernel written in Bass

```python
import math
import jax
import jax.numpy as jnp
from jax import lax
import numpy as np

D_MODEL = 1024
BATCH = 8
SEQ = 2048
DEPTH = 4

N_MIXERS = 4
GROUP_WIDTH = D_MODEL // N_MIXERS
HEAD_DIM = 64
GROUP_HEADS = GROUP_WIDTH // HEAD_DIM
MIX_WIDTH = N_MIXERS * GROUP_WIDTH
DILATED_PATTERNS = ((128, 1), (512, 4), (2048, 16))
MLA_Q_RANK = 384
MLA_KV_RANK = 256
MLA_NOPE_DIM = 64
MLA_ROPE_DIM = 32
MLA_V_DIM = HEAD_DIM
DIFF_QK_DIM = HEAD_DIM // 2
DIFF_EPS = 1e-5
GRID_W = 64
NA_WIN_ROWS = 8
NA_WIN_COLS = 16
D_FF = 2816
CONV_WIDTH = 3
PLE_DIM = 256

ROPE_THETA = 10000.0
NORM_EPS = 1e-6
Q_BLOCK = 128
NEG_INF = -1e30

A_COLS = 3 * GROUP_WIDTH
MLA_COLS = MLA_Q_RANK + MLA_KV_RANK + MLA_ROPE_DIM
DIFF_COLS = 2 * (GROUP_HEADS * 2 * DIFF_QK_DIM) + GROUP_WIDTH
NA_COLS = 3 * GROUP_WIDTH
IN_COLS = A_COLS + MLA_COLS + DIFF_COLS + NA_COLS

kernel_name = 'hybrid_parallel_head_group_encoder'


def rms_norm(x, g, eps=NORM_EPS):
    xf = x.astype(jnp.float32)
    y = xf * lax.rsqrt(jnp.mean(xf * xf, axis=-1, keepdims=True) + eps)
    return (y * g.astype(jnp.float32)).astype(x.dtype)


def rope(x, pos):
    d = x.shape[-1]
    inv = jnp.power(ROPE_THETA, -jnp.arange(0, d, 2, dtype=jnp.float32) / d)
    ang = pos.astype(jnp.float32)[:, None] * inv[None, :]
    cos = jnp.cos(ang).astype(x.dtype)
    sin = jnp.sin(ang).astype(x.dtype)
    x1, x2 = jnp.split(x, 2, axis=-1)
    return jnp.concatenate([x1 * cos - x2 * sin, x1 * sin + x2 * cos], axis=-1)


def split_heads(t, n):
    b, s, _ = t.shape
    return t.reshape(b, s, n, -1).transpose(0, 2, 1, 3)


def merge_heads(t):
    b, h, s, d = t.shape
    return t.transpose(0, 2, 1, 3).reshape(b, s, h * d)


def softmax_f32(s):
    return jax.nn.softmax(s.astype(jnp.float32), axis=-1)


def block_sweep(f, *qs):
    b, h, s, _ = qs[0].shape
    nb = s // Q_BLOCK
    blocks = tuple(jnp.moveaxis(q.reshape(b, h, nb, Q_BLOCK, q.shape[-1]), 2, 0) for q in qs)
    out = lax.map(lambda args: f(*args), blocks)
    out = jnp.moveaxis(out, 0, 2)
    return out.reshape(b, h, s, out.shape[-1])


def banded_attn(q, k, v, radius):
    lead = q.shape[:-2]
    L, d = q.shape[-2:]
    nb = -(-L // radius)
    lp = nb * radius
    zeros = [(0, 0)] * len(lead)
    qb = jnp.pad(q, zeros + [(0, lp - L), (0, 0)]).reshape(*lead, nb, radius, d)

    def windows(t):
        tb = jnp.pad(t, zeros + [(radius, lp - L + radius), (0, 0)])
        tb = tb.reshape(*lead, nb + 2, radius, t.shape[-1])
        return jnp.concatenate([tb[..., :-2, :, :], tb[..., 1:-1, :, :], tb[..., 2:, :, :]], axis=-2)

    kw, vw = windows(k), windows(v)
    s = jnp.einsum('...nqd,...nkd->...nqk', qb, kw).astype(jnp.float32) * (d ** -0.5)
    blk = jnp.arange(nb)[:, None, None] * radius
    qpos = blk + jnp.arange(radius)[None, :, None]
    kpos = blk - radius + jnp.arange(3 * radius)[None, None, :]
    mask = (jnp.abs(kpos - qpos) <= radius) & (kpos >= 0) & (kpos < L)
    s = jnp.where(mask, s, NEG_INF)
    lse = jax.nn.logsumexp(s, axis=-1)
    prob = jnp.exp(s - lse[..., None])
    out = jnp.einsum('...nqk,...nkd->...nqd', prob.astype(v.dtype), vw)
    return out.reshape(*lead, lp, d)[..., :L, :], lse.reshape(*lead, lp)[..., :L]


def dilated_mixture(q, k, v):
    b, h, s, d = q.shape
    outs, lses = [], []
    for window, dil in DILATED_PATTERNS:
        radius = window // (2 * dil)

        def to_residue(t):
            return t.reshape(b, h, s // dil, dil, t.shape[-1]).swapaxes(2, 3)

        o, l = banded_attn(to_residue(q), to_residue(k), to_residue(v), radius)
        outs.append(o.swapaxes(2, 3).reshape(b, h, s, d))
        lses.append(l.swapaxes(2, 3).reshape(b, h, s))
    wts = jax.nn.softmax(jnp.stack(lses), axis=0)
    out = jnp.sum(wts[..., None] * jnp.stack(outs).astype(jnp.float32), axis=0)
    return out.astype(q.dtype)


def mla_mixer(cols, q_norm, w_uq, kv_norm, w_ukv, pos):
    c_q, c_kv, k_rope = jnp.split(cols, [MLA_Q_RANK, MLA_Q_RANK + MLA_KV_RANK], axis=-1)
    q = split_heads(rms_norm(c_q, q_norm) @ w_uq, GROUP_HEADS)
    q_nope, q_rot = jnp.split(q, [MLA_NOPE_DIM], axis=-1)
    q = jnp.concatenate([q_nope, rope(q_rot, pos)], axis=-1)
    kv = split_heads(rms_norm(c_kv, kv_norm) @ w_ukv, GROUP_HEADS)
    k_nope, v = jnp.split(kv, [MLA_NOPE_DIM], axis=-1)
    k_r = rope(k_rope[:, None], pos)
    k = jnp.concatenate([k_nope, jnp.broadcast_to(k_r, k_nope.shape[:-1] + (MLA_ROPE_DIM,))], axis=-1)
    scale = (MLA_NOPE_DIM + MLA_ROPE_DIM) ** -0.5

    def attend(qb):
        pr = softmax_f32(jnp.einsum('bhqd,bhkd->bhqk', qb, k).astype(jnp.float32) * scale)
        return jnp.einsum('bhqk,bhkd->bhqd', pr.astype(v.dtype), v)

    return block_sweep(attend, q)


def diff_mixer(cols, lam_q1, lam_k1, lam_q2, lam_k2, subln, lam_init, pos):
    qk_w = GROUP_HEADS * 2 * DIFF_QK_DIM
    q, k, v = jnp.split(cols, [qk_w, 2 * qk_w], axis=-1)
    b, s, _ = q.shape
    q = rope(split_heads(q, 2 * GROUP_HEADS), pos).reshape(b, GROUP_HEADS, 2, s, DIFF_QK_DIM)
    k = rope(split_heads(k, 2 * GROUP_HEADS), pos).reshape(b, GROUP_HEADS, 2, s, DIFF_QK_DIM)
    v = split_heads(v, GROUP_HEADS)
    q1, q2 = q[:, :, 0], q[:, :, 1]
    k1, k2 = k[:, :, 0], k[:, :, 1]
    lam = (jnp.exp(jnp.sum(lam_q1.astype(jnp.float32) * lam_k1.astype(jnp.float32)))
           - jnp.exp(jnp.sum(lam_q2.astype(jnp.float32) * lam_k2.astype(jnp.float32))) + lam_init)
    scale = DIFF_QK_DIM ** -0.5

    def attend(q1b, q2b):
        p1 = softmax_f32(jnp.einsum('bhqd,bhkd->bhqk', q1b, k1).astype(jnp.float32) * scale)
        p2 = softmax_f32(jnp.einsum('bhqd,bhkd->bhqk', q2b, k2).astype(jnp.float32) * scale)
        return jnp.einsum('bhqk,bhkd->bhqd', (p1 - lam * p2).astype(v.dtype), v)

    o = block_sweep(attend, q1, q2)
    return rms_norm(o, subln, DIFF_EPS) * (1.0 - lam_init)


def neighbourhood_attn(q, k, v, rpb):
    b, h, s, d = q.shape
    rows = s // GRID_W
    kr = min(NA_WIN_ROWS, rows)
    q = q.reshape(b, h, rows, GRID_W, d)
    k = k.reshape(b, h, rows, GRID_W, d)
    v = v.reshape(b, h, rows, GRID_W, d)
    r = jnp.arange(rows)
    key_rows = jnp.clip(r - kr // 2, 0, rows - kr)[:, None] + jnp.arange(kr)[None, :]
    kg = k[:, :, key_rows]
    vg = v[:, :, key_rows]
    sc = jnp.einsum('bhrqd,bhrjkd->bhrqjk', q, kg).astype(jnp.float32) * (d ** -0.5)
    c = jnp.arange(GRID_W)
    c_start = jnp.clip(c - NA_WIN_COLS // 2, 0, GRID_W - NA_WIN_COLS)
    col_ok = (c[None, :] >= c_start[:, None]) & (c[None, :] < c_start[:, None] + NA_WIN_COLS)
    dr = key_rows - r[:, None]
    dc = jnp.clip(c[None, :] - c[:, None], -(NA_WIN_COLS - 1), NA_WIN_COLS - 1)
    idx_r = dr[:, None, :, None] + (NA_WIN_ROWS - 1)
    idx_c = dc[None, :, None, :] + (NA_WIN_COLS - 1)
    bias = rpb[:, idx_r, idx_c].astype(jnp.float32)
    sc = jnp.where(col_ok[None, None, None, :, None, :], sc + bias[None], NEG_INF)
    pr = softmax_f32(sc.reshape(b, h, rows, GRID_W, kr * GRID_W)).reshape(sc.shape)
    o = jnp.einsum('bhrqjk,bhrjkd->bhrqd', pr.astype(v.dtype), vg)
    return o.reshape(b, h, s, d)


def dwconv_centred(u, w, bias):
    s = u.shape[1]
    half = CONV_WIDTH // 2
    up = jnp.pad(u, ((0, 0), (half, CONV_WIDTH - 1 - half), (0, 0)))
    return sum(up[:, j:j + s] * w[j] for j in range(CONV_WIDTH)) + bias


def setup_inputs(seed: int = 0) -> dict:
    key = jax.random.key(seed)
    ks = jax.random.split(key, 24)

    def nrm(k, shape, scale):
        return jax.random.normal(k, shape, jnp.float32) * scale

    def gain(k, shape):
        return 1.0 + nrm(k, shape, 0.01)

    return {
        'x': nrm(ks[0], (BATCH, SEQ, D_MODEL), 1.0),
        'p': nrm(ks[1], (DEPTH, BATCH, SEQ, PLE_DIM), 1.0),
        'attn_norm': gain(ks[2], (DEPTH, D_MODEL)),
        'w_in': nrm(ks[3], (DEPTH, D_MODEL, IN_COLS), D_MODEL ** -0.5),
        'mla_q_norm': gain(ks[4], (DEPTH, MLA_Q_RANK)),
        'w_uq': nrm(ks[5], (DEPTH, MLA_Q_RANK, GROUP_HEADS * (MLA_NOPE_DIM + MLA_ROPE_DIM)), MLA_Q_RANK ** -0.5),
        'mla_kv_norm': gain(ks[6], (DEPTH, MLA_KV_RANK)),
        'w_ukv': nrm(ks[7], (DEPTH, MLA_KV_RANK, GROUP_HEADS * (MLA_NOPE_DIM + MLA_V_DIM)), MLA_KV_RANK ** -0.5),
        'lam_q1': nrm(ks[8], (DEPTH, DIFF_QK_DIM), 0.1),
        'lam_k1': nrm(ks[9], (DEPTH, DIFF_QK_DIM), 0.1),
        'lam_q2': nrm(ks[10], (DEPTH, DIFF_QK_DIM), 0.1),
        'lam_k2': nrm(ks[11], (DEPTH, DIFF_QK_DIM), 0.1),
        'diff_subln': gain(ks[12], (DEPTH, HEAD_DIM)),
        'na_rpb': nrm(ks[13], (DEPTH, GROUP_HEADS, 2 * NA_WIN_ROWS - 1, 2 * NA_WIN_COLS - 1), 0.1),
        'w_o': nrm(ks[14], (DEPTH, MIX_WIDTH, D_MODEL), MIX_WIDTH ** -0.5),
        'ffn_norm': gain(ks[15], (DEPTH, D_MODEL)),
        'w_up': nrm(ks[16], (DEPTH, D_MODEL, 2 * D_FF), D_MODEL ** -0.5),
        'conv_w': nrm(ks[17], (DEPTH, CONV_WIDTH, 2 * D_FF), CONV_WIDTH ** -0.5),
        'conv_b': nrm(ks[18], (DEPTH, 2 * D_FF), 0.01),
        'w_down': nrm(ks[19], (DEPTH, D_FF, D_MODEL), D_FF ** -0.5),
        'ple_norm': gain(ks[20], (DEPTH, D_MODEL)),
        'w_ple_gate': nrm(ks[21], (DEPTH, D_MODEL, D_MODEL), D_MODEL ** -0.5),
        'w_ple_proj': nrm(ks[22], (DEPTH, PLE_DIM, D_MODEL), PLE_DIM ** -0.5),
        'final_norm': gain(ks[23], (D_MODEL,)),
    }


def reference(x, p, attn_norm, w_in, mla_q_norm, w_uq, mla_kv_norm, w_ukv, lam_q1, lam_k1, lam_q2, lam_k2,
              diff_subln, na_rpb, w_o, ffn_norm, w_up, conv_w, conv_b, w_down, ple_norm, w_ple_gate,
              w_ple_proj, final_norm):
    s = x.shape[1]
    pos = jnp.arange(s, dtype=jnp.int32)
    split_at = [A_COLS, A_COLS + MLA_COLS, A_COLS + MLA_COLS + DIFF_COLS]
    h = x
    for i in range(DEPTH):
        hn = rms_norm(h, attn_norm[i])
        cols = hn @ w_in[i]
        a_cols, b_cols, c_cols, d_cols = jnp.split(cols, split_at, axis=-1)
        qa, ka, va = (split_heads(t, GROUP_HEADS) for t in jnp.split(a_cols, 3, axis=-1))
        o_a = dilated_mixture(rope(qa, pos), rope(ka, pos), va)
        o_b = mla_mixer(b_cols, mla_q_norm[i], w_uq[i], mla_kv_norm[i], w_ukv[i], pos)
        lam_init = 0.8 - 0.6 * math.exp(-0.3 * i)
        o_c = diff_mixer(c_cols, lam_q1[i], lam_k1[i], lam_q2[i], lam_k2[i], diff_subln[i], lam_init, pos)
        qd, kd, vd = (split_heads(t, GROUP_HEADS) for t in jnp.split(d_cols, 3, axis=-1))
        o_d = neighbourhood_attn(qd, kd, vd, na_rpb[i])
        mix = jnp.concatenate([merge_heads(o) for o in (o_a, o_b, o_c, o_d)], axis=-1)
        h = h + mix @ w_o[i]
        hn = rms_norm(h, ffn_norm[i])
        u = dwconv_centred(hn @ w_up[i], conv_w[i], conv_b[i])
        gate, val = jnp.split(u, 2, axis=-1)
        h = h + (jax.nn.gelu(gate) * val) @ w_down[i]
        e = p[i] @ w_ple_proj[i]
        g = jax.nn.sigmoid(rms_norm(h, ple_norm[i]) @ w_ple_gate[i])
        h = h + g * e
    return rms_norm(h, final_norm)
```

```python
import math
from contextlib import ExitStack
import numpy as np
import concourse.bass as bass
import concourse.mybir as mybir
from concourse.bass_utils import run_bass_kernel_spmd

F32 = mybir.dt.float32
BF16 = mybir.dt.bfloat16
AF = mybir.ActivationFunctionType
ALU = mybir.AluOpType
AX = mybir.AxisListType

FUSED = True
NLAYER = 4
S = 2048
D = 1024
DFF = 2816
NEGB = -30000.0
import os as _os
NFILL = {"A": int(_os.environ.get("NF_A", "1")), "B": int(_os.environ.get("NF_B", "1")), "C": int(_os.environ.get("NF_C", "1"))}
FILLN = int(_os.environ.get("FILLN", "256"))

A_OFF, B_OFF, C_OFF, D_OFF = 0, 768, 1440, 2208


def _sw(n, d):
    idx = np.arange(n)
    return (idx // d) * d + (idx % d + d // 2) % d


WT_SPECS = []


def _build_specs():
    sp = []
    sp.append(("b_cq", 8, 384))
    sp.append(("b_ckv", 8, 320))
    sp.append(("ukv", 2, 512))
    for h in range(4):
        sp.append((f"uq{h}", 3, 128))
    for c in range(2):
        sp.append((f"a_q{c}", 8, 256))
    for c in range(2):
        sp.append((f"a_k{c}", 8, 256))
    sp.append(("a_v", 8, 256))
    for c in range(2):
        sp.append((f"c_q{c}", 8, 256))
    for c in range(2):
        sp.append((f"c_k{c}", 8, 256))
    sp.append(("c_v", 8, 256))
    sp.append(("d_q", 8, 256))
    sp.append(("d_k", 8, 256))
    sp.append(("d_v", 8, 256))
    for g, m in (("a", 384), ("b", 384), ("c", 256)):
        sp.append((f"wo_{g}", 8, m))
    for c in range(22):
        sp.append((f"up{c}", 8, 256))
    for n in range(8):
        sp.append((f"dn{n}", 22, 128))
    for g, m in (("a", 384), ("b", 384), ("c", 256)):
        sp.append((f"pg_{g}", 8, m))
    sp.append(("pp", 2, 1024))
    return sp


WT_SPECS = _build_specs()
WT_OFF = {}
_o = 0
for _n, _kc, _m in WT_SPECS:
    WT_OFF[_n] = (_o, _kc, _m)
    _o += _kc * _m
WTOT = _o
SLOT = 3072
NSLOT = 4

CV = {}
_c = 0
for _n, _w in (("attn_norm", 8), ("ffn_norm", 8), ("ple_norm", 8), ("q_norm", 3), ("kv_norm", 2), ("subln", 1),
               ("cw0", 44), ("cw1", 44), ("cw2", 44), ("cb", 44), ("laminit", 1), ("omli", 1)):
    CV[_n] = _c
    _c += _w
NCL = _c
NCV = NLAYER * NCL + 8

NTW = 19
TM_F0 = 1408
TM_W = 2944


def _T(mat):
    K, M = mat.shape
    kc = K // 128
    return mat.reshape(kc, 128, M).transpose(1, 0, 2).reshape(128, kc * M)


def _layer_tiles(inp, l):
    w_in = inp["w_in"][l]
    w_uq = inp["w_uq"][l]
    w_ukv = inp["w_ukv"][l]
    t = {}
    ar = np.arange
    t["b_cq"] = w_in[:, B_OFF:B_OFF + 384]
    t["b_ckv"] = w_in[:, np.concatenate([B_OFF + 384 + ar(256), B_OFF + 640 + ar(32), B_OFF + 640 + _sw(32, 32)])]
    kn = np.concatenate([h * 128 + ar(64) for h in range(4)])
    vv = np.concatenate([h * 128 + 64 + ar(64) for h in range(4)])
    t["ukv"] = w_ukv[:, np.concatenate([kn, vv])]
    for h in range(4):
        t[f"uq{h}"] = w_uq[:, np.concatenate([h * 96 + ar(96), h * 96 + 64 + _sw(32, 32)])]
    for c in range(2):
        t[f"a_q{c}"] = w_in[:, np.concatenate([A_OFF + c * 128 + ar(128), A_OFF + c * 128 + _sw(128, 64)])]
        t[f"a_k{c}"] = w_in[:, np.concatenate([A_OFF + 256 + c * 128 + ar(128), A_OFF + 256 + c * 128 + _sw(128, 64)])]
        t[f"c_q{c}"] = w_in[:, np.concatenate([C_OFF + c * 128 + ar(128), C_OFF + c * 128 + _sw(128, 32)])]
        t[f"c_k{c}"] = w_in[:, np.concatenate([C_OFF + 256 + c * 128 + ar(128), C_OFF + 256 + c * 128 + _sw(128, 32)])]
    t["a_v"] = w_in[:, A_OFF + 512:A_OFF + 768]
    t["c_v"] = w_in[:, C_OFF + 512:C_OFF + 768]
    t["d_q"] = w_in[:, D_OFF:D_OFF + 256]
    t["d_k"] = w_in[:, D_OFF + 256:D_OFF + 512]
    t["d_v"] = w_in[:, D_OFF + 512:D_OFF + 768]
    for n in range(8):
        t[f"dn{n}"] = inp["w_down"][l][:, n * 128:(n + 1) * 128]
    for g, (c0, c1) in (("a", (0, 384)), ("b", (384, 768)), ("c", (768, 1024))):
        t[f"wo_{g}"] = inp["w_o"][l][:, c0:c1]
        t[f"pg_{g}"] = inp["w_ple_gate"][l][:, c0:c1]
    w_up = inp["w_up"][l]
    for c in range(22):
        t[f"up{c}"] = w_up[:, np.concatenate([c * 128 + ar(128), DFF + c * 128 + ar(128)])]
    t["pp"] = inp["w_ple_proj"][l]
    return t


def _pack_layer(inp, l):
    t = _layer_tiles(inp, l)
    out = np.empty((128, WTOT), np.float32)
    for n, kc, m in WT_SPECS:
        o = WT_OFF[n][0]
        out[:, o:o + kc * m] = _T(np.asarray(t[n], np.float32))
    return out


def _cvec(inp):
    cv = np.zeros((128, NCV), np.float32)

    def put(col, v):
        v = np.asarray(v, np.float32)
        n = v.shape[0] // 128
        cv[:, col:col + n] = v.reshape(n, 128).T

    for l in range(NLAYER):
        b = l * NCL
        put(b + CV["attn_norm"], inp["attn_norm"][l])
        put(b + CV["ffn_norm"], inp["ffn_norm"][l])
        put(b + CV["ple_norm"], inp["ple_norm"][l])
        put(b + CV["q_norm"], inp["mla_q_norm"][l])
        put(b + CV["kv_norm"], inp["mla_kv_norm"][l])
        put(b + CV["subln"], np.tile(np.asarray(inp["diff_subln"][l]), 2))
        for j in range(3):
            put(b + CV[f"cw{j}"], inp["conv_w"][l][j])
        put(b + CV["cb"], inp["conv_b"][l])
        lam_init = 0.8 - 0.6 * math.exp(-0.3 * l)
        cv[:, b + CV["laminit"]] = lam_init
        cv[:, b + CV["omli"]] = 1.0 - lam_init
    put(NLAYER * NCL, inp["final_norm"])
    return cv


def _rope_tables():
    pos = np.arange(S, dtype=np.float32)
    out = {}
    for name, d in (("ropeA", 64), ("ropeC", 32)):
        inv = np.power(np.float32(10000.0), -np.arange(0, d, 2, dtype=np.float32) / np.float32(d)).astype(np.float32)
        ang = (pos[:, None] * inv[None, :]).astype(np.float32)
        cos = np.cos(ang).astype(np.float32)
        sin = np.sin(ang).astype(np.float32)
        p = np.arange(128)
        j = p % d
        i = j % (d // 2)
        sign = np.where(j < d // 2, -1.0, 1.0).astype(np.float32)
        tab = np.empty((4, 128, 2, 512), np.float32)
        cosT = cos[:, i].T
        sinT = (sin[:, i] * sign[None, :]).T
        for blk in range(4):
            tab[blk, :, 0, :] = cosT[:, blk * 512:(blk + 1) * 512]
            tab[blk, :, 1, :] = sinT[:, blk * 512:(blk + 1) * 512]
        out[name] = tab
    return out


def _na_tile_dr0():
    tiles = []
    for i in range(7):
        tiles.append((-6 + 2 * i, (True, True)))
    for i in range(7):
        tiles.append((-7 + 2 * i, (True, True)))
    for i in range(5):
        dr0 = -5 + 2 * i
        tiles.append((dr0, (i != 0, i != 4)))
    return tiles


def _bias_w(rpb_l):
    tiles = _na_tile_dr0()
    cq = np.arange(64)
    ck = np.arange(64)
    c_start = np.clip(cq - 8, 0, 48)
    ok = (ck[:, None] >= c_start[None, :]) & (ck[:, None] < c_start[None, :] + 16)
    dc = np.clip(ck[:, None] - cq[None, :], -15, 15) + 15
    out = np.full((128, 4, NTW, 64), NEGB, np.float32)
    for h in range(4):
        for t, (dr0, valid) in enumerate(tiles):
            for a in range(2):
                if not valid[a]:
                    continue
                dr = dr0 + a
                g = rpb_l[h, dr + 7][dc]
                out[a * 64:(a + 1) * 64, h, t, :] = np.where(ok, g, np.float32(NEGB))
    return out.reshape(128, 4 * NTW * 64)


def _tmask():
    p = np.arange(128)[:, None]
    f = np.arange(TM_W)[None, :] - TM_F0
    d = f - p
    ad = np.abs(d)
    m = (ad <= 64).astype(np.float32) + ((d % 4 == 0) & (ad <= 256)) + ((d % 16 == 0) & (ad <= 1024))
    return m.astype(np.float32)


def _na_row_plan(rq):
    kr0 = min(max(rq - 4, 0), 24)
    if kr0 % 2 == 0:
        kt0 = kr0 // 2
        nt = 4
        dr0 = kr0 - rq
        if dr0 % 2 == 0:
            t0 = (dr0 + 6) // 2
        else:
            t0 = 7 + (dr0 + 7) // 2
    else:
        kt0 = (kr0 - 1) // 2
        nt = 5
        t0 = 14
    return kt0, nt, t0


class Buf:
    __slots__ = ("w", "r", "name", "excl")

    def __init__(self, name="", excl=False):
        self.w = {}
        self.r = {}
        self.name = name
        self.excl = excl


def _merge(d, s):
    for k, v in s.items():
        if d.get(k, 0) < v:
            d[k] = v


class Sched:
    def __init__(self, nc, es):
        self.nc = nc
        self.es = es
        self.e = dict(pe=nc.tensor, act=nc.scalar, dve=nc.vector, pool=nc.gpsimd, sp=nc.sync)
        self.sems = {}
        self.cnt = {}
        self.waited = {k: {} for k in self.e}
        self.epoch = 0
        self.pend = {k: [] for k in self.e}
        self.nwait = 0
        self.nins = 0

    def sem(self, key):
        if key not in self.sems:
            nm = "s_" + "_".join(str(x) for x in key)
            self.sems[key] = self.es.enter_context(self.nc.semaphore(nm))
            self.cnt[key] = 0
        return self.sems[key]

    def wait(self, eng, deps):
        w = self.waited[eng]
        for k, v in deps.items():
            if w.get(k, 0) < v:
                self.e[eng].wait_ge(self.sem(k), v)
                w[k] = v
                self.nwait += 1

    def deps_of(self, reads, writes):
        d = {}
        for b in reads:
            _merge(d, b.w)
            if b.excl:
                _merge(d, b.r)
        for b in writes:
            _merge(d, b.w)
            _merge(d, b.r)
        return d

    def op(self, eng, fn, reads=(), writes=(), signal=True):
        d = self.deps_of(reads, writes)
        self.wait(eng, d)
        ins = fn(self.e[eng])
        self.nins += 1
        self.pend[eng].append((reads, writes))
        if not signal:
            return None
        key = (eng, self.epoch)
        s = self.sem(key)
        self.cnt[key] += 1
        ins.then_inc(s, 1)
        tok = {key: self.cnt[key]}
        for rs, ws in self.pend[eng]:
            for b in rs:
                if b.excl:
                    b.w = dict(tok)
                    b.r = {}
                else:
                    _merge(b.r, tok)
            for b in ws:
                b.w = dict(tok)
                b.r = {}
        self.pend[eng] = []
        return tok

    def dma(self, q, pairs, reads=(), writes=(), key=None, **kw):
        d = self.deps_of(reads, writes)
        self.wait(q, d)
        s = self.sem(key)
        for out, in_ in pairs:
            ins = self.e[q].dma_start(out=out, in_=in_, **kw)
            self.cnt[key] += 16
            ins.then_inc(s, 16)
            self.nins += 1
        tok = {key: self.cnt[key]}
        for b in reads:
            _merge(b.r, tok)
        for b in writes:
            b.w = dict(tok)
            b.r = {}
        return tok

    def barrier(self):
        d = {k: v for k, v in self.cnt.items() if v > 0}
        for eng in self.e:
            self.wait(eng, d)


class Item:
    def __init__(self, w, fn):
        self.w = w
        self.fn = fn


KB = 256
ARENA_KB = 206


class Builder:
    def __init__(self, layer_ids, src_is_x, do_final):
        self.layer_ids = layer_ids
        self.do_final = do_final
        nl = len(layer_ids)
        nc = bass.Bass("TRN2", target_bir_lowering=False)
        self.nc = nc
        self.hin = nc.dram_tensor("hin", [D, S], F32, kind="ExternalInput").ap()
        self.pT = nc.dram_tensor("pT", [nl, 256, S], F32, kind="ExternalInput").ap()
        self.wpk = nc.dram_tensor("wpk", [nl, 128, WTOT], F32, kind="ExternalInput").ap()
        self.cvec = nc.dram_tensor("cvec", [128, NCV], F32, kind="ExternalInput").ap()
        self.lamv = nc.dram_tensor("lamv", [nl, 128], F32, kind="ExternalInput").ap()
        self.biasw = nc.dram_tensor("biasw", [nl, 128, 4 * NTW * 64], F32, kind="ExternalInput").ap()
        self.tmask = nc.dram_tensor("tmask", [128, TM_W], F32, kind="ExternalInput").ap()
        self.ropeA = nc.dram_tensor("ropeA", [4, 128, 2, 512], F32, kind="ExternalInput").ap()
        self.ropeC = nc.dram_tensor("ropeC", [4, 128, 2, 512], F32, kind="ExternalInput").ap()
        self.hout = nc.dram_tensor("hout", [D, S], F32, kind="ExternalOutput").ap()
        self.hs = nc.dram_tensor("hscr", [D, S], F32, kind="Internal").ap()
        with ExitStack() as es:
            self.es = es
            self.S = Sched(nc, es)
            self.AR = es.enter_context(nc.sbuf_tensor("arena", [128, ARENA_KB * KB], F32))
            self.PS = es.enter_context(nc.psum_tensor("ps", [128, 4096], F32))
            self.PB = [Buf(f"pb{i}", excl=True) for i in range(8)]
            self._layout()
            self._emit()

    def f32v(self, off_kb, n):
        o = int(off_kb * KB)
        return self.AR[:, o:o + n]

    def bfv(self, off_kb, n):
        o = int(off_kb * KB)
        return self.AR[:, o:o + n // 2].bitcast(BF16)

    def _layout(self):
        self.HN = self.bfv(0, 8 * S).rearrange("p (c t) -> p c t", c=8)
        self.RING = [self.bfv(32 + 6 * i, SLOT) for i in range(NSLOT)]
        self.RINGb = [Buf(f"ring{i}") for i in range(NSLOT)]
        self.CVt = self.f32v(56, NCV)
        self.ONES = self.bfv(60, 128)
        self.ONESBD = self.bfv(60.25, 128)
        self.LQ = self.f32v(60.5, 128)
        self.SM = self.f32v(61, 64)
        self.TMP32 = self.f32v(61.25, 64)
        self.PPW = self.bfv(62, 2048).rearrange("p (k n) -> p k n", k=2)
        P0 = 66
        self.QT = self.bfv(P0, 4 * S).rearrange("p (c t) -> p c t", c=4)
        self.KT = self.bfv(P0 + 16, 4 * S).rearrange("p (c t) -> p c t", c=4)
        self.VA = self.bfv(P0 + 32, 16 * 4 * 128).rearrange("p (j h d) -> p j h d", j=16, h=4)
        self.VAflat = self.bfv(P0 + 32, 16 * 4 * 128)
        o = P0 + 48
        self.ROPE = [self.f32v(o + 4 * i, 1024).rearrange("p (a t) -> p a t", a=2) for i in range(2)]
        o += 8
        self.TM = self.bfv(o, TM_W)
        o += 6
        self.W = self.bfv(o, 4 * NTW * 64).rearrange("p (h n) -> p h n", h=4)
        o += 10
        self.PT = [self.bfv(o + i, 512) for i in range(6)]
        self.PTpair = [self.bfv(o + 2 * i, 1024) for i in range(3)]
        self.PTN = [self.bfv(o + 1.25 * i, 640).rearrange("p (j n) -> p j n", j=2) for i in range(3)]
        o += 6
        self.T1 = [self.f32v(o + 2 * i, 512) for i in range(2)]
        o += 4
        self.T2 = [self.f32v(o + 2 * i, 512) for i in range(2)]
        o += 4
        self.REC = [self.f32v(o + 2 * i, 512) for i in range(2)]
        o += 4
        self.RS = [self.f32v(o + 2 * i, 512) for i in range(2)]
        o += 4
        self.OC = [self.f32v(o + 2 * i, 512) for i in range(2)]
        o += 4
        self.BSTG = self.f32v(o, NTW * 64)
        o += 5
        self.SQT = self.bfv(o, 5 * 512).rearrange("p (c t) -> p c t", c=5)
        o += 5
        self.MIXo = o
        self.MIX = self.bfv(o, 8 * S).rearrange("p (c t) -> p c t", c=8)
        o += 32
        assert o <= ARENA_KB, o
        self.H = self.f32v(P0, 8 * S).rearrange("p (c t) -> p c t", c=8)
        ob = P0 + 64
        assert ob <= self.MIXo
        self.ACTB = self.bfv(ob, 22 * 1024).rearrange("p (c t) -> p c t", c=22)
        self.PBt = self.bfv(ob, 2 * S).rearrange("p (k t) -> p k t", k=2)
        self.ET = [self.f32v(ob + 8 + 2 * i, 512) for i in range(2)]
        ob += 44
        self.CA = [self.f32v(ob + 4 * i, 1024) for i in range(3)]
        ob += 12
        self.CG = [self.bfv(ob + 2 * i, 1024) for i in range(2)]
        ob += 4
        self.RSB = [self.f32v(ob + 2 * i, 512) for i in range(2)]
        ob += 4
        assert ob <= ARENA_KB, ob

    def bank(self, b, n=512, off=0):
        return self.PS[:, b * 512 + off:b * 512 + off + n]

    def mm(self, out, obufs, lhsT, rhs, rd, start, stop, signal=None):
        if signal is None:
            signal = stop
        return self.S.op("pe", lambda e: e.matmul(out, lhsT=lhsT, rhs=rhs, start=start, stop=stop),
                         reads=rd, writes=obufs, signal=signal)

    def cv(self, li, name, idx=0):
        c = li * NCL + CV[name] + idx
        return self.CVt[:, c:c + 1]

    def run_items(self, items):
        reqs = []
        for i, it in enumerate(items):
            for w in it.w:
                reqs.append(w)
        state = {"issued": 0}

        def issue_upto(n):
            n = min(n, len(reqs))
            while state["issued"] < n:
                j = state["issued"]
                li, name = reqs[j]
                o, kc, m = WT_OFF[name]
                sl = j % NSLOT
                dst = self.RING[sl][:, 0:kc * m].rearrange("p (k m) -> p k m", k=kc)
                src = self.wpk[li, :, o:o + kc * m].rearrange("p (k m) -> p k m", k=kc)
                self.S.dma("pool", [(dst, src)], writes=[self.RINGb[sl]], key=("ring", sl))
                state["issued"] += 1

        ptr = 0
        for it in items:
            n = len(it.w)
            issue_upto(ptr + n)
            tiles = []
            for j in range(ptr, ptr + n):
                li, name = reqs[j]
                o, kc, m = WT_OFF[name]
                sl = j % NSLOT
                tiles.append((self.RING[sl][:, 0:kc * m].rearrange("p (k m) -> p k m", k=kc), self.RINGb[sl]))
            it.fn(tiles)
            ptr += n
            issue_upto(ptr + NSLOT)

    def _emit(self):
        S_ = self.S
        nc = self.nc
        items = []
        nl = len(self.layer_ids)

        def add(w, fn):
            items.append(Item(w, fn))

        add([], self.prologue)
        for li in range(nl):
            self.add_phase_a(items, li)
            self.add_phase_b(items, li)
        import os
        nit = int(os.environ.get("NITEMS", "0"))
        if nit:
            items = items[:nit]
        self.run_items(items)
        S_.barrier()

    def new_h_bufs(self):
        self.Hb = [[Buf(f"h{k}_{b}") for b in range(4)] for k in range(8)]
        self.HNb = [[Buf(f"hn{k}_{b}") for b in range(4)] for k in range(8)]

    def prologue(self, tiles):
        S_ = self.S
        self.new_h_bufs()
        self.cvb = Buf("cv")
        S_.dma("sp", [(self.CVt, self.cvec)], writes=[self.cvb], key=("cv",))
        self.onesb = Buf("ones")
        S_.op("pool", lambda e: e.memset(self.ONES, 1.0), writes=[self.onesb])
        S_.op("pool", lambda e: e.memset(self.ONESBD, 0.0), writes=[self.onesb])
        S_.op("pool", lambda e: e.memset(self.ONESBD[0:64, 0:64], 1.0), writes=[self.onesb])
        S_.op("pool", lambda e: e.memset(self.ONESBD[64:128, 64:128], 1.0), writes=[self.onesb])
        self.load_h(self.hin)
        self.rmsnorm_main(0, "attn_norm")
        S_.barrier()

    def load_h(self, src):
        for kc in range(8):
            self.S.dma("sp", [(self.H[:, kc, :], src[kc * 128:(kc + 1) * 128, :])],
                       writes=self.Hb[kc], key=("hld", kc))

    def store_h(self, dst):
        for kc in range(8):
            self.S.dma("sp", [(dst[kc * 128:(kc + 1) * 128, :], self.H[:, kc, :])],
                       reads=self.Hb[kc], key=("hst",))

    def _rsbufs(self):
        rsb = getattr(self, "_rsb", None)
        if rsb is None:
            rsb = self._rsb = [Buf("rsb0"), Buf("rsb1")]
        return rsb

    def norm_s1(self, blk):
        ts = slice(blk * 512, (blk + 1) * 512)
        hb = [self.Hb[k][blk] for k in range(8)]
        self.S.op("act", lambda e: e.activation(out=self.HN[:, :, ts], in_=self.H[:, :, ts], func=AF.Square),
                  reads=hb, writes=[self.HNb[k][blk] for k in range(8)])

    def norm_s2(self, li, gname, blk, final=False, gcol=None, rsv=None):
        S_ = self.S
        rsb = self._rsbufs()
        rsv = rsv or self.RSB
        ts = slice(blk * 512, (blk + 1) * 512)
        for kc in range(8):
            self.mm(self.bank(7), [self.PB[7]], self.ONES, self.HN[:, kc, ts],
                    [self.HNb[kc][blk], self.onesb], kc == 0, kc == 7)
        r = blk % 2
        rs = rsv[r]
        S_.op("act", lambda e: e.activation(out=rs, in_=self.bank(7), func=AF.Sqrt, bias=1e-6, scale=1.0 / D),
              reads=[self.PB[7]], writes=[rsb[r]])
        S_.op("dve", lambda e: e.reciprocal(out=rs, in_=rs), reads=[rsb[r]], writes=[rsb[r]])
        for kc in range(8):
            if gcol is not None:
                g = self.CVt[:, gcol + kc:gcol + kc + 1]
            else:
                g = self.cv(li, gname, kc)
            if final:
                S_.op("dve", lambda e: e.scalar_tensor_tensor(out=self.H[:, kc, ts], in0=self.H[:, kc, ts], scalar=g,
                                                              in1=rs, op0=ALU.mult, op1=ALU.mult),
                      reads=[rsb[r], self.cvb], writes=[self.Hb[kc][blk]])
            else:
                S_.op("dve", lambda e: e.scalar_tensor_tensor(out=self.HN[:, kc, ts], in0=self.H[:, kc, ts], scalar=g,
                                                              in1=rs, op0=ALU.mult, op1=ALU.mult),
                      reads=[self.Hb[kc][blk], rsb[r], self.cvb], writes=[self.HNb[kc][blk]])

    def rmsnorm_main(self, li, gname, final=False, gcol=None):
        for blk in range(4):
            self.norm_s1(blk)
            self.norm_s2(li, gname, blk, final=final, gcol=gcol)

    def get_rope(self, table, blk):
        i = self._rope_i = (getattr(self, "_rope_i", -1) + 1) % 2
        src = (self.ropeA if table == "A" else self.ropeC)[blk]
        self.S.dma("sp", [(self.ROPE[i], src)], writes=[self.ROPEb[i]], key=("rope", i))
        return self.ROPE[i], self.ROPEb[i]

    def rot(self, name, n):
        v = getattr(self, "_rot_" + name, -1)
        v = (v + 1) % n
        setattr(self, "_rot_" + name, v)
        return v

    def add_phase_a(self, items, li):
        S_ = self.S

        def start(tiles):
            self.QTb = [[Buf() for _ in range(4)] for _ in range(4)]
            self.KTb = [[Buf() for _ in range(4)] for _ in range(4)]
            self.VAb = [Buf() for _ in range(16)]
            self.MIXb = [[Buf() for _ in range(4)] for _ in range(8)]
            self.ROPEb = [Buf(), Buf()]
            self.PTb = [Buf() for _ in range(6)]
            self.T1b = [Buf(), Buf()]
            self.T2b = [Buf(), Buf()]
            self.RECb = [Buf(), Buf()]
            self.RSb = [Buf(), Buf()]
            self.OCb = [Buf(), Buf()]
            self.SQTb = Buf()
            self.TMb = Buf()
            self.Wb = Buf()
            self.BSTGb = Buf()
            self.LQb = Buf()
            self.SMb = Buf()
            for b in self.PB:
                b.w = {}
                b.r = {}
            S_.op("pool", lambda e: e.memset(self.VAflat, 1.0), writes=self.VAb)
            S_.dma("pool", [(self.TM.rearrange("p (a n) -> p a n", a=2),
                             self.tmask.rearrange("p (a n) -> p a n", a=2))], writes=[self.TMb], key=("tm",))
            for h in range(4):
                S_.dma("sp", [(self.BSTG, self.biasw[li, :, h * NTW * 64:(h + 1) * NTW * 64])],
                       writes=[self.BSTGb], key=("bstg",))
                S_.op("act", lambda e: e.activation(out=self.W[:, h, :], in_=self.BSTG, func=AF.Exp),
                      reads=[self.BSTGb], writes=[self.Wb])
            S_.dma("sp", [(self.LQ.rearrange("p (a n) -> p a n", a=1), self.lamv[li:li + 1, :].partition_broadcast(128))], writes=[self.LQb], key=("lq",))
            S_.op("dve", lambda e: e.tensor_tensor(out=self.TMP32[:, 0:32], in0=self.LQ[:, 0:32], in1=self.LQ[:, 32:64],
                                                   op=ALU.mult), reads=[self.LQb], writes=[self.SMb])
            S_.op("dve", lambda e: e.tensor_tensor(out=self.TMP32[:, 32:64], in0=self.LQ[:, 64:96], in1=self.LQ[:, 96:128],
                                                   op=ALU.mult), reads=[self.LQb], writes=[self.SMb])
            S_.op("dve", lambda e: e.reduce_sum(out=self.SM[:, 0:1], in_=self.TMP32[:, 0:32], axis=AX.X),
                  reads=[self.SMb], writes=[self.SMb])
            S_.op("dve", lambda e: e.reduce_sum(out=self.SM[:, 1:2], in_=self.TMP32[:, 32:64], axis=AX.X),
                  reads=[self.SMb], writes=[self.SMb])
            S_.op("act", lambda e: e.activation(out=self.SM[:, 2:4], in_=self.SM[:, 0:2], func=AF.Exp),
                  reads=[self.SMb], writes=[self.SMb])
            S_.op("dve", lambda e: e.tensor_tensor(out=self.SM[:, 4:5], in0=self.SM[:, 3:4], in1=self.SM[:, 2:3],
                                                   op=ALU.subtract), reads=[self.SMb], writes=[self.SMb])
            S_.op("dve", lambda e: e.tensor_tensor(out=self.SM[:, 5:6], in0=self.SM[:, 4:5], in1=self.cv(li, "laminit"),
                                                   op=ALU.subtract), reads=[self.SMb, self.cvb], writes=[self.SMb])
            S_.op("dve", lambda e: e.tensor_tensor(out=self.SM[:, 6:7], in0=self.cv(li, "subln"), in1=self.cv(li, "omli"),
                                                   op=ALU.mult), reads=[self.SMb, self.cvb], writes=[self.SMb])

        items.append(Item([], start))
        self.add_mixer_b(items, li)
        self.add_mixer_a(items, li)
        self.add_mixer_c(items, li)
        self.add_mixer_d(items, li)

    def evac_copy(self, dst, src, rd, wr):
        eng = "act" if self.rot("evac", 2) == 0 else "dve"
        if eng == "act":
            self.S.op("act", lambda e: e.activation(out=dst, in_=src, func=AF.Copy), reads=rd, writes=wr)
        else:
            self.S.op("dve", lambda e: e.tensor_copy(out=dst, in_=src), reads=rd, writes=wr)

    def proj_rope_chunk(self, wt, wb, col0, table, blk, dsts):
        S_ = self.S
        ts = slice(blk * 512, (blk + 1) * 512)
        b1 = self.rot("pbank", 7)
        b2 = self.rot("pbank", 7)
        for kc in range(8):
            self.mm(self.bank(b1), [self.PB[b1]], wt[:, kc, col0:col0 + 128], self.HN[:, kc, ts],
                    [wb, self.HNb[kc][blk]], kc == 0, kc == 7)
        for kc in range(8):
            self.mm(self.bank(b2), [self.PB[b2]], wt[:, kc, col0 + 128:col0 + 256], self.HN[:, kc, ts],
                    [wb, self.HNb[kc][blk]], kc == 0, kc == 7)
        rope, ropeb = self.get_rope(table, blk)
        i = self.rot("t12", 2)
        S_.op("dve", lambda e: e.tensor_tensor(out=self.T1[i], in0=self.bank(b1), in1=rope[:, 0, :], op=ALU.mult),
              reads=[self.PB[b1], ropeb], writes=[self.T1b[i]])
        S_.op("dve", lambda e: e.tensor_tensor(out=self.T2[i], in0=self.bank(b2), in1=rope[:, 1, :], op=ALU.mult),
              reads=[self.PB[b2], ropeb], writes=[self.T2b[i]])
        for lo, hi, dst, db in dsts:
            S_.op("pool", lambda e: e.tensor_tensor(out=dst, in0=self.T1[i][lo:hi, :], in1=self.T2[i][lo:hi, :], op=ALU.add),
                  reads=[self.T1b[i], self.T2b[i]], writes=[db])

    def proj_v(self, wt, wb, col0, nkc, src, srcb_fn):
        for jt in range(16):
            b = self.rot("pbank", 7)
            for kc in range(nkc):
                self.mm(self.bank(b, 256), [self.PB[b]], src[:, kc, jt * 128:(jt + 1) * 128], wt[:, kc, col0:col0 + 256],
                        [wb] + srcb_fn(kc, jt), kc == 0, kc == nkc - 1)
            pv = self.bank(b, 256).rearrange("p (hp two d) -> p hp two d", two=2, d=64)
            va = self.VA[:, jt, :, :].rearrange("p (hp two) d -> p hp two d", two=2)
            self.S.op("act", lambda e: e.activation(out=va[:, :, 0, 0:64], in_=pv[:, :, 0, :], func=AF.Copy),
                      reads=[self.PB[b]], writes=[self.VAb[jt]])
            self.S.op("dve", lambda e: e.tensor_copy(out=va[:, :, 1, 64:128], in_=pv[:, :, 1, :]),
                      reads=[self.PB[b]], writes=[self.VAb[jt]])

    def attn_dense(self, kind, li, chunk0, scale):
        S_ = self.S
        nmaps = 2 if kind == "C" else 1
        groups = []
        for c in range(2):
            for ib in range(4):
                if kind == "A":
                    tl = [jt for jt in range(16) if -TM_F0 <= ib * 512 - jt * 128 <= 1024]
                    for idx, jt in enumerate(tl):
                        groups.append([dict(c=c, ib=ib, j=j, h=2 * c + j, m=0, jt=jt, first=idx == 0,
                                            last=idx == len(tl) - 1) for j in range(2)])
                elif kind == "C":
                    for j in range(2):
                        for jt in range(16):
                            groups.append([dict(c=c, ib=ib, j=j, h=2 * c + j, m=m, jt=jt, first=jt == 0, last=jt == 15)
                                           for m in range(2)])
                else:
                    for j in range(2):
                        for jt in range(16):
                            groups.append([dict(c=c, ib=ib, j=j, h=2 * c + j, m=0, jt=jt, first=jt == 0, last=jt == 15)])
        if kind == "A":
            SB, UB, LAG = [0, 1, 2, 3], [4, 5, 6], 1
        elif kind == "B":
            SB, UB, LAG = [0, 1, 2, 3], [4, 5, 6], 3
        else:
            SB, UB, LAG = [0, 1, 2, 3], [4, 5, 6], 1
        paired = kind != "B"
        ucur = {}
        deferred = []
        un = [0]

        def qk_group(grp):
            if paired:
                s0 = SB[2 * self.rot("sgrp", len(SB) // 2)]
                sbs = [s0, s0 + 1]
            else:
                sbs = [SB[self.rot("sbank", len(SB))]]
            for st, sb in zip(grp, sbs):
                st["sb"] = sb
                h, m, jt, ib = st["h"], st["m"], st["jt"], st["ib"]
                qs = slice(ib * 512, (ib + 1) * 512)
                ks = slice(jt * 128, (jt + 1) * 128)
                kblk = jt // 4
                if kind in ("A",):
                    tile, lo, hi = h // 2, (h % 2) * 64, (h % 2) * 64 + 64
                elif kind == "B":
                    tile, lo, hi = h, 0, 96
                else:
                    tile, lo, hi = h, 32 * m, 32 * m + 32
                self.mm(self.bank(sb), [self.PB[sb]], self.KT[lo:hi, tile, ks], self.QT[lo:hi, tile, qs],
                        [self.KTb[tile][kblk], self.QTb[tile][ib]], True, True)

        for g in range(min(LAG, len(groups))):
            qk_group(groups[g])
        D = 1 if paired else 0
        NPT = len(self.PT)

        def do_exp(g):
            grp = groups[g]
            for i, st in enumerate(grp):
                sb = st["sb"]
                ib, jt, j = st["ib"], st["jt"], st["j"]
                if paired:
                    if i == 0:
                        p0 = 2 * self.rot("ptgrp", NPT // 2)
                        s0 = sb
                        outv = self.PTpair[p0 // 2]
                        S_.op("act", lambda e: e.activation(out=outv, in_=self.PS[:, s0 * 512:s0 * 512 + 1024],
                                                            func=AF.Exp, scale=scale),
                              reads=[self.PB[s0], self.PB[s0 + 1]], writes=[self.PTb[p0], self.PTb[p0 + 1]])
                        self._grp_p0 = p0
                    p = self._grp_p0 + i
                else:
                    p = self.rot("pt", NPT)
                    S_.op("act", lambda e: e.activation(out=self.PT[p], in_=self.bank(sb), func=AF.Exp, scale=scale),
                          reads=[self.PB[sb]], writes=[self.PTb[p]])
                st["p"] = p
                if kind == "A":
                    off = ib * 512 - jt * 128 + TM_F0
                    S_.op("pool" if (j == 1 and jt % 2 == 1) else "dve",
                          lambda e: e.tensor_tensor(out=self.PT[p], in0=self.PT[p], in1=self.TM[:, off:off + 512],
                                                    op=ALU.mult), reads=[self.PTb[p], self.TMb], writes=[self.PTb[p]])

        sched = []
        for g in range(len(groups) + D):
            if g < len(groups):
                sched.append(("front", g, 0, None))
            if g - D >= 0:
                for i, st in enumerate(groups[g - D]):
                    sched.append(("pv", g - D, i, st))
        for what, g, i, st in sched:
            if what == "front":
                if g + LAG < len(groups):
                    qk_group(groups[g + LAG])
                do_exp(g)
                continue
            h, m, jt, ib, c, j, p = st["h"], st["m"], st["jt"], st["ib"], st["c"], st["j"], st["p"]
            if st["first"]:
                ucur[(j, m)] = UB[un[0] % len(UB)]
                un[0] += 1
            ub = ucur[(j, m)]
            self.mm(self.bank(ub), [self.PB[ub]], self.VA[:, jt, h, :], self.PT[p],
                    [self.VAb[jt], self.PTb[p]], st["first"], st["last"],
                    signal=(st["last"] or g + LAG + D + 2 >= len(groups)))
            for _ in range(NFILL.get(kind, 0) if (not paired or i == 1) else 0):
                self.mm(self.bank(7, FILLN), [self.PB[7]], self.ONES, self.TM[:, 0:FILLN], [self.onesb, self.TMb], True, True,
                        signal=False)
            for d in list(deferred):
                d[0] -= 1
                if d[0] <= 0:
                    d[1]()
                    deferred.remove(d)
            if st["last"] and m == nmaps - 1:
                orow = slice(0, 64) if j == 0 else slice(64, 128)
                drow = slice(64, 128) if j == 0 else slice(0, 64)
                qs = slice(ib * 512, (ib + 1) * 512)
                def copy_out(u):
                    r = self.rot("rec", 2)
                    a = self.rot("t12", 2)
                    S_.op("dve", lambda e: e.tensor_copy(out=self.REC[r][orow, :], in_=self.bank(u)[drow, :]),
                          reads=[self.PB[u]], writes=[self.RECb[r]])
                    S_.op("dve", lambda e: e.tensor_copy(out=self.T1[a][orow, :], in_=self.bank(u)[orow, :]),
                          reads=[self.PB[u]], writes=[self.T1b[a]])
                    return r, a

                if kind != "C":
                    r, a = copy_out(ub)
                    S_.op("dve", lambda e: e.reciprocal(out=self.REC[r][orow, :], in_=self.REC[r][orow, :]),
                          reads=[self.RECb[r]], writes=[self.RECb[r]])
                    S_.op("dve", lambda e: e.tensor_tensor(out=self.MIX[orow, chunk0 + c, qs], in0=self.T1[a][orow, :],
                                                           in1=self.REC[r][orow, :], op=ALU.mult),
                          reads=[self.T1b[a], self.RECb[r]], writes=[self.MIXb[chunk0 + c][ib]])
                else:
                    u1, u2 = ucur[(j, 0)], ucur[(j, 1)]
                    oc = (c * 4 + ib) % 2
                    r1, a1 = copy_out(u1)
                    r2, a2 = copy_out(u2)
                    S_.op("dve", lambda e: e.reciprocal(out=self.REC[r1][orow, :], in_=self.REC[r1][orow, :]),
                          reads=[self.RECb[r1]], writes=[self.RECb[r1]])
                    S_.op("dve", lambda e: e.tensor_tensor(out=self.OC[oc][orow, :], in0=self.T1[a1][orow, :],
                                                           in1=self.REC[r1][orow, :], op=ALU.mult),
                          reads=[self.T1b[a1], self.RECb[r1]], writes=[self.OCb[oc]])
                    S_.op("dve", lambda e: e.reciprocal(out=self.REC[r2][orow, :], in_=self.REC[r2][orow, :]),
                          reads=[self.RECb[r2]], writes=[self.RECb[r2]])
                    S_.op("dve", lambda e: e.tensor_tensor(out=self.T1[a2][orow, :], in0=self.T1[a2][orow, :],
                                                           in1=self.REC[r2][orow, :], op=ALU.mult),
                          reads=[self.T1b[a2], self.RECb[r2]], writes=[self.T1b[a2]])
                    S_.op("dve", lambda e: e.scalar_tensor_tensor(out=self.OC[oc][orow, :], in0=self.T1[a2][orow, :],
                                                                  scalar=self.SM[orow, 5:6], in1=self.OC[oc][orow, :],
                                                                  op0=ALU.mult, op1=ALU.add),
                          reads=[self.T1b[a2], self.SMb, self.OCb[oc]], writes=[self.OCb[oc]])
                    if j == 1:
                        def fin(oc=oc, c=c, ib=ib, qs=qs):
                            S_.op("act", lambda e: e.activation(out=self.SQT[:, 0, :], in_=self.OC[oc], func=AF.Square),
                                  reads=[self.OCb[oc]], writes=[self.SQTb])
                            fb = 7
                            self.mm(self.bank(fb), [self.PB[fb]], self.ONESBD, self.SQT[:, 0, :], [self.SQTb, self.onesb], True, True)
                            rr = self.rot("rs", 2)
                            S_.op("act", lambda e: e.activation(out=self.RS[rr], in_=self.bank(fb), func=AF.Sqrt, bias=1e-5,
                                                                scale=1.0 / 64), reads=[self.PB[fb]], writes=[self.RSb[rr]])
                            S_.op("dve", lambda e: e.reciprocal(out=self.RS[rr], in_=self.RS[rr]), reads=[self.RSb[rr]],
                                  writes=[self.RSb[rr]])
                            S_.op("dve", lambda e: e.scalar_tensor_tensor(out=self.MIX[:, chunk0 + c, qs], in0=self.OC[oc],
                                                                          scalar=self.SM[:, 6:7], in1=self.RS[rr],
                                                                          op0=ALU.mult, op1=ALU.mult),
                                  reads=[self.OCb[oc], self.SMb, self.RSb[rr]], writes=[self.MIXb[chunk0 + c][ib]])
                        deferred.append([6, fin])
        for d in deferred:
            d[1]()

    def add_mixer_b(self, items, li):
        S_ = self.S
        CQN = self.MIX[:, 4:7, :]
        CKVN = self.MIX[:, 0:2, :]

        def b1(tiles):
            (t1, t1b), (t2, t2b) = tiles
            for blk in range(4):
                ts = slice(blk * 512, (blk + 1) * 512)
                for c in range(3):
                    for kc in range(8):
                        self.mm(self.bank(c), [self.PB[c]], t1[:, kc, c * 128:(c + 1) * 128], self.HN[:, kc, ts],
                                [t1b, self.HNb[kc][blk]], kc == 0, kc == 7)
                for c in range(2):
                    for kc in range(8):
                        self.mm(self.bank(3 + c), [self.PB[3 + c]], t2[:, kc, c * 128:(c + 1) * 128], self.HN[:, kc, ts],
                                [t2b, self.HNb[kc][blk]], kc == 0, kc == 7)
                for kc in range(8):
                    self.mm(self.bank(5)[0:64, :], [self.PB[5]], t2[:, kc, 256:320], self.HN[:, kc, ts],
                            [t2b, self.HNb[kc][blk]], kc == 0, kc == 7)
                S_.op("act", lambda e: e.activation(out=self.SQT[:, 0:3, :], in_=self.PS[:, 0:1536].rearrange("p (c t) -> p c t", c=3),
                                                    func=AF.Square), reads=self.PB[0:3], writes=[self.SQTb])
                S_.op("act", lambda e: e.activation(out=self.SQT[:, 3:5, :], in_=self.PS[:, 1536:2560].rearrange("p (c t) -> p c t", c=2),
                                                    func=AF.Square), reads=self.PB[3:5], writes=[self.SQTb])
                for c in range(3):
                    self.mm(self.bank(6), [self.PB[6]], self.ONES, self.SQT[:, c, :], [self.SQTb, self.onesb], c == 0, c == 2)
                for c in range(2):
                    self.mm(self.bank(7), [self.PB[7]], self.ONES, self.SQT[:, 3 + c, :], [self.SQTb, self.onesb], c == 0, c == 1)
                S_.op("act", lambda e: e.activation(out=self.RS[0], in_=self.bank(6), func=AF.Sqrt, bias=1e-6, scale=1.0 / 384),
                      reads=[self.PB[6]], writes=[self.RSb[0]])
                S_.op("act", lambda e: e.activation(out=self.RS[1], in_=self.bank(7), func=AF.Sqrt, bias=1e-6, scale=1.0 / 256),
                      reads=[self.PB[7]], writes=[self.RSb[1]])
                S_.op("dve", lambda e: e.reciprocal(out=self.RS[0], in_=self.RS[0]), reads=[self.RSb[0]], writes=[self.RSb[0]])
                S_.op("dve", lambda e: e.reciprocal(out=self.RS[1], in_=self.RS[1]), reads=[self.RSb[1]], writes=[self.RSb[1]])
                for c in range(3):
                    S_.op("dve", lambda e: e.scalar_tensor_tensor(out=CQN[:, c, ts], in0=self.bank(c), scalar=self.cv(li, "q_norm", c),
                                                                  in1=self.RS[0], op0=ALU.mult, op1=ALU.mult),
                          reads=[self.PB[c], self.RSb[0], self.cvb], writes=[self.MIXb[4 + c][blk]])
                for c in range(2):
                    S_.op("dve", lambda e: e.scalar_tensor_tensor(out=CKVN[:, c, ts], in0=self.bank(3 + c),
                                                                  scalar=self.cv(li, "kv_norm", c), in1=self.RS[1],
                                                                  op0=ALU.mult, op1=ALU.mult),
                          reads=[self.PB[3 + c], self.RSb[1], self.cvb], writes=[self.MIXb[c][blk]])
                rope, ropeb = self.get_rope("C", blk)
                i = self.rot("t12", 2)
                S_.op("dve", lambda e: e.tensor_tensor(out=self.T1[i][0:32, :], in0=self.bank(5)[0:32, :], in1=rope[0:32, 0, :],
                                                       op=ALU.mult), reads=[self.PB[5], ropeb], writes=[self.T1b[i]])
                S_.op("dve", lambda e: e.tensor_tensor(out=self.T2[i][0:32, :], in0=self.bank(5)[32:64, :], in1=rope[32:64, 1, :],
                                                       op=ALU.mult), reads=[self.PB[5], ropeb], writes=[self.T2b[i]])
                for h in range(4):
                    S_.op("pool", lambda e: e.tensor_tensor(out=self.KT[64:96, h, ts], in0=self.T1[i][0:32, :],
                                                            in1=self.T2[i][0:32, :], op=ALU.add),
                          reads=[self.T1b[i], self.T2b[i]], writes=[self.KTb[h][blk]])

        items.append(Item([(li, "b_cq"), (li, "b_ckv")], b1))

        def b2(tiles):
            (t, tb), = tiles
            for c in range(2):
                for blk in range(4):
                    ts = slice(blk * 512, (blk + 1) * 512)
                    b = self.rot("pbank", 7)
                    for kc in range(2):
                        self.mm(self.bank(b), [self.PB[b]], t[:, kc, c * 128:(c + 1) * 128], CKVN[:, kc, ts],
                                [tb, self.MIXb[kc][blk]], kc == 0, kc == 1)
                    S_.op("act", lambda e: e.activation(out=self.KT[0:64, 2 * c, ts], in_=self.bank(b)[0:64, :], func=AF.Copy),
                          reads=[self.PB[b]], writes=[self.KTb[2 * c][blk]])
                    S_.op("dve", lambda e: e.tensor_copy(out=self.KT[0:64, 2 * c + 1, ts], in_=self.bank(b)[64:128, :]),
                          reads=[self.PB[b]], writes=[self.KTb[2 * c + 1][blk]])
            self.proj_v(t, tb, 256, 2, CKVN, lambda kc, jt: [self.MIXb[kc][jt // 4]])

        items.append(Item([(li, "ukv")], b2))

        def mk_b3(h):
            def b3(tiles):
                (t, tb), = tiles
                for blk in range(4):
                    ts = slice(blk * 512, (blk + 1) * 512)
                    b1_ = self.rot("pbank", 7)
                    b2_ = self.rot("pbank", 7)
                    for kc in range(3):
                        self.mm(self.bank(b1_)[0:96, :], [self.PB[b1_]], t[:, kc, 0:96], CQN[:, kc, ts],
                                [tb, self.MIXb[4 + kc][blk]], kc == 0, kc == 2)
                    for kc in range(3):
                        self.mm(self.bank(b2_)[0:32, :], [self.PB[b2_]], t[:, kc, 96:128], CQN[:, kc, ts],
                                [tb, self.MIXb[4 + kc][blk]], kc == 0, kc == 2)
                    S_.op("act", lambda e: e.activation(out=self.QT[0:64, h, ts], in_=self.bank(b1_)[0:64, :], func=AF.Copy),
                          reads=[self.PB[b1_]], writes=[self.QTb[h][blk]])
                    rope, ropeb = self.get_rope("C", blk)
                    i = self.rot("t12", 2)
                    S_.op("dve", lambda e: e.tensor_tensor(out=self.T1[i][64:96, :], in0=self.bank(b1_)[64:96, :],
                                                           in1=rope[64:96, 0, :], op=ALU.mult),
                          reads=[self.PB[b1_], ropeb], writes=[self.T1b[i]])
                    S_.op("dve", lambda e: e.tensor_tensor(out=self.T2[i][64:96, :], in0=self.bank(b2_)[0:32, :],
                                                           in1=rope[0:32, 1, :], op=ALU.mult),
                          reads=[self.PB[b2_], ropeb], writes=[self.T2b[i]])
                    S_.op("pool", lambda e: e.tensor_tensor(out=self.QT[64:96, h, ts], in0=self.T1[i][64:96, :],
                                                            in1=self.T2[i][64:96, :], op=ALU.add),
                          reads=[self.T1b[i], self.T2b[i]], writes=[self.QTb[h][blk]])
            return b3

        for h in range(4):
            items.append(Item([(li, f"uq{h}")], mk_b3(h)))
        items.append(Item([], lambda tiles: self.attn_dense("B", li, 2, 96 ** -0.5)))

    def add_mixer_a(self, items, li):
        def mk(which, c):
            def f(tiles):
                (t, tb), = tiles
                dst = self.QT if which == "q" else self.KT
                dstb = self.QTb if which == "q" else self.KTb
                for blk in range(4):
                    ts = slice(blk * 512, (blk + 1) * 512)
                    self.proj_rope_chunk(t, tb, 0, "A", blk, [(0, 128, dst[:, c, ts], dstb[c][blk])])
            return f

        for c in range(2):
            items.append(Item([(li, f"a_q{c}")], mk("q", c)))
        for c in range(2):
            items.append(Item([(li, f"a_k{c}")], mk("k", c)))

        def av(tiles):
            (t, tb), = tiles
            self.proj_v(t, tb, 0, 8, self.HN, lambda kc, jt: [self.HNb[kc][jt // 4]])

        items.append(Item([(li, "a_v")], av))
        items.append(Item([], lambda tiles: self.attn_dense("A", li, 0, 0.125)))

    def add_mixer_c(self, items, li):
        def mk(which, c):
            def f(tiles):
                (t, tb), = tiles
                dst = self.QT if which == "q" else self.KT
                dstb = self.QTb if which == "q" else self.KTb
                for blk in range(4):
                    ts = slice(blk * 512, (blk + 1) * 512)
                    self.proj_rope_chunk(t, tb, 0, "C", blk,
                                         [(0, 64, dst[0:64, 2 * c, ts], dstb[2 * c][blk]),
                                          (64, 128, dst[0:64, 2 * c + 1, ts], dstb[2 * c + 1][blk])])
            return f

        for c in range(2):
            items.append(Item([(li, f"c_q{c}")], mk("q", c)))
        for c in range(2):
            items.append(Item([(li, f"c_k{c}")], mk("k", c)))

        def cvf(tiles):
            (t, tb), = tiles
            self.proj_v(t, tb, 0, 8, self.HN, lambda kc, jt: [self.HNb[kc][jt // 4]])

        items.append(Item([(li, "c_v")], cvf))
        items.append(Item([], lambda tiles: self.attn_dense("C", li, 4, 32 ** -0.5)))

    def add_mixer_d(self, items, li):
        S_ = self.S

        def mk(which):
            def f(tiles):
                (t, tb), = tiles
                dst = self.QT if which == "q" else self.KT
                dstb = self.QTb if which == "q" else self.KTb
                for c in range(2):
                    for blk in range(4):
                        ts = slice(blk * 512, (blk + 1) * 512)
                        b = self.rot("pbank", 7)
                        for kc in range(8):
                            self.mm(self.bank(b), [self.PB[b]], t[:, kc, c * 128:(c + 1) * 128], self.HN[:, kc, ts],
                                    [tb, self.HNb[kc][blk]], kc == 0, kc == 7)
                        self.evac_copy(dst[:, c, ts], self.bank(b), [self.PB[b]], [dstb[c][blk]])
            return f

        items.append(Item([(li, "d_q")], mk("q")))
        items.append(Item([(li, "d_k")], mk("k")))

        def dv(tiles):
            (t, tb), = tiles
            self.proj_v(t, tb, 0, 8, self.HN, lambda kc, jt: [self.HNb[kc][jt // 4]])

        items.append(Item([(li, "d_v")], dv))

        def na(tiles):
            steps = [(rq, hp) for hp in range(2) for rq in range(32)]
            SR = [(0, 1), (2, 3), (4, 5)]
            OB = [6, 7]
            LA = 2
            plans = {}

            def qk(k):
                rq, hp = steps[k]
                kt0, nt, t0 = _na_row_plan(rq)
                plans[k] = (kt0, nt, t0)
                sr = SR[k % 3]
                last = None
                n = 2 * nt
                cnt = 0
                for j in range(2):
                    for m in range(nt):
                        cnt += 1
                        col = j * 512 + m * 64
                        o = self.PS[:, sr[0] * 512 + col: sr[0] * 512 + col + 64]
                        kt = kt0 + m
                        self.mm(o, [self.PB[sr[j]]],
                                self.KT[j * 64:(j + 1) * 64, hp, kt * 128:(kt + 1) * 128],
                                self.QT[j * 64:(j + 1) * 64, hp, rq * 64:(rq + 1) * 64],
                                [self.KTb[hp][kt // 4], self.QTb[hp][rq // 8]], True, True, signal=(cnt == n))

            for k in range(min(LA, len(steps))):
                qk(k)
            for k, (rq, hp) in enumerate(steps):
                if k + LA < len(steps):
                    qk(k + LA)
                kt0, nt, t0 = plans.pop(k)
                sr = SR[k % 3]
                pi = k % 3
                ptv = self.PTN[pi]
                sv = self.PS[:, sr[0] * 512: sr[0] * 512 + 1024].rearrange("p (j n) -> p j n", j=2)[:, :, 0:nt * 64]
                S_.op("act", lambda e: e.activation(out=ptv[:, :, 0:nt * 64], in_=sv, func=AF.Exp, scale=0.125),
                      reads=[self.PB[sr[0]], self.PB[sr[1]]], writes=[self.PTNb[pi]])
                S_.op("pool", lambda e: e.tensor_tensor(out=ptv[:, :, 0:nt * 64], in0=ptv[:, :, 0:nt * 64],
                                                        in1=self.W[:, 2 * hp:2 * hp + 2, t0 * 64:(t0 + nt) * 64], op=ALU.mult),
                      reads=[self.PTNb[pi], self.Wb], writes=[self.PTNb[pi]])
                ob = OB[(k // 4) % 2]
                for j in range(2):
                    for m in range(nt):
                        oc = ((rq % 4) * 2 + j) * 64
                        self.mm(self.bank(ob)[:, oc:oc + 64], [self.PB[ob]], self.VA[:, kt0 + m, 2 * hp + j, :],
                                ptv[:, j, m * 64:(m + 1) * 64], [self.VAb[kt0 + m], self.PTNb[pi]], m == 0, m == nt - 1,
                                signal=(j == 1 and m == nt - 1))
                if k % 4 == 3:
                    r = self.rot("rec", 2)
                    rq0 = rq - 3
                    qs = slice(rq0 * 64, rq0 * 64 + 256)
                    blk = rq0 // 8
                    ov = self.bank(ob).rearrange("p (r j d) -> p r j d", r=4, j=2)
                    rv = self.REC[r].rearrange("p (r j d) -> p r j d", r=4, j=2)
                    S_.op("dve", lambda e: e.reciprocal(out=rv[64:128, :, 0, :], in_=ov[64:128, :, 0, :]),
                          reads=[self.PB[ob]], writes=[self.RECb[r]])
                    S_.op("dve", lambda e: e.reciprocal(out=rv[0:64, :, 1, :], in_=ov[0:64, :, 1, :]),
                          reads=[self.PB[ob]], writes=[self.RECb[r]])
                    S_.op("dve", lambda e: e.tensor_tensor(out=self.MIX[0:64, 6 + hp, qs].rearrange("p (r d) -> p r d", r=4),
                                                           in0=ov[0:64, :, 0, :], in1=rv[64:128, :, 0, :], op=ALU.mult),
                          reads=[self.PB[ob], self.RECb[r]], writes=[self.MIXb[6 + hp][blk]])
                    S_.op("dve", lambda e: e.tensor_tensor(out=self.MIX[64:128, 6 + hp, qs].rearrange("p (r d) -> p r d", r=4),
                                                           in0=ov[64:128, :, 1, :], in1=rv[0:64, :, 1, :], op=ALU.mult),
                          reads=[self.PB[ob], self.RECb[r]], writes=[self.MIXb[6 + hp][blk]])

        def na_wrap(tiles):
            self.PTNb = [Buf(), Buf(), Buf()]
            for i in range(3):
                for bb in self.PTb:
                    _merge(self.PTNb[i].r, bb.w)
                    _merge(self.PTNb[i].r, bb.r)
            na(tiles)

        items.append(Item([], na_wrap))

    def add_phase_b(self, items, li):
        S_ = self.S
        nl = len(self.layer_ids)
        lastl = li == nl - 1

        def start(tiles):
            S_.barrier()
            mixb = self.MIXb
            self.new_h_bufs()
            self.MIXb = mixb
            self._rsb = None
            for b in self.PB:
                b.w = {}
                b.r = {}
            self.load_h(self.hin if li == 0 else self.hs)

        items.append(Item([], start))

        def wo_all(tiles):
            rsw = [self.f32v(130 + 2 * i, 512) for i in range(2)]
            for blk in range(4):
                ts = slice(blk * 512, (blk + 1) * 512)
                for n in range(8):
                    t, tb = tiles[n // 3]
                    c0 = (n % 3) * 128
                    b = self.rot("pbank", 7)
                    for kc in range(8):
                        self.mm(self.bank(b), [self.PB[b]], t[:, kc, c0:c0 + 128], self.MIX[:, kc, ts], [tb, self.MIXb[kc][blk]],
                                kc == 0, kc == 7)
                    S_.op("dve", lambda e: e.tensor_tensor(out=self.H[:, n, ts], in0=self.bank(b), in1=self.H[:, n, ts], op=ALU.add),
                          reads=[self.PB[b], self.Hb[n][blk]], writes=[self.Hb[n][blk]])
                self.norm_s1(blk)
                if blk >= 1:
                    self.norm_s2(li, "ffn_norm", blk - 1, rsv=rsw)
            self.norm_s2(li, "ffn_norm", 3, rsv=rsw)

        items.append(Item([(li, "wo_a"), (li, "wo_b"), (li, "wo_c")], wo_all))

        def ffn_start(tiles):
            S_.barrier()
            for b in self.PB:
                b.w = {}
                b.r = {}
            self.ACTBb = [Buf() for _ in range(22)]
            self.CAb = [Buf() for _ in range(3)]
            self.CGb = [Buf() for _ in range(2)]
            self._rsb = None

        items.append(Item([], ffn_start))

        SLOTS = [(0, 1), (2, 3), (4, 5)]

        def mk_up(hf, c):
            def f(tiles):
                (t, tb), = tiles
                t0 = hf * 1024
                res = {}
                for part in range(2):
                    sl = SLOTS[self.rot("fslot", 3)]
                    for tb_ in range(2):
                        ts = slice(t0 + tb_ * 512, t0 + (tb_ + 1) * 512)
                        blk = (t0 + tb_ * 512) // 512
                        for kc in range(8):
                            self.mm(self.bank(sl[tb_]), [self.PB[sl[tb_]]], t[:, kc, part * 128:(part + 1) * 128],
                                    self.HN[:, kc, ts], [tb, self.HNb[kc][blk]], kc == 0, kc == 7)
                    hi = self.rot("halo", 16)
                    hbk = 6 + hi % 2
                    htok = 1024 if hf == 0 else 1023
                    hcol = self.bank(hbk)[:, hi * 2:hi * 2 + 1]
                    for kc in range(8):
                        self.mm(hcol, [self.PB[hbk]], t[:, kc, part * 128:(part + 1) * 128], self.HN[:, kc, htok:htok + 1],
                                [tb, self.HNb[kc][htok // 512]], kc == 0, kc == 7)
                    u = self.PS[:, sl[0] * 512: sl[0] * 512 + 1024]
                    ub = [self.PB[sl[0]], self.PB[sl[1]]]
                    ci = part * 22 + c
                    ai = self.rot("ca", 3)
                    A = self.CA[ai]
                    Ab = self.CAb[ai]
                    S_.op("act", lambda e: e.activation(out=A, in_=u, func=AF.Identity, bias=self.cv(li, "cb", ci),
                                                        scale=self.cv(li, "cw1", ci)), reads=ub + [self.cvb], writes=[Ab])
                    S_.op("dve", lambda e: e.scalar_tensor_tensor(out=A[:, 1:1024], in0=u[:, 0:1023], scalar=self.cv(li, "cw0", ci),
                                                                  in1=A[:, 1:1024], op0=ALU.mult, op1=ALU.add),
                          reads=ub + [self.cvb, Ab], writes=[Ab])
                    S_.op("dve", lambda e: e.scalar_tensor_tensor(out=A[:, 0:1023], in0=u[:, 1:1024], scalar=self.cv(li, "cw2", ci),
                                                                  in1=A[:, 0:1023], op0=ALU.mult, op1=ALU.add),
                          reads=ub + [self.cvb, Ab], writes=[Ab])
                    if hf == 0:
                        S_.op("dve", lambda e: e.scalar_tensor_tensor(out=A[:, 1023:1024], in0=hcol, scalar=self.cv(li, "cw2", ci),
                                                                      in1=A[:, 1023:1024], op0=ALU.mult, op1=ALU.add),
                              reads=[self.PB[hbk], self.cvb, Ab], writes=[Ab])
                    else:
                        S_.op("dve", lambda e: e.scalar_tensor_tensor(out=A[:, 0:1], in0=hcol, scalar=self.cv(li, "cw0", ci),
                                                                      in1=A[:, 0:1], op0=ALU.mult, op1=ALU.add),
                              reads=[self.PB[hbk], self.cvb, Ab], writes=[Ab])
                    res[part] = (A, Ab)
                    if part == 0:
                        gi = self.rot("cg", 2)
                        S_.op("act", lambda e: e.activation(out=self.CG[gi], in_=A, func=AF.Gelu_apprx_tanh),
                              reads=[Ab], writes=[self.CGb[gi]])
                        res["g"] = (self.CG[gi], self.CGb[gi])
                G, Gb = res["g"]
                Av, Avb = res[1]
                S_.op("pool", lambda e: e.tensor_tensor(out=self.ACTB[:, c, :], in0=G, in1=Av, op=ALU.mult),
                      reads=[Gb, Avb], writes=[self.ACTBb[c]])
            return f

        def mk_dn(hf, n):
            def f(tiles):
                (t, tb), = tiles
                for tb_ in range(2):
                    blk = hf * 2 + tb_
                    ts = slice(blk * 512, (blk + 1) * 512)
                    b = self.rot("dbank", 6)
                    for kc in range(22):
                        self.mm(self.bank(b), [self.PB[b]], t[:, kc, :], self.ACTB[:, kc, tb_ * 512:(tb_ + 1) * 512],
                                [tb, self.ACTBb[kc]], kc == 0, kc == 21)
                    S_.op("dve", lambda e: e.tensor_tensor(out=self.H[:, n, ts], in0=self.bank(b), in1=self.H[:, n, ts], op=ALU.add),
                          reads=[self.PB[b], self.Hb[n][blk]], writes=[self.Hb[n][blk]])
            return f

        for hf in range(2):
            for c in range(22):
                items.append(Item([(li, f"up{c}")], mk_up(hf, c)))
            for n in range(8):
                items.append(Item([(li, f"dn{n}")], mk_dn(hf, n)))

        def ple_start(tiles):
            S_.barrier()
            for b in self.PB:
                b.w = {}
                b.r = {}
            self.PBtb = Buf()
            self.PPWb = Buf()
            self.ETb = [Buf(), Buf()]
            self.CAb = [Buf() for _ in range(3)]
            S_.dma("pool", [(self.PBt[:, k, :], self.pT[li, k * 128:(k + 1) * 128, :]) for k in range(2)],
                   writes=[self.PBtb], key=("pbt",))
            o, kc, m = WT_OFF["pp"]
            S_.dma("pool", [(self.PPW, self.wpk[li, :, o:o + 2048].rearrange("p (k n) -> p k n", k=2))],
                   writes=[self.PPWb], key=("ppw",))
            self.rmsnorm_main(li, "ple_norm")

        items.append(Item([], ple_start))

        def ple_all(tiles):
            hs_v = (self.hout if lastl else self.hs).rearrange("(k p) t -> p k t", p=128)

            def post(blk):
                ts = slice(blk * 512, (blk + 1) * 512)
                if lastl and not self.do_final:
                    pass
                elif lastl:
                    self.norm_s2(li, None, blk, final=True, gcol=NLAYER * NCL)
                else:
                    self.norm_s2(li + 1, "attn_norm", blk)
                if lastl:
                    S_.dma("sp", [(hs_v[:, :, ts], self.H[:, :, ts])], reads=[self.Hb[k][blk] for k in range(8)], key=("hst",))

            for blk in range(4):
                ts = slice(blk * 512, (blk + 1) * 512)
                for n in range(8):
                    t, tb = tiles[n // 3]
                    c0 = (n % 3) * 128
                    bg = self.rot("pbank", 7)
                    be = self.rot("pbank", 7)
                    for kc in range(8):
                        self.mm(self.bank(bg), [self.PB[bg]], t[:, kc, c0:c0 + 128], self.HN[:, kc, ts], [tb, self.HNb[kc][blk]],
                                kc == 0, kc == 7)
                    for kc in range(2):
                        self.mm(self.bank(be), [self.PB[be]], self.PPW[:, kc, n * 128:(n + 1) * 128], self.PBt[:, kc, ts],
                                [self.PPWb, self.PBtb], kc == 0, kc == 1)
                    ei = self.rot("et", 2)
                    S_.op("act", lambda e: e.activation(out=self.ET[ei], in_=self.bank(bg), func=AF.Sigmoid),
                          reads=[self.PB[bg]], writes=[self.ETb[ei]])
                    S_.op("dve", lambda e: e.tensor_tensor(out=self.ET[ei], in0=self.bank(be), in1=self.ET[ei], op=ALU.mult),
                          reads=[self.PB[be], self.ETb[ei]], writes=[self.ETb[ei]])
                    S_.op("pool", lambda e: e.tensor_tensor(out=self.H[:, n, ts], in0=self.H[:, n, ts], in1=self.ET[ei], op=ALU.add),
                          reads=[self.ETb[ei], self.Hb[n][blk]], writes=[self.Hb[n][blk]])
                if not lastl:
                    S_.dma("sp", [(hs_v[:, :, ts], self.H[:, :, ts])], reads=[self.Hb[k][blk] for k in range(8)], key=("hst",))
                if not (lastl and not self.do_final):
                    self.norm_s1(blk)
                if blk >= 1:
                    post(blk - 1)
            post(3)

        items.append(Item([(li, "pg_a"), (li, "pg_b"), (li, "pg_c")], ple_all))

        def finish(tiles):
            S_.barrier()
            S_.epoch += 1

        items.append(Item([], finish))


_CACHE = {}


def _get_prog(layer_ids, src_is_x, do_final):
    key = (tuple(range(len(layer_ids))), do_final)
    if key not in _CACHE:
        _CACHE[key] = Builder(list(range(len(layer_ids))), src_is_x, do_final).nc
    return _CACHE[key]


def kernel(**inputs):
    inp = {k: np.asarray(v) for k, v in inputs.items()}
    x = inp["x"].astype(np.float32, copy=False)
    p = inp["p"].astype(np.float32, copy=False)
    B = x.shape[0]
    ncores = 8
    cv = _cvec(inp)
    ropes = _rope_tables()
    tm = _tmask()
    wpk_all = np.stack([_pack_layer(inp, l) for l in range(NLAYER)])
    bw_all = np.stack([_bias_w(np.asarray(inp["na_rpb"][l], np.float32)) for l in range(NLAYER)])
    lam_all = np.stack([np.concatenate([inp["lam_q1"][l], inp["lam_k1"][l], inp["lam_q2"][l], inp["lam_k2"][l]])
                        for l in range(NLAYER)]).astype(np.float32)

    def run(hT_list, layers, do_final):
        nc = _get_prog(layers, False, do_final)
        cvl = np.zeros_like(cv)
        for i, l in enumerate(layers):
            cvl[:, i * NCL:(i + 1) * NCL] = cv[:, l * NCL:(l + 1) * NCL]
        cvl[:, NLAYER * NCL:] = cv[:, NLAYER * NCL:]
        wl = np.ascontiguousarray(wpk_all[layers])
        bl = np.ascontiguousarray(bw_all[layers])
        ll = np.ascontiguousarray(lam_all[layers])
        in_maps = []
        for b in range(ncores):
            in_maps.append({
                "hin": hT_list[b],
                "pT": np.ascontiguousarray(p[layers, b].transpose(0, 2, 1)),
                "wpk": wl, "cvec": cvl, "lamv": ll, "biasw": bl, "tmask": tm,
                "ropeA": ropes["ropeA"], "ropeC": ropes["ropeC"],
            })
        res = run_bass_kernel_spmd(nc, in_maps, core_ids=list(range(ncores)))
        return [np.asarray(r["hout"]) for r in res.results]

    hT = [np.ascontiguousarray(x[b].T) for b in range(B)]
    if FUSED:
        hT = run(hT, list(range(NLAYER)), True)
    else:
        for l in range(NLAYER):
            hT = run(hT, [l], l == NLAYER - 1)
    out = np.stack([h.T for h in hT]).astype(np.float32)
    return out
```

```python
import math
from contextlib import ExitStack
import numpy as np
import concourse.bass as bass
import concourse.mybir as mybir
from concourse.bass_utils import run_bass_kernel_spmd

F32 = mybir.dt.float32
BF16 = mybir.dt.bfloat16
AF = mybir.ActivationFunctionType
ALU = mybir.AluOpType
AX = mybir.AxisListType

FUSED = True
NLAYER = 4
S = 2048
D = 1024
DFF = 2816
NEGB = -30000.0
import os as _os
NFILL = {"A": int(_os.environ.get("NF_A", "1")), "B": int(_os.environ.get("NF_B", "1")), "C": int(_os.environ.get("NF_C", "1"))}
FILLN = int(_os.environ.get("FILLN", "256"))

A_OFF, B_OFF, C_OFF, D_OFF = 0, 768, 1440, 2208


def _sw(n, d):
    idx = np.arange(n)
    return (idx // d) * d + (idx % d + d // 2) % d


WT_SPECS = []


def _build_specs():
    sp = []
    sp.append(("b_cq", 8, 384))
    sp.append(("b_ckv", 8, 320))
    sp.append(("ukv", 2, 512))
    for h in range(4):
        sp.append((f"uq{h}", 3, 128))
    for c in range(2):
        sp.append((f"a_q{c}", 8, 256))
    for c in range(2):
        sp.append((f"a_k{c}", 8, 256))
    sp.append(("a_v", 8, 256))
    for c in range(2):
        sp.append((f"c_q{c}", 8, 256))
    for c in range(2):
        sp.append((f"c_k{c}", 8, 256))
    sp.append(("c_v", 8, 256))
    sp.append(("d_q", 8, 256))
    sp.append(("d_k", 8, 256))
    sp.append(("d_v", 8, 256))
    for g, m in (("a", 384), ("b", 384), ("c", 256)):
        sp.append((f"wo_{g}", 8, m))
    for c in range(22):
        sp.append((f"up{c}", 8, 256))
    for n in range(8):
        sp.append((f"dn{n}", 22, 128))
    for g, m in (("a", 384), ("b", 384), ("c", 256)):
        sp.append((f"pg_{g}", 8, m))
    sp.append(("pp", 2, 1024))
    return sp


WT_SPECS = _build_specs()
WT_OFF = {}
_o = 0
for _n, _kc, _m in WT_SPECS:
    WT_OFF[_n] = (_o, _kc, _m)
    _o += _kc * _m
WTOT = _o
SLOT = 3072
NSLOT = 4

CV = {}
_c = 0
for _n, _w in (("attn_norm", 8), ("ffn_norm", 8), ("ple_norm", 8), ("q_norm", 3), ("kv_norm", 2), ("subln", 1),
               ("cw0", 44), ("cw1", 44), ("cw2", 44), ("cb", 44), ("laminit", 1), ("omli", 1)):
    CV[_n] = _c
    _c += _w
NCL = _c
NCV = NLAYER * NCL + 8

NTW = 19
TM_F0 = 1408
TM_W = 2944


def _T(mat):
    K, M = mat.shape
    kc = K // 128
    return mat.reshape(kc, 128, M).transpose(1, 0, 2).reshape(128, kc * M)


def _layer_tiles(inp, l):
    w_in = inp["w_in"][l]
    w_uq = inp["w_uq"][l]
    w_ukv = inp["w_ukv"][l]
    t = {}
    ar = np.arange
    t["b_cq"] = w_in[:, B_OFF:B_OFF + 384]
    t["b_ckv"] = w_in[:, np.concatenate([B_OFF + 384 + ar(256), B_OFF + 640 + ar(32), B_OFF + 640 + _sw(32, 32)])]
    kn = np.concatenate([h * 128 + ar(64) for h in range(4)])
    vv = np.concatenate([h * 128 + 64 + ar(64) for h in range(4)])
    t["ukv"] = w_ukv[:, np.concatenate([kn, vv])]
    for h in range(4):
        t[f"uq{h}"] = w_uq[:, np.concatenate([h * 96 + ar(96), h * 96 + 64 + _sw(32, 32)])]
    for c in range(2):
        t[f"a_q{c}"] = w_in[:, np.concatenate([A_OFF + c * 128 + ar(128), A_OFF + c * 128 + _sw(128, 64)])]
        t[f"a_k{c}"] = w_in[:, np.concatenate([A_OFF + 256 + c * 128 + ar(128), A_OFF + 256 + c * 128 + _sw(128, 64)])]
        t[f"c_q{c}"] = w_in[:, np.concatenate([C_OFF + c * 128 + ar(128), C_OFF + c * 128 + _sw(128, 32)])]
        t[f"c_k{c}"] = w_in[:, np.concatenate([C_OFF + 256 + c * 128 + ar(128), C_OFF + 256 + c * 128 + _sw(128, 32)])]
    t["a_v"] = w_in[:, A_OFF + 512:A_OFF + 768]
    t["c_v"] = w_in[:, C_OFF + 512:C_OFF + 768]
    t["d_q"] = w_in[:, D_OFF:D_OFF + 256]
    t["d_k"] = w_in[:, D_OFF + 256:D_OFF + 512]
    t["d_v"] = w_in[:, D_OFF + 512:D_OFF + 768]
    for n in range(8):
        t[f"dn{n}"] = inp["w_down"][l][:, n * 128:(n + 1) * 128]
    for g, (c0, c1) in (("a", (0, 384)), ("b", (384, 768)), ("c", (768, 1024))):
        t[f"wo_{g}"] = inp["w_o"][l][:, c0:c1]
        t[f"pg_{g}"] = inp["w_ple_gate"][l][:, c0:c1]
    w_up = inp["w_up"][l]
    for c in range(22):
        t[f"up{c}"] = w_up[:, np.concatenate([c * 128 + ar(128), DFF + c * 128 + ar(128)])]
    t["pp"] = inp["w_ple_proj"][l]
    return t


def _pack_layer(inp, l):
    t = _layer_tiles(inp, l)
    out = np.empty((128, WTOT), np.float32)
    for n, kc, m in WT_SPECS:
        o = WT_OFF[n][0]
        out[:, o:o + kc * m] = _T(np.asarray(t[n], np.float32))
    return out


def _cvec(inp):
    cv = np.zeros((128, NCV), np.float32)

    def put(col, v):
        v = np.asarray(v, np.float32)
        n = v.shape[0] // 128
        cv[:, col:col + n] = v.reshape(n, 128).T

    for l in range(NLAYER):
        b = l * NCL
        put(b + CV["attn_norm"], inp["attn_norm"][l])
        put(b + CV["ffn_norm"], inp["ffn_norm"][l])
        put(b + CV["ple_norm"], inp["ple_norm"][l])
        put(b + CV["q_norm"], inp["mla_q_norm"][l])
        put(b + CV["kv_norm"], inp["mla_kv_norm"][l])
        put(b + CV["subln"], np.tile(np.asarray(inp["diff_subln"][l]), 2))
        for j in range(3):
            put(b + CV[f"cw{j}"], inp["conv_w"][l][j])
        put(b + CV["cb"], inp["conv_b"][l])
        lam_init = 0.8 - 0.6 * math.exp(-0.3 * l)
        cv[:, b + CV["laminit"]] = lam_init
        cv[:, b + CV["omli"]] = 1.0 - lam_init
    put(NLAYER * NCL, inp["final_norm"])
    return cv


def _rope_tables():
    pos = np.arange(S, dtype=np.float32)
    out = {}
    for name, d in (("ropeA", 64), ("ropeC", 32)):
        inv = np.power(np.float32(10000.0), -np.arange(0, d, 2, dtype=np.float32) / np.float32(d)).astype(np.float32)
        ang = (pos[:, None] * inv[None, :]).astype(np.float32)
        cos = np.cos(ang).astype(np.float32)
        sin = np.sin(ang).astype(np.float32)
        p = np.arange(128)
        j = p % d
        i = j % (d // 2)
        sign = np.where(j < d // 2, -1.0, 1.0).astype(np.float32)
        tab = np.empty((4, 128, 2, 512), np.float32)
        cosT = cos[:, i].T
        sinT = (sin[:, i] * sign[None, :]).T
        for blk in range(4):
            tab[blk, :, 0, :] = cosT[:, blk * 512:(blk + 1) * 512]
            tab[blk, :, 1, :] = sinT[:, blk * 512:(blk + 1) * 512]
        out[name] = tab
    return out


def _na_tile_dr0():
    tiles = []
    for i in range(7):
        tiles.append((-6 + 2 * i, (True, True)))
    for i in range(7):
        tiles.append((-7 + 2 * i, (True, True)))
    for i in range(5):
        dr0 = -5 + 2 * i
        tiles.append((dr0, (i != 0, i != 4)))
    return tiles


def _bias_w(rpb_l):
    tiles = _na_tile_dr0()
    cq = np.arange(64)
    ck = np.arange(64)
    c_start = np.clip(cq - 8, 0, 48)
    ok = (ck[:, None] >= c_start[None, :]) & (ck[:, None] < c_start[None, :] + 16)
    dc = np.clip(ck[:, None] - cq[None, :], -15, 15) + 15
    out = np.full((128, 4, NTW, 64), NEGB, np.float32)
    for h in range(4):
        for t, (dr0, valid) in enumerate(tiles):
            for a in range(2):
                if not valid[a]:
                    continue
                dr = dr0 + a
                g = rpb_l[h, dr + 7][dc]
                out[a * 64:(a + 1) * 64, h, t, :] = np.where(ok, g, np.float32(NEGB))
    return out.reshape(128, 4 * NTW * 64)


def _tmask():
    p = np.arange(128)[:, None]
    f = np.arange(TM_W)[None, :] - TM_F0
    d = f - p
    ad = np.abs(d)
    m = (ad <= 64).astype(np.float32) + ((d % 4 == 0) & (ad <= 256)) + ((d % 16 == 0) & (ad <= 1024))
    return m.astype(np.float32)


def _na_row_plan(rq):
    kr0 = min(max(rq - 4, 0), 24)
    if kr0 % 2 == 0:
        kt0 = kr0 // 2
        nt = 4
        dr0 = kr0 - rq
        if dr0 % 2 == 0:
            t0 = (dr0 + 6) // 2
        else:
            t0 = 7 + (dr0 + 7) // 2
    else:
        kt0 = (kr0 - 1) // 2
        nt = 5
        t0 = 14
    return kt0, nt, t0


class Buf:
    __slots__ = ("w", "r", "name", "excl")

    def __init__(self, name="", excl=False):
        self.w = {}
        self.r = {}
        self.name = name
        self.excl = excl


def _merge(d, s):
    for k, v in s.items():
        if d.get(k, 0) < v:
            d[k] = v


class Sched:
    def __init__(self, nc, es):
        self.nc = nc
        self.es = es
        self.e = dict(pe=nc.tensor, act=nc.scalar, dve=nc.vector, pool=nc.gpsimd, sp=nc.sync)
        self.sems = {}
        self.cnt = {}
        self.waited = {k: {} for k in self.e}
        self.epoch = 0
        self.pend = {k: [] for k in self.e}
        self.nwait = 0
        self.nins = 0

    def sem(self, key):
        if key not in self.sems:
            nm = "s_" + "_".join(str(x) for x in key)
            self.sems[key] = self.es.enter_context(self.nc.semaphore(nm))
            self.cnt[key] = 0
        return self.sems[key]

    def wait(self, eng, deps):
        w = self.waited[eng]
        for k, v in deps.items():
            if w.get(k, 0) < v:
                self.e[eng].wait_ge(self.sem(k), v)
                w[k] = v
                self.nwait += 1

    def deps_of(self, reads, writes):
        d = {}
        for b in reads:
            _merge(d, b.w)
            if b.excl:
                _merge(d, b.r)
        for b in writes:
            _merge(d, b.w)
            _merge(d, b.r)
        return d

    def op(self, eng, fn, reads=(), writes=(), signal=True):
        d = self.deps_of(reads, writes)
        self.wait(eng, d)
        ins = fn(self.e[eng])
        self.nins += 1
        self.pend[eng].append((reads, writes))
        if not signal:
            return None
        key = (eng, self.epoch)
        s = self.sem(key)
        self.cnt[key] += 1
        ins.then_inc(s, 1)
        tok = {key: self.cnt[key]}
        for rs, ws in self.pend[eng]:
            for b in rs:
                if b.excl:
                    b.w = dict(tok)
                    b.r = {}
                else:
                    _merge(b.r, tok)
            for b in ws:
                b.w = dict(tok)
                b.r = {}
        self.pend[eng] = []
        return tok

    def dma(self, q, pairs, reads=(), writes=(), key=None, **kw):
        d = self.deps_of(reads, writes)
        self.wait(q, d)
        s = self.sem(key)
        for out, in_ in pairs:
            ins = self.e[q].dma_start(out=out, in_=in_, **kw)
            self.cnt[key] += 16
            ins.then_inc(s, 16)
            self.nins += 1
        tok = {key: self.cnt[key]}
        for b in reads:
            _merge(b.r, tok)
        for b in writes:
            b.w = dict(tok)
            b.r = {}
        return tok

    def barrier(self):
        d = {k: v for k, v in self.cnt.items() if v > 0}
        for eng in self.e:
            self.wait(eng, d)


class Item:
    def __init__(self, w, fn):
        self.w = w
        self.fn = fn


KB = 256
ARENA_KB = 206


class Builder:
    def __init__(self, layer_ids, src_is_x, do_final):
        self.layer_ids = layer_ids
        self.do_final = do_final
        nl = len(layer_ids)
        nc = bass.Bass("TRN2", target_bir_lowering=False)
        self.nc = nc
        self.hin = nc.dram_tensor("hin", [D, S], F32, kind="ExternalInput").ap()
        self.pT = nc.dram_tensor("pT", [nl, 256, S], F32, kind="ExternalInput").ap()
        self.wpk = nc.dram_tensor("wpk", [nl, 128, WTOT], F32, kind="ExternalInput").ap()
        self.cvec = nc.dram_tensor("cvec", [128, NCV], F32, kind="ExternalInput").ap()
        self.lamv = nc.dram_tensor("lamv", [nl, 128], F32, kind="ExternalInput").ap()
        self.biasw = nc.dram_tensor("biasw", [nl, 128, 4 * NTW * 64], F32, kind="ExternalInput").ap()
        self.tmask = nc.dram_tensor("tmask", [128, TM_W], F32, kind="ExternalInput").ap()
        self.ropeA = nc.dram_tensor("ropeA", [4, 128, 2, 512], F32, kind="ExternalInput").ap()
        self.ropeC = nc.dram_tensor("ropeC", [4, 128, 2, 512], F32, kind="ExternalInput").ap()
        self.hout = nc.dram_tensor("hout", [D, S], F32, kind="ExternalOutput").ap()
        self.hs = nc.dram_tensor("hscr", [D, S], F32, kind="Internal").ap()
        with ExitStack() as es:
            self.es = es
            self.S = Sched(nc, es)
            self.AR = es.enter_context(nc.sbuf_tensor("arena", [128, ARENA_KB * KB], F32))
            self.PS = es.enter_context(nc.psum_tensor("ps", [128, 4096], F32))
            self.PB = [Buf(f"pb{i}", excl=True) for i in range(8)]
            self._layout()
            self._emit()

    def f32v(self, off_kb, n):
        o = int(off_kb * KB)
        return self.AR[:, o:o + n]

    def bfv(self, off_kb, n):
        o = int(off_kb * KB)
        return self.AR[:, o:o + n // 2].bitcast(BF16)

    def _layout(self):
        self.HN = self.bfv(0, 8 * S).rearrange("p (c t) -> p c t", c=8)
        self.RING = [self.bfv(32 + 6 * i, SLOT) for i in range(NSLOT)]
        self.RINGb = [Buf(f"ring{i}") for i in range(NSLOT)]
        self.CVt = self.f32v(56, NCV)
        self.ONES = self.bfv(60, 128)
        self.ONESBD = self.bfv(60.25, 128)
        self.LQ = self.f32v(60.5, 128)
        self.SM = self.f32v(61, 64)
        self.TMP32 = self.f32v(61.25, 64)
        self.PPW = self.bfv(62, 2048).rearrange("p (k n) -> p k n", k=2)
        P0 = 66
        self.QT = self.bfv(P0, 4 * S).rearrange("p (c t) -> p c t", c=4)
        self.KT = self.bfv(P0 + 16, 4 * S).rearrange("p (c t) -> p c t", c=4)
        self.VA = self.bfv(P0 + 32, 16 * 4 * 128).rearrange("p (j h d) -> p j h d", j=16, h=4)
        self.VAflat = self.bfv(P0 + 32, 16 * 4 * 128)
        o = P0 + 48
        self.ROPE = [self.f32v(o + 4 * i, 1024).rearrange("p (a t) -> p a t", a=2) for i in range(2)]
        o += 8
        self.TM = self.bfv(o, TM_W)
        o += 6
        self.W = self.bfv(o, 4 * NTW * 64).rearrange("p (h n) -> p h n", h=4)
        o += 10
        self.PT = [self.bfv(o + i, 512) for i in range(6)]
        self.PTpair = [self.bfv(o + 2 * i, 1024) for i in range(3)]
        self.PTN = [self.bfv(o + 1.25 * i, 640).rearrange("p (j n) -> p j n", j=2) for i in range(3)]
        o += 6
        self.T1 = [self.f32v(o + 2 * i, 512) for i in range(2)]
        o += 4
        self.T2 = [self.f32v(o + 2 * i, 512) for i in range(2)]
        o += 4
        self.REC = [self.f32v(o + 2 * i, 512) for i in range(2)]
        o += 4
        self.RS = [self.f32v(o + 2 * i, 512) for i in range(2)]
        o += 4
        self.OC = [self.f32v(o + 2 * i, 512) for i in range(2)]
        o += 4
        self.BSTG = self.f32v(o, NTW * 64)
        o += 5
        self.SQT = self.bfv(o, 5 * 512).rearrange("p (c t) -> p c t", c=5)
        o += 5
        self.MIXo = o
        self.MIX = self.bfv(o, 8 * S).rearrange("p (c t) -> p c t", c=8)
        o += 32
        assert o <= ARENA_KB, o
        self.H = self.f32v(P0, 8 * S).rearrange("p (c t) -> p c t", c=8)
        ob = P0 + 64
        assert ob <= self.MIXo
        self.ACTB = self.bfv(ob, 22 * 1024).rearrange("p (c t) -> p c t", c=22)
        self.PBt = self.bfv(ob, 2 * S).rearrange("p (k t) -> p k t", k=2)
        self.ET = [self.f32v(ob + 8 + 2 * i, 512) for i in range(2)]
        ob += 44
        self.CA = [self.f32v(ob + 4 * i, 1024) for i in range(3)]
        ob += 12
        self.CG = [self.bfv(ob + 2 * i, 1024) for i in range(2)]
        ob += 4
        self.RSB = [self.f32v(ob + 2 * i, 512) for i in range(2)]
        ob += 4
        assert ob <= ARENA_KB, ob

    def bank(self, b, n=512, off=0):
        return self.PS[:, b * 512 + off:b * 512 + off + n]

    def mm(self, out, obufs, lhsT, rhs, rd, start, stop, signal=None):
        if signal is None:
            signal = stop
        return self.S.op("pe", lambda e: e.matmul(out, lhsT=lhsT, rhs=rhs, start=start, stop=stop),
                         reads=rd, writes=obufs, signal=signal)

    def cv(self, li, name, idx=0):
        c = li * NCL + CV[name] + idx
        return self.CVt[:, c:c + 1]

    def run_items(self, items):
        reqs = []
        for i, it in enumerate(items):
            for w in it.w:
                reqs.append(w)
        state = {"issued": 0}

        def issue_upto(n):
            n = min(n, len(reqs))
            while state["issued"] < n:
                j = state["issued"]
                li, name = reqs[j]
                o, kc, m = WT_OFF[name]
                sl = j % NSLOT
                dst = self.RING[sl][:, 0:kc * m].rearrange("p (k m) -> p k m", k=kc)
                src = self.wpk[li, :, o:o + kc * m].rearrange("p (k m) -> p k m", k=kc)
                self.S.dma("pool", [(dst, src)], writes=[self.RINGb[sl]], key=("ring", sl))
                state["issued"] += 1

        ptr = 0
        for it in items:
            n = len(it.w)
            issue_upto(ptr + n)
            tiles = []
            for j in range(ptr, ptr + n):
                li, name = reqs[j]
                o, kc, m = WT_OFF[name]
                sl = j % NSLOT
                tiles.append((self.RING[sl][:, 0:kc * m].rearrange("p (k m) -> p k m", k=kc), self.RINGb[sl]))
            it.fn(tiles)
            ptr += n
            issue_upto(ptr + NSLOT)

    def _emit(self):
        S_ = self.S
        nc = self.nc
        items = []
        nl = len(self.layer_ids)

        def add(w, fn):
            items.append(Item(w, fn))

        add([], self.prologue)
        for li in range(nl):
            self.add_phase_a(items, li)
            self.add_phase_b(items, li)
        import os
        nit = int(os.environ.get("NITEMS", "0"))
        if nit:
            items = items[:nit]
        self.run_items(items)
        S_.barrier()

    def new_h_bufs(self):
        self.Hb = [[Buf(f"h{k}_{b}") for b in range(4)] for k in range(8)]
        self.HNb = [[Buf(f"hn{k}_{b}") for b in range(4)] for k in range(8)]

    def prologue(self, tiles):
        S_ = self.S
        self.new_h_bufs()
        self.cvb = Buf("cv")
        S_.dma("sp", [(self.CVt, self.cvec)], writes=[self.cvb], key=("cv",))
        self.onesb = Buf("ones")
        S_.op("pool", lambda e: e.memset(self.ONES, 1.0), writes=[self.onesb])
        S_.op("pool", lambda e: e.memset(self.ONESBD, 0.0), writes=[self.onesb])
        S_.op("pool", lambda e: e.memset(self.ONESBD[0:64, 0:64], 1.0), writes=[self.onesb])
        S_.op("pool", lambda e: e.memset(self.ONESBD[64:128, 64:128], 1.0), writes=[self.onesb])
        self.load_h(self.hin)
        self.rmsnorm_main(0, "attn_norm")
        S_.barrier()

    def load_h(self, src):
        for kc in range(8):
            self.S.dma("sp", [(self.H[:, kc, :], src[kc * 128:(kc + 1) * 128, :])],
                       writes=self.Hb[kc], key=("hld", kc))

    def store_h(self, dst):
        for kc in range(8):
            self.S.dma("sp", [(dst[kc * 128:(kc + 1) * 128, :], self.H[:, kc, :])],
                       reads=self.Hb[kc], key=("hst",))

    def _rsbufs(self):
        rsb = getattr(self, "_rsb", None)
        if rsb is None:
            rsb = self._rsb = [Buf("rsb0"), Buf("rsb1")]
        return rsb

    def norm_s1(self, blk):
        ts = slice(blk * 512, (blk + 1) * 512)
        hb = [self.Hb[k][blk] for k in range(8)]
        self.S.op("act", lambda e: e.activation(out=self.HN[:, :, ts], in_=self.H[:, :, ts], func=AF.Square),
                  reads=hb, writes=[self.HNb[k][blk] for k in range(8)])

    def norm_s2(self, li, gname, blk, final=False, gcol=None, rsv=None):
        S_ = self.S
        rsb = self._rsbufs()
        rsv = rsv or self.RSB
        ts = slice(blk * 512, (blk + 1) * 512)
        for kc in range(8):
            self.mm(self.bank(7), [self.PB[7]], self.ONES, self.HN[:, kc, ts],
                    [self.HNb[kc][blk], self.onesb], kc == 0, kc == 7)
        r = blk % 2
        rs = rsv[r]
        S_.op("act", lambda e: e.activation(out=rs, in_=self.bank(7), func=AF.Ln, bias=1e-6, scale=1.0 / D),
              reads=[self.PB[7]], writes=[rsb[r]])
        S_.op("act", lambda e: e.activation(out=rs, in_=rs, func=AF.Exp, scale=-0.5), reads=[rsb[r]], writes=[rsb[r]])
        for kc in range(8):
            if gcol is not None:
                g = self.CVt[:, gcol + kc:gcol + kc + 1]
            else:
                g = self.cv(li, gname, kc)
            if final:
                S_.op("dve", lambda e: e.scalar_tensor_tensor(out=self.H[:, kc, ts], in0=self.H[:, kc, ts], scalar=g,
                                                              in1=rs, op0=ALU.mult, op1=ALU.mult),
                      reads=[rsb[r], self.cvb], writes=[self.Hb[kc][blk]])
            else:
                S_.op("dve", lambda e: e.scalar_tensor_tensor(out=self.HN[:, kc, ts], in0=self.H[:, kc, ts], scalar=g,
                                                              in1=rs, op0=ALU.mult, op1=ALU.mult),
                      reads=[self.Hb[kc][blk], rsb[r], self.cvb], writes=[self.HNb[kc][blk]])

    def rmsnorm_main(self, li, gname, final=False, gcol=None):
        for blk in range(4):
            self.norm_s1(blk)
            self.norm_s2(li, gname, blk, final=final, gcol=gcol)

    def get_rope(self, table, blk):
        i = self._rope_i = (getattr(self, "_rope_i", -1) + 1) % 2
        src = (self.ropeA if table == "A" else self.ropeC)[blk]
        self.S.dma("sp", [(self.ROPE[i], src)], writes=[self.ROPEb[i]], key=("rope", i))
        return self.ROPE[i], self.ROPEb[i]

    def rot(self, name, n):
        v = getattr(self, "_rot_" + name, -1)
        v = (v + 1) % n
        setattr(self, "_rot_" + name, v)
        return v

    def add_phase_a(self, items, li):
        S_ = self.S

        def start(tiles):
            self.QTb = [[Buf() for _ in range(4)] for _ in range(4)]
            self.KTb = [[Buf() for _ in range(4)] for _ in range(4)]
            self.VAb = [Buf() for _ in range(16)]
            self.MIXb = [[Buf() for _ in range(4)] for _ in range(8)]
            self.ROPEb = [Buf(), Buf()]
            self.PTb = [Buf() for _ in range(6)]
            self.T1b = [Buf(), Buf()]
            self.T2b = [Buf(), Buf()]
            self.RECb = [Buf(), Buf()]
            self.RSb = [Buf(), Buf()]
            self.OCb = [Buf(), Buf()]
            self.SQTb = Buf()
            self.TMb = Buf()
            self.Wb = Buf()
            self.BSTGb = Buf()
            self.LQb = Buf()
            self.SMb = Buf()
            for b in self.PB:
                b.w = {}
                b.r = {}
            S_.op("pool", lambda e: e.memset(self.VAflat, 1.0), writes=self.VAb)
            S_.dma("pool", [(self.TM.rearrange("p (a n) -> p a n", a=2),
                             self.tmask.rearrange("p (a n) -> p a n", a=2))], writes=[self.TMb], key=("tm",))
            for h in range(4):
                S_.dma("sp", [(self.BSTG, self.biasw[li, :, h * NTW * 64:(h + 1) * NTW * 64])],
                       writes=[self.BSTGb], key=("bstg",))
                S_.op("act", lambda e: e.activation(out=self.W[:, h, :], in_=self.BSTG, func=AF.Exp),
                      reads=[self.BSTGb], writes=[self.Wb])
            S_.dma("sp", [(self.LQ.rearrange("p (a n) -> p a n", a=1), self.lamv[li:li + 1, :].partition_broadcast(128))], writes=[self.LQb], key=("lq",))
            S_.op("dve", lambda e: e.tensor_tensor(out=self.TMP32[:, 0:32], in0=self.LQ[:, 0:32], in1=self.LQ[:, 32:64],
                                                   op=ALU.mult), reads=[self.LQb], writes=[self.SMb])
            S_.op("dve", lambda e: e.tensor_tensor(out=self.TMP32[:, 32:64], in0=self.LQ[:, 64:96], in1=self.LQ[:, 96:128],
                                                   op=ALU.mult), reads=[self.LQb], writes=[self.SMb])
            S_.op("dve", lambda e: e.reduce_sum(out=self.SM[:, 0:1], in_=self.TMP32[:, 0:32], axis=AX.X),
                  reads=[self.SMb], writes=[self.SMb])
            S_.op("dve", lambda e: e.reduce_sum(out=self.SM[:, 1:2], in_=self.TMP32[:, 32:64], axis=AX.X),
                  reads=[self.SMb], writes=[self.SMb])
            S_.op("act", lambda e: e.activation(out=self.SM[:, 2:4], in_=self.SM[:, 0:2], func=AF.Exp),
                  reads=[self.SMb], writes=[self.SMb])
            S_.op("dve", lambda e: e.tensor_tensor(out=self.SM[:, 4:5], in0=self.SM[:, 3:4], in1=self.SM[:, 2:3],
                                                   op=ALU.subtract), reads=[self.SMb], writes=[self.SMb])
            S_.op("dve", lambda e: e.tensor_tensor(out=self.SM[:, 5:6], in0=self.SM[:, 4:5], in1=self.cv(li, "laminit"),
                                                   op=ALU.subtract), reads=[self.SMb, self.cvb], writes=[self.SMb])
            S_.op("dve", lambda e: e.tensor_tensor(out=self.SM[:, 6:7], in0=self.cv(li, "subln"), in1=self.cv(li, "omli"),
                                                   op=ALU.mult), reads=[self.SMb, self.cvb], writes=[self.SMb])

        items.append(Item([], start))
        self.add_mixer_b(items, li)
        self.add_mixer_a(items, li)
        self.add_mixer_c(items, li)
        self.add_mixer_d(items, li)

    def evac_copy(self, dst, src, rd, wr):
        eng = "act" if self.rot("evac", 2) == 0 else "dve"
        if eng == "act":
            self.S.op("act", lambda e: e.activation(out=dst, in_=src, func=AF.Copy), reads=rd, writes=wr)
        else:
            self.S.op("dve", lambda e: e.tensor_copy(out=dst, in_=src), reads=rd, writes=wr)

    def proj_rope_chunk(self, wt, wb, col0, table, blk, dsts):
        S_ = self.S
        ts = slice(blk * 512, (blk + 1) * 512)
        b1 = self.rot("pbank", 7)
        b2 = self.rot("pbank", 7)
        for kc in range(8):
            self.mm(self.bank(b1), [self.PB[b1]], wt[:, kc, col0:col0 + 128], self.HN[:, kc, ts],
                    [wb, self.HNb[kc][blk]], kc == 0, kc == 7)
        for kc in range(8):
            self.mm(self.bank(b2), [self.PB[b2]], wt[:, kc, col0 + 128:col0 + 256], self.HN[:, kc, ts],
                    [wb, self.HNb[kc][blk]], kc == 0, kc == 7)
        rope, ropeb = self.get_rope(table, blk)
        i = self.rot("t12", 2)
        S_.op("dve", lambda e: e.tensor_tensor(out=self.T1[i], in0=self.bank(b1), in1=rope[:, 0, :], op=ALU.mult),
              reads=[self.PB[b1], ropeb], writes=[self.T1b[i]])
        S_.op("dve", lambda e: e.tensor_tensor(out=self.T2[i], in0=self.bank(b2), in1=rope[:, 1, :], op=ALU.mult),
              reads=[self.PB[b2], ropeb], writes=[self.T2b[i]])
        for lo, hi, dst, db in dsts:
            S_.op("pool", lambda e: e.tensor_tensor(out=dst, in0=self.T1[i][lo:hi, :], in1=self.T2[i][lo:hi, :], op=ALU.add),
                  reads=[self.T1b[i], self.T2b[i]], writes=[db])

    def proj_v(self, wt, wb, col0, nkc, src, srcb_fn):
        for jt in range(16):
            b = self.rot("pbank", 7)
            for kc in range(nkc):
                self.mm(self.bank(b, 256), [self.PB[b]], src[:, kc, jt * 128:(jt + 1) * 128], wt[:, kc, col0:col0 + 256],
                        [wb] + srcb_fn(kc, jt), kc == 0, kc == nkc - 1)
            pv = self.bank(b, 256).rearrange("p (hp two d) -> p hp two d", two=2, d=64)
            va = self.VA[:, jt, :, :].rearrange("p (hp two) d -> p hp two d", two=2)
            self.S.op("act", lambda e: e.activation(out=va[:, :, 0, 0:64], in_=pv[:, :, 0, :], func=AF.Copy),
                      reads=[self.PB[b]], writes=[self.VAb[jt]])
            self.S.op("dve", lambda e: e.tensor_copy(out=va[:, :, 1, 64:128], in_=pv[:, :, 1, :]),
                      reads=[self.PB[b]], writes=[self.VAb[jt]])

    def attn_dense(self, kind, li, chunk0, scale):
        S_ = self.S
        nmaps = 2 if kind == "C" else 1
        groups = []
        for c in range(2):
            for ib in range(4):
                if kind == "A":
                    tl = [jt for jt in range(16) if -TM_F0 <= ib * 512 - jt * 128 <= 1024]
                    for idx, jt in enumerate(tl):
                        groups.append([dict(c=c, ib=ib, j=j, h=2 * c + j, m=0, jt=jt, first=idx == 0,
                                            last=idx == len(tl) - 1) for j in range(2)])
                elif kind == "C":
                    for j in range(2):
                        for jt in range(16):
                            groups.append([dict(c=c, ib=ib, j=j, h=2 * c + j, m=m, jt=jt, first=jt == 0, last=jt == 15)
                                           for m in range(2)])
                else:
                    for j in range(2):
                        for jt in range(16):
                            groups.append([dict(c=c, ib=ib, j=j, h=2 * c + j, m=0, jt=jt, first=jt == 0, last=jt == 15)])
        if kind == "A":
            SB, UB, LAG = [0, 1, 2, 3], [4, 5, 6], 1
        elif kind == "B":
            SB, UB, LAG = [0, 1, 2, 3], [4, 5, 6], 3
        else:
            SB, UB, LAG = [0, 1, 2, 3], [4, 5, 6], 1
        paired = kind != "B"
        ucur = {}
        deferred = []
        un = [0]

        def qk_group(grp):
            if paired:
                s0 = SB[2 * self.rot("sgrp", len(SB) // 2)]
                sbs = [s0, s0 + 1]
            else:
                sbs = [SB[self.rot("sbank", len(SB))]]
            for st, sb in zip(grp, sbs):
                st["sb"] = sb
                h, m, jt, ib = st["h"], st["m"], st["jt"], st["ib"]
                qs = slice(ib * 512, (ib + 1) * 512)
                ks = slice(jt * 128, (jt + 1) * 128)
                kblk = jt // 4
                if kind in ("A",):
                    tile, lo, hi = h // 2, (h % 2) * 64, (h % 2) * 64 + 64
                elif kind == "B":
                    tile, lo, hi = h, 0, 96
                else:
                    tile, lo, hi = h, 32 * m, 32 * m + 32
                self.mm(self.bank(sb), [self.PB[sb]], self.KT[lo:hi, tile, ks], self.QT[lo:hi, tile, qs],
                        [self.KTb[tile][kblk], self.QTb[tile][ib]], True, True)

        for g in range(min(LAG, len(groups))):
            qk_group(groups[g])
        D = 1 if paired else 0
        NPT = len(self.PT)

        def do_exp(g):
            grp = groups[g]
            for i, st in enumerate(grp):
                sb = st["sb"]
                ib, jt, j = st["ib"], st["jt"], st["j"]
                if paired:
                    if i == 0:
                        p0 = 2 * self.rot("ptgrp", NPT // 2)
                        s0 = sb
                        outv = self.PTpair[p0 // 2]
                        S_.op("act", lambda e: e.activation(out=outv, in_=self.PS[:, s0 * 512:s0 * 512 + 1024],
                                                            func=AF.Exp, scale=scale),
                              reads=[self.PB[s0], self.PB[s0 + 1]], writes=[self.PTb[p0], self.PTb[p0 + 1]])
                        self._grp_p0 = p0
                    p = self._grp_p0 + i
                else:
                    p = self.rot("pt", NPT)
                    S_.op("act", lambda e: e.activation(out=self.PT[p], in_=self.bank(sb), func=AF.Exp, scale=scale),
                          reads=[self.PB[sb]], writes=[self.PTb[p]])
                st["p"] = p
                if kind == "A":
                    off = ib * 512 - jt * 128 + TM_F0
                    S_.op("pool" if (j == 1 and jt % 2 == 1) else "dve",
                          lambda e: e.tensor_tensor(out=self.PT[p], in0=self.PT[p], in1=self.TM[:, off:off + 512],
                                                    op=ALU.mult), reads=[self.PTb[p], self.TMb], writes=[self.PTb[p]])

        sched = []
        for g in range(len(groups) + D):
            if g < len(groups):
                sched.append(("front", g, 0, None))
            if g - D >= 0:
                for i, st in enumerate(groups[g - D]):
                    sched.append(("pv", g - D, i, st))
        for what, g, i, st in sched:
            if what == "front":
                if g + LAG < len(groups):
                    qk_group(groups[g + LAG])
                do_exp(g)
                continue
            h, m, jt, ib, c, j, p = st["h"], st["m"], st["jt"], st["ib"], st["c"], st["j"], st["p"]
            if st["first"]:
                ucur[(j, m)] = UB[un[0] % len(UB)]
                un[0] += 1
            ub = ucur[(j, m)]
            self.mm(self.bank(ub), [self.PB[ub]], self.VA[:, jt, h, :], self.PT[p],
                    [self.VAb[jt], self.PTb[p]], st["first"], st["last"],
                    signal=(st["last"] or g + LAG + D + 2 >= len(groups)))
            for _ in range(NFILL.get(kind, 0) if (not paired or i == 1) else 0):
                self.mm(self.bank(7, FILLN), [self.PB[7]], self.ONES, self.TM[:, 0:FILLN], [self.onesb, self.TMb], True, True,
                        signal=False)
            for d in list(deferred):
                d[0] -= 1
                if d[0] <= 0:
                    d[1]()
                    deferred.remove(d)
            if st["last"] and m == nmaps - 1:
                orow = slice(0, 64) if j == 0 else slice(64, 128)
                drow = slice(64, 128) if j == 0 else slice(0, 64)
                qs = slice(ib * 512, (ib + 1) * 512)
                def copy_out(u):
                    r = self.rot("rec", 2)
                    a = self.rot("t12", 2)
                    S_.op("dve", lambda e: e.tensor_copy(out=self.REC[r][orow, :], in_=self.bank(u)[drow, :]),
                          reads=[self.PB[u]], writes=[self.RECb[r]])
                    S_.op("dve", lambda e: e.tensor_copy(out=self.T1[a][orow, :], in_=self.bank(u)[orow, :]),
                          reads=[self.PB[u]], writes=[self.T1b[a]])
                    return r, a

                if kind != "C":
                    r, a = copy_out(ub)
                    S_.op("dve", lambda e: e.reciprocal(out=self.REC[r][orow, :], in_=self.REC[r][orow, :]),
                          reads=[self.RECb[r]], writes=[self.RECb[r]])
                    S_.op("dve", lambda e: e.tensor_tensor(out=self.MIX[orow, chunk0 + c, qs], in0=self.T1[a][orow, :],
                                                           in1=self.REC[r][orow, :], op=ALU.mult),
                          reads=[self.T1b[a], self.RECb[r]], writes=[self.MIXb[chunk0 + c][ib]])
                else:
                    u1, u2 = ucur[(j, 0)], ucur[(j, 1)]
                    oc = (c * 4 + ib) % 2
                    r1, a1 = copy_out(u1)
                    r2, a2 = copy_out(u2)
                    S_.op("dve", lambda e: e.reciprocal(out=self.REC[r1][orow, :], in_=self.REC[r1][orow, :]),
                          reads=[self.RECb[r1]], writes=[self.RECb[r1]])
                    S_.op("dve", lambda e: e.tensor_tensor(out=self.OC[oc][orow, :], in0=self.T1[a1][orow, :],
                                                           in1=self.REC[r1][orow, :], op=ALU.mult),
                          reads=[self.T1b[a1], self.RECb[r1]], writes=[self.OCb[oc]])
                    S_.op("dve", lambda e: e.reciprocal(out=self.REC[r2][orow, :], in_=self.REC[r2][orow, :]),
                          reads=[self.RECb[r2]], writes=[self.RECb[r2]])
                    S_.op("dve", lambda e: e.tensor_tensor(out=self.T1[a2][orow, :], in0=self.T1[a2][orow, :],
                                                           in1=self.REC[r2][orow, :], op=ALU.mult),
                          reads=[self.T1b[a2], self.RECb[r2]], writes=[self.T1b[a2]])
                    S_.op("dve", lambda e: e.scalar_tensor_tensor(out=self.OC[oc][orow, :], in0=self.T1[a2][orow, :],
                                                                  scalar=self.SM[orow, 5:6], in1=self.OC[oc][orow, :],
                                                                  op0=ALU.mult, op1=ALU.add),
                          reads=[self.T1b[a2], self.SMb, self.OCb[oc]], writes=[self.OCb[oc]])
                    if j == 1:
                        def fin(oc=oc, c=c, ib=ib, qs=qs):
                            S_.op("act", lambda e: e.activation(out=self.SQT[:, 0, :], in_=self.OC[oc], func=AF.Square),
                                  reads=[self.OCb[oc]], writes=[self.SQTb])
                            fb = 7
                            self.mm(self.bank(fb), [self.PB[fb]], self.ONESBD, self.SQT[:, 0, :], [self.SQTb, self.onesb], True, True)
                            rr = self.rot("rs", 2)
                            S_.op("act", lambda e: e.activation(out=self.RS[rr], in_=self.bank(fb), func=AF.Ln, bias=1e-5,
                                                                scale=1.0 / 64), reads=[self.PB[fb]], writes=[self.RSb[rr]])
                            S_.op("act", lambda e: e.activation(out=self.RS[rr], in_=self.RS[rr], func=AF.Exp, scale=-0.5),
                                  reads=[self.RSb[rr]], writes=[self.RSb[rr]])
                            S_.op("dve", lambda e: e.scalar_tensor_tensor(out=self.MIX[:, chunk0 + c, qs], in0=self.OC[oc],
                                                                          scalar=self.SM[:, 6:7], in1=self.RS[rr],
                                                                          op0=ALU.mult, op1=ALU.mult),
                                  reads=[self.OCb[oc], self.SMb, self.RSb[rr]], writes=[self.MIXb[chunk0 + c][ib]])
                        deferred.append([6, fin])
        for d in deferred:
            d[1]()

    def add_mixer_b(self, items, li):
        S_ = self.S
        CQN = self.MIX[:, 4:7, :]
        CKVN = self.MIX[:, 0:2, :]

        def b1(tiles):
            (t1, t1b), (t2, t2b) = tiles
            for blk in range(4):
                ts = slice(blk * 512, (blk + 1) * 512)
                for c in range(3):
                    for kc in range(8):
                        self.mm(self.bank(c), [self.PB[c]], t1[:, kc, c * 128:(c + 1) * 128], self.HN[:, kc, ts],
                                [t1b, self.HNb[kc][blk]], kc == 0, kc == 7)
                for c in range(2):
                    for kc in range(8):
                        self.mm(self.bank(3 + c), [self.PB[3 + c]], t2[:, kc, c * 128:(c + 1) * 128], self.HN[:, kc, ts],
                                [t2b, self.HNb[kc][blk]], kc == 0, kc == 7)
                for kc in range(8):
                    self.mm(self.bank(5)[0:64, :], [self.PB[5]], t2[:, kc, 256:320], self.HN[:, kc, ts],
                            [t2b, self.HNb[kc][blk]], kc == 0, kc == 7)
                S_.op("act", lambda e: e.activation(out=self.SQT[:, 0:3, :], in_=self.PS[:, 0:1536].rearrange("p (c t) -> p c t", c=3),
                                                    func=AF.Square), reads=self.PB[0:3], writes=[self.SQTb])
                S_.op("act", lambda e: e.activation(out=self.SQT[:, 3:5, :], in_=self.PS[:, 1536:2560].rearrange("p (c t) -> p c t", c=2),
                                                    func=AF.Square), reads=self.PB[3:5], writes=[self.SQTb])
                for c in range(3):
                    self.mm(self.bank(6), [self.PB[6]], self.ONES, self.SQT[:, c, :], [self.SQTb, self.onesb], c == 0, c == 2)
                for c in range(2):
                    self.mm(self.bank(7), [self.PB[7]], self.ONES, self.SQT[:, 3 + c, :], [self.SQTb, self.onesb], c == 0, c == 1)
                S_.op("act", lambda e: e.activation(out=self.RS[0], in_=self.bank(6), func=AF.Ln, bias=1e-6, scale=1.0 / 384),
                      reads=[self.PB[6]], writes=[self.RSb[0]])
                S_.op("act", lambda e: e.activation(out=self.RS[1], in_=self.bank(7), func=AF.Ln, bias=1e-6, scale=1.0 / 256),
                      reads=[self.PB[7]], writes=[self.RSb[1]])
                S_.op("act", lambda e: e.activation(out=self.RS[0], in_=self.RS[0], func=AF.Exp, scale=-0.5),
                      reads=[self.RSb[0]], writes=[self.RSb[0]])
                S_.op("act", lambda e: e.activation(out=self.RS[1], in_=self.RS[1], func=AF.Exp, scale=-0.5),
                      reads=[self.RSb[1]], writes=[self.RSb[1]])
                for c in range(3):
                    S_.op("dve", lambda e: e.scalar_tensor_tensor(out=CQN[:, c, ts], in0=self.bank(c), scalar=self.cv(li, "q_norm", c),
                                                                  in1=self.RS[0], op0=ALU.mult, op1=ALU.mult),
                          reads=[self.PB[c], self.RSb[0], self.cvb], writes=[self.MIXb[4 + c][blk]])
                for c in range(2):
                    S_.op("dve", lambda e: e.scalar_tensor_tensor(out=CKVN[:, c, ts], in0=self.bank(3 + c),
                                                                  scalar=self.cv(li, "kv_norm", c), in1=self.RS[1],
                                                                  op0=ALU.mult, op1=ALU.mult),
                          reads=[self.PB[3 + c], self.RSb[1], self.cvb], writes=[self.MIXb[c][blk]])
                rope, ropeb = self.get_rope("C", blk)
                i = self.rot("t12", 2)
                S_.op("dve", lambda e: e.tensor_tensor(out=self.T1[i][0:32, :], in0=self.bank(5)[0:32, :], in1=rope[0:32, 0, :],
                                                       op=ALU.mult), reads=[self.PB[5], ropeb], writes=[self.T1b[i]])
                S_.op("dve", lambda e: e.tensor_tensor(out=self.T2[i][0:32, :], in0=self.bank(5)[32:64, :], in1=rope[32:64, 1, :],
                                                       op=ALU.mult), reads=[self.PB[5], ropeb], writes=[self.T2b[i]])
                for h in range(4):
                    S_.op("pool", lambda e: e.tensor_tensor(out=self.KT[64:96, h, ts], in0=self.T1[i][0:32, :],
                                                            in1=self.T2[i][0:32, :], op=ALU.add),
                          reads=[self.T1b[i], self.T2b[i]], writes=[self.KTb[h][blk]])

        items.append(Item([(li, "b_cq"), (li, "b_ckv")], b1))

        def b2(tiles):
            (t, tb), = tiles
            for c in range(2):
                for blk in range(4):
                    ts = slice(blk * 512, (blk + 1) * 512)
                    b = self.rot("pbank", 7)
                    for kc in range(2):
                        self.mm(self.bank(b), [self.PB[b]], t[:, kc, c * 128:(c + 1) * 128], CKVN[:, kc, ts],
                                [tb, self.MIXb[kc][blk]], kc == 0, kc == 1)
                    S_.op("act", lambda e: e.activation(out=self.KT[0:64, 2 * c, ts], in_=self.bank(b)[0:64, :], func=AF.Copy),
                          reads=[self.PB[b]], writes=[self.KTb[2 * c][blk]])
                    S_.op("dve", lambda e: e.tensor_copy(out=self.KT[0:64, 2 * c + 1, ts], in_=self.bank(b)[64:128, :]),
                          reads=[self.PB[b]], writes=[self.KTb[2 * c + 1][blk]])
            self.proj_v(t, tb, 256, 2, CKVN, lambda kc, jt: [self.MIXb[kc][jt // 4]])

        items.append(Item([(li, "ukv")], b2))

        def mk_b3(h):
            def b3(tiles):
                (t, tb), = tiles
                for blk in range(4):
                    ts = slice(blk * 512, (blk + 1) * 512)
                    b1_ = self.rot("pbank", 7)
                    b2_ = self.rot("pbank", 7)
                    for kc in range(3):
                        self.mm(self.bank(b1_)[0:96, :], [self.PB[b1_]], t[:, kc, 0:96], CQN[:, kc, ts],
                                [tb, self.MIXb[4 + kc][blk]], kc == 0, kc == 2)
                    for kc in range(3):
                        self.mm(self.bank(b2_)[0:32, :], [self.PB[b2_]], t[:, kc, 96:128], CQN[:, kc, ts],
                                [tb, self.MIXb[4 + kc][blk]], kc == 0, kc == 2)
                    S_.op("act", lambda e: e.activation(out=self.QT[0:64, h, ts], in_=self.bank(b1_)[0:64, :], func=AF.Copy),
                          reads=[self.PB[b1_]], writes=[self.QTb[h][blk]])
                    rope, ropeb = self.get_rope("C", blk)
                    i = self.rot("t12", 2)
                    S_.op("dve", lambda e: e.tensor_tensor(out=self.T1[i][64:96, :], in0=self.bank(b1_)[64:96, :],
                                                           in1=rope[64:96, 0, :], op=ALU.mult),
                          reads=[self.PB[b1_], ropeb], writes=[self.T1b[i]])
                    S_.op("dve", lambda e: e.tensor_tensor(out=self.T2[i][64:96, :], in0=self.bank(b2_)[0:32, :],
                                                           in1=rope[0:32, 1, :], op=ALU.mult),
                          reads=[self.PB[b2_], ropeb], writes=[self.T2b[i]])
                    S_.op("pool", lambda e: e.tensor_tensor(out=self.QT[64:96, h, ts], in0=self.T1[i][64:96, :],
                                                            in1=self.T2[i][64:96, :], op=ALU.add),
                          reads=[self.T1b[i], self.T2b[i]], writes=[self.QTb[h][blk]])
            return b3

        for h in range(4):
            items.append(Item([(li, f"uq{h}")], mk_b3(h)))
        items.append(Item([], lambda tiles: self.attn_dense("B", li, 2, 96 ** -0.5)))

    def add_mixer_a(self, items, li):
        def mk(which, c):
            def f(tiles):
                (t, tb), = tiles
                dst = self.QT if which == "q" else self.KT
                dstb = self.QTb if which == "q" else self.KTb
                for blk in range(4):
                    ts = slice(blk * 512, (blk + 1) * 512)
                    self.proj_rope_chunk(t, tb, 0, "A", blk, [(0, 128, dst[:, c, ts], dstb[c][blk])])
            return f

        for c in range(2):
            items.append(Item([(li, f"a_q{c}")], mk("q", c)))
        for c in range(2):
            items.append(Item([(li, f"a_k{c}")], mk("k", c)))

        def av(tiles):
            (t, tb), = tiles
            self.proj_v(t, tb, 0, 8, self.HN, lambda kc, jt: [self.HNb[kc][jt // 4]])

        items.append(Item([(li, "a_v")], av))
        items.append(Item([], lambda tiles: self.attn_dense("A", li, 0, 0.125)))

    def add_mixer_c(self, items, li):
        def mk(which, c):
            def f(tiles):
                (t, tb), = tiles
                dst = self.QT if which == "q" else self.KT
                dstb = self.QTb if which == "q" else self.KTb
                for blk in range(4):
                    ts = slice(blk * 512, (blk + 1) * 512)
                    self.proj_rope_chunk(t, tb, 0, "C", blk,
                                         [(0, 64, dst[0:64, 2 * c, ts], dstb[2 * c][blk]),
                                          (64, 128, dst[0:64, 2 * c + 1, ts], dstb[2 * c + 1][blk])])
            return f

        for c in range(2):
            items.append(Item([(li, f"c_q{c}")], mk("q", c)))
        for c in range(2):
            items.append(Item([(li, f"c_k{c}")], mk("k", c)))

        def cvf(tiles):
            (t, tb), = tiles
            self.proj_v(t, tb, 0, 8, self.HN, lambda kc, jt: [self.HNb[kc][jt // 4]])

        items.append(Item([(li, "c_v")], cvf))
        items.append(Item([], lambda tiles: self.attn_dense("C", li, 4, 32 ** -0.5)))

    def add_mixer_d(self, items, li):
        S_ = self.S

        def mk(which):
            def f(tiles):
                (t, tb), = tiles
                dst = self.QT if which == "q" else self.KT
                dstb = self.QTb if which == "q" else self.KTb
                for c in range(2):
                    for blk in range(4):
                        ts = slice(blk * 512, (blk + 1) * 512)
                        b = self.rot("pbank", 7)
                        for kc in range(8):
                            self.mm(self.bank(b), [self.PB[b]], t[:, kc, c * 128:(c + 1) * 128], self.HN[:, kc, ts],
                                    [tb, self.HNb[kc][blk]], kc == 0, kc == 7)
                        self.evac_copy(dst[:, c, ts], self.bank(b), [self.PB[b]], [dstb[c][blk]])
            return f

        items.append(Item([(li, "d_q")], mk("q")))
        items.append(Item([(li, "d_k")], mk("k")))

        def dv(tiles):
            (t, tb), = tiles
            self.proj_v(t, tb, 0, 8, self.HN, lambda kc, jt: [self.HNb[kc][jt // 4]])

        items.append(Item([(li, "d_v")], dv))

        def na(tiles):
            steps = [(rq, hp) for hp in range(2) for rq in range(32)]
            SR = [(0, 1), (2, 3), (4, 5)]
            OB = [6, 7]
            LA = 2
            plans = {}

            def qk(k):
                rq, hp = steps[k]
                kt0, nt, t0 = _na_row_plan(rq)
                plans[k] = (kt0, nt, t0)
                sr = SR[k % 3]
                last = None
                n = 2 * nt
                cnt = 0
                for j in range(2):
                    for m in range(nt):
                        cnt += 1
                        col = j * 512 + m * 64
                        o = self.PS[:, sr[0] * 512 + col: sr[0] * 512 + col + 64]
                        kt = kt0 + m
                        self.mm(o, [self.PB[sr[j]]],
                                self.KT[j * 64:(j + 1) * 64, hp, kt * 128:(kt + 1) * 128],
                                self.QT[j * 64:(j + 1) * 64, hp, rq * 64:(rq + 1) * 64],
                                [self.KTb[hp][kt // 4], self.QTb[hp][rq // 8]], True, True, signal=(cnt == n))

            for k in range(min(LA, len(steps))):
                qk(k)
            for k, (rq, hp) in enumerate(steps):
                if k + LA < len(steps):
                    qk(k + LA)
                kt0, nt, t0 = plans.pop(k)
                sr = SR[k % 3]
                pi = k % 3
                ptv = self.PTN[pi]
                sv = self.PS[:, sr[0] * 512: sr[0] * 512 + 1024].rearrange("p (j n) -> p j n", j=2)[:, :, 0:nt * 64]
                S_.op("act", lambda e: e.activation(out=ptv[:, :, 0:nt * 64], in_=sv, func=AF.Exp, scale=0.125),
                      reads=[self.PB[sr[0]], self.PB[sr[1]]], writes=[self.PTNb[pi]])
                S_.op("pool", lambda e: e.tensor_tensor(out=ptv[:, :, 0:nt * 64], in0=ptv[:, :, 0:nt * 64],
                                                        in1=self.W[:, 2 * hp:2 * hp + 2, t0 * 64:(t0 + nt) * 64], op=ALU.mult),
                      reads=[self.PTNb[pi], self.Wb], writes=[self.PTNb[pi]])
                ob = OB[(k // 4) % 2]
                for j in range(2):
                    for m in range(nt):
                        oc = ((rq % 4) * 2 + j) * 64
                        self.mm(self.bank(ob)[:, oc:oc + 64], [self.PB[ob]], self.VA[:, kt0 + m, 2 * hp + j, :],
                                ptv[:, j, m * 64:(m + 1) * 64], [self.VAb[kt0 + m], self.PTNb[pi]], m == 0, m == nt - 1,
                                signal=(j == 1 and m == nt - 1))
                if k % 4 == 3:
                    r = self.rot("rec", 2)
                    rq0 = rq - 3
                    qs = slice(rq0 * 64, rq0 * 64 + 256)
                    blk = rq0 // 8
                    ov = self.bank(ob).rearrange("p (r j d) -> p r j d", r=4, j=2)
                    rv = self.REC[r].rearrange("p (r j d) -> p r j d", r=4, j=2)
                    S_.op("dve", lambda e: e.reciprocal(out=rv[64:128, :, 0, :], in_=ov[64:128, :, 0, :]),
                          reads=[self.PB[ob]], writes=[self.RECb[r]])
                    S_.op("dve", lambda e: e.reciprocal(out=rv[0:64, :, 1, :], in_=ov[0:64, :, 1, :]),
                          reads=[self.PB[ob]], writes=[self.RECb[r]])
                    S_.op("dve", lambda e: e.tensor_tensor(out=self.MIX[0:64, 6 + hp, qs].rearrange("p (r d) -> p r d", r=4),
                                                           in0=ov[0:64, :, 0, :], in1=rv[64:128, :, 0, :], op=ALU.mult),
                          reads=[self.PB[ob], self.RECb[r]], writes=[self.MIXb[6 + hp][blk]])
                    S_.op("dve", lambda e: e.tensor_tensor(out=self.MIX[64:128, 6 + hp, qs].rearrange("p (r d) -> p r d", r=4),
                                                           in0=ov[64:128, :, 1, :], in1=rv[0:64, :, 1, :], op=ALU.mult),
                          reads=[self.PB[ob], self.RECb[r]], writes=[self.MIXb[6 + hp][blk]])

        def na_wrap(tiles):
            self.PTNb = [Buf(), Buf(), Buf()]
            for i in range(3):
                for bb in self.PTb:
                    _merge(self.PTNb[i].r, bb.w)
                    _merge(self.PTNb[i].r, bb.r)
            na(tiles)

        items.append(Item([], na_wrap))

    def add_phase_b(self, items, li):
        S_ = self.S
        nl = len(self.layer_ids)
        lastl = li == nl - 1

        def start(tiles):
            S_.barrier()
            mixb = self.MIXb
            self.new_h_bufs()
            self.MIXb = mixb
            self._rsb = None
            for b in self.PB:
                b.w = {}
                b.r = {}
            self.load_h(self.hin if li == 0 else self.hs)

        items.append(Item([], start))

        def wo_all(tiles):
            rsw = [self.f32v(130 + 2 * i, 512) for i in range(2)]
            for blk in range(4):
                ts = slice(blk * 512, (blk + 1) * 512)
                for n in range(8):
                    t, tb = tiles[n // 3]
                    c0 = (n % 3) * 128
                    b = self.rot("pbank", 7)
                    for kc in range(8):
                        self.mm(self.bank(b), [self.PB[b]], t[:, kc, c0:c0 + 128], self.MIX[:, kc, ts], [tb, self.MIXb[kc][blk]],
                                kc == 0, kc == 7)
                    S_.op("dve", lambda e: e.tensor_tensor(out=self.H[:, n, ts], in0=self.bank(b), in1=self.H[:, n, ts], op=ALU.add),
                          reads=[self.PB[b], self.Hb[n][blk]], writes=[self.Hb[n][blk]])
                self.norm_s1(blk)
                if blk >= 1:
                    self.norm_s2(li, "ffn_norm", blk - 1, rsv=rsw)
            self.norm_s2(li, "ffn_norm", 3, rsv=rsw)

        items.append(Item([(li, "wo_a"), (li, "wo_b"), (li, "wo_c")], wo_all))

        def ffn_start(tiles):
            S_.barrier()
            for b in self.PB:
                b.w = {}
                b.r = {}
            self.ACTBb = [Buf() for _ in range(22)]
            self.CAb = [Buf() for _ in range(3)]
            self.CGb = [Buf() for _ in range(2)]
            self._rsb = None

        items.append(Item([], ffn_start))

        SLOTS = [(0, 1), (2, 3), (4, 5)]

        def mk_up(hf, c):
            def f(tiles):
                (t, tb), = tiles
                t0 = hf * 1024
                res = {}
                for part in range(2):
                    sl = SLOTS[self.rot("fslot", 3)]
                    for tb_ in range(2):
                        ts = slice(t0 + tb_ * 512, t0 + (tb_ + 1) * 512)
                        blk = (t0 + tb_ * 512) // 512
                        for kc in range(8):
                            self.mm(self.bank(sl[tb_]), [self.PB[sl[tb_]]], t[:, kc, part * 128:(part + 1) * 128],
                                    self.HN[:, kc, ts], [tb, self.HNb[kc][blk]], kc == 0, kc == 7)
                    hi = self.rot("halo", 16)
                    hbk = 6 + hi % 2
                    htok = 1024 if hf == 0 else 1023
                    hcol = self.bank(hbk)[:, hi * 2:hi * 2 + 1]
                    for kc in range(8):
                        self.mm(hcol, [self.PB[hbk]], t[:, kc, part * 128:(part + 1) * 128], self.HN[:, kc, htok:htok + 1],
                                [tb, self.HNb[kc][htok // 512]], kc == 0, kc == 7)
                    u = self.PS[:, sl[0] * 512: sl[0] * 512 + 1024]
                    ub = [self.PB[sl[0]], self.PB[sl[1]]]
                    ci = part * 22 + c
                    ai = self.rot("ca", 3)
                    A = self.CA[ai]
                    Ab = self.CAb[ai]
                    S_.op("act", lambda e: e.activation(out=A, in_=u, func=AF.Identity, bias=self.cv(li, "cb", ci),
                                                        scale=self.cv(li, "cw1", ci)), reads=ub + [self.cvb], writes=[Ab])
                    S_.op("dve", lambda e: e.scalar_tensor_tensor(out=A[:, 1:1024], in0=u[:, 0:1023], scalar=self.cv(li, "cw0", ci),
                                                                  in1=A[:, 1:1024], op0=ALU.mult, op1=ALU.add),
                          reads=ub + [self.cvb, Ab], writes=[Ab])
                    S_.op("dve", lambda e: e.scalar_tensor_tensor(out=A[:, 0:1023], in0=u[:, 1:1024], scalar=self.cv(li, "cw2", ci),
                                                                  in1=A[:, 0:1023], op0=ALU.mult, op1=ALU.add),
                          reads=ub + [self.cvb, Ab], writes=[Ab])
                    if hf == 0:
                        S_.op("dve", lambda e: e.scalar_tensor_tensor(out=A[:, 1023:1024], in0=hcol, scalar=self.cv(li, "cw2", ci),
                                                                      in1=A[:, 1023:1024], op0=ALU.mult, op1=ALU.add),
                              reads=[self.PB[hbk], self.cvb, Ab], writes=[Ab])
                    else:
                        S_.op("dve", lambda e: e.scalar_tensor_tensor(out=A[:, 0:1], in0=hcol, scalar=self.cv(li, "cw0", ci),
                                                                      in1=A[:, 0:1], op0=ALU.mult, op1=ALU.add),
                              reads=[self.PB[hbk], self.cvb, Ab], writes=[Ab])
                    res[part] = (A, Ab)
                    if part == 0:
                        gi = self.rot("cg", 2)
                        S_.op("act", lambda e: e.activation(out=self.CG[gi], in_=A, func=AF.Gelu_apprx_tanh),
                              reads=[Ab], writes=[self.CGb[gi]])
                        res["g"] = (self.CG[gi], self.CGb[gi])
                G, Gb = res["g"]
                Av, Avb = res[1]
                S_.op("pool", lambda e: e.tensor_tensor(out=self.ACTB[:, c, :], in0=G, in1=Av, op=ALU.mult),
                      reads=[Gb, Avb], writes=[self.ACTBb[c]])
            return f

        def mk_dn(hf, n):
            def f(tiles):
                (t, tb), = tiles
                for tb_ in range(2):
                    blk = hf * 2 + tb_
                    ts = slice(blk * 512, (blk + 1) * 512)
                    b = self.rot("dbank", 6)
                    for kc in range(22):
                        self.mm(self.bank(b), [self.PB[b]], t[:, kc, :], self.ACTB[:, kc, tb_ * 512:(tb_ + 1) * 512],
                                [tb, self.ACTBb[kc]], kc == 0, kc == 21)
                    S_.op("dve", lambda e: e.tensor_tensor(out=self.H[:, n, ts], in0=self.bank(b), in1=self.H[:, n, ts], op=ALU.add),
                          reads=[self.PB[b], self.Hb[n][blk]], writes=[self.Hb[n][blk]])
            return f

        for hf in range(2):
            for c in range(22):
                items.append(Item([(li, f"up{c}")], mk_up(hf, c)))
            for n in range(8):
                items.append(Item([(li, f"dn{n}")], mk_dn(hf, n)))

        def ple_start(tiles):
            S_.barrier()
            for b in self.PB:
                b.w = {}
                b.r = {}
            self.PBtb = Buf()
            self.PPWb = Buf()
            self.ETb = [Buf(), Buf()]
            self.CAb = [Buf() for _ in range(3)]
            S_.dma("pool", [(self.PBt[:, k, :], self.pT[li, k * 128:(k + 1) * 128, :]) for k in range(2)],
                   writes=[self.PBtb], key=("pbt",))
            o, kc, m = WT_OFF["pp"]
            S_.dma("pool", [(self.PPW, self.wpk[li, :, o:o + 2048].rearrange("p (k n) -> p k n", k=2))],
                   writes=[self.PPWb], key=("ppw",))
            self.rmsnorm_main(li, "ple_norm")

        items.append(Item([], ple_start))

        def ple_all(tiles):
            hs_v = (self.hout if lastl else self.hs).rearrange("(k p) t -> p k t", p=128)

            def post(blk):
                ts = slice(blk * 512, (blk + 1) * 512)
                if lastl and not self.do_final:
                    pass
                elif lastl:
                    self.norm_s2(li, None, blk, final=True, gcol=NLAYER * NCL)
                else:
                    self.norm_s2(li + 1, "attn_norm", blk)
                if lastl:
                    S_.dma("sp", [(hs_v[:, :, ts], self.H[:, :, ts])], reads=[self.Hb[k][blk] for k in range(8)], key=("hst",))

            for blk in range(4):
                ts = slice(blk * 512, (blk + 1) * 512)
                for n in range(8):
                    t, tb = tiles[n // 3]
                    c0 = (n % 3) * 128
                    bg = self.rot("pbank", 7)
                    be = self.rot("pbank", 7)
                    for kc in range(8):
                        self.mm(self.bank(bg), [self.PB[bg]], t[:, kc, c0:c0 + 128], self.HN[:, kc, ts], [tb, self.HNb[kc][blk]],
                                kc == 0, kc == 7)
                    for kc in range(2):
                        self.mm(self.bank(be), [self.PB[be]], self.PPW[:, kc, n * 128:(n + 1) * 128], self.PBt[:, kc, ts],
                                [self.PPWb, self.PBtb], kc == 0, kc == 1)
                    ei = self.rot("et", 2)
                    S_.op("act", lambda e: e.activation(out=self.ET[ei], in_=self.bank(bg), func=AF.Sigmoid),
                          reads=[self.PB[bg]], writes=[self.ETb[ei]])
                    S_.op("dve", lambda e: e.tensor_tensor(out=self.ET[ei], in0=self.bank(be), in1=self.ET[ei], op=ALU.mult),
                          reads=[self.PB[be], self.ETb[ei]], writes=[self.ETb[ei]])
                    S_.op("pool", lambda e: e.tensor_tensor(out=self.H[:, n, ts], in0=self.H[:, n, ts], in1=self.ET[ei], op=ALU.add),
                          reads=[self.ETb[ei], self.Hb[n][blk]], writes=[self.Hb[n][blk]])
                if not lastl:
                    S_.dma("sp", [(hs_v[:, :, ts], self.H[:, :, ts])], reads=[self.Hb[k][blk] for k in range(8)], key=("hst",))
                if not (lastl and not self.do_final):
                    self.norm_s1(blk)
                if blk >= 1:
                    post(blk - 1)
            post(3)

        items.append(Item([(li, "pg_a"), (li, "pg_b"), (li, "pg_c")], ple_all))

        def finish(tiles):
            S_.barrier()
            S_.epoch += 1

        items.append(Item([], finish))


_CACHE = {}


def _get_prog(layer_ids, src_is_x, do_final):
    key = (tuple(range(len(layer_ids))), do_final)
    if key not in _CACHE:
        _CACHE[key] = Builder(list(range(len(layer_ids))), src_is_x, do_final).nc
    return _CACHE[key]


def kernel(**inputs):
    inp = {k: np.asarray(v) for k, v in inputs.items()}
    x = inp["x"].astype(np.float32, copy=False)
    p = inp["p"].astype(np.float32, copy=False)
    B = x.shape[0]
    ncores = 8
    cv = _cvec(inp)
    ropes = _rope_tables()
    tm = _tmask()
    wpk_all = np.stack([_pack_layer(inp, l) for l in range(NLAYER)])
    bw_all = np.stack([_bias_w(np.asarray(inp["na_rpb"][l], np.float32)) for l in range(NLAYER)])
    lam_all = np.stack([np.concatenate([inp["lam_q1"][l], inp["lam_k1"][l], inp["lam_q2"][l], inp["lam_k2"][l]])
                        for l in range(NLAYER)]).astype(np.float32)

    def run(hT_list, layers, do_final):
        nc = _get_prog(layers, False, do_final)
        cvl = np.zeros_like(cv)
        for i, l in enumerate(layers):
            cvl[:, i * NCL:(i + 1) * NCL] = cv[:, l * NCL:(l + 1) * NCL]
        cvl[:, NLAYER * NCL:] = cv[:, NLAYER * NCL:]
        wl = np.ascontiguousarray(wpk_all[layers])
        bl = np.ascontiguousarray(bw_all[layers])
        ll = np.ascontiguousarray(lam_all[layers])
        in_maps = []
        for b in range(ncores):
            in_maps.append({
                "hin": hT_list[b],
                "pT": np.ascontiguousarray(p[layers, b].transpose(0, 2, 1)),
                "wpk": wl, "cvec": cvl, "lamv": ll, "biasw": bl, "tmask": tm,
                "ropeA": ropes["ropeA"], "ropeC": ropes["ropeC"],
            })
        res = run_bass_kernel_spmd(nc, in_maps, core_ids=list(range(ncores)))
        return [np.asarray(r["hout"]) for r in res.results]

    hT = [np.ascontiguousarray(x[b].T) for b in range(B)]
    if FUSED:
        hT = run(hT, list(range(NLAYER)), True)
    else:
        for l in range(NLAYER):
            hT = run(hT, [l], l == NLAYER - 1)
    out = np.stack([h.T for h in hT]).astype(np.float32)
    return out
```

```python
import math
from contextlib import ExitStack
import numpy as np
import concourse.bass as bass
import concourse.mybir as mybir
from concourse.bass_utils import run_bass_kernel_spmd

F32 = mybir.dt.float32
BF16 = mybir.dt.bfloat16
AF = mybir.ActivationFunctionType
ALU = mybir.AluOpType
AX = mybir.AxisListType

FUSED = True
NLAYER = 4
S = 2048
D = 1024
DFF = 2816
NEGB = -30000.0
import os as _os
NFILL = {"A": int(_os.environ.get("NF_A", "1")), "B": int(_os.environ.get("NF_B", "1")), "C": int(_os.environ.get("NF_C", "2"))}
FILLN = int(_os.environ.get("FILLN", "256"))

A_OFF, B_OFF, C_OFF, D_OFF = 0, 768, 1440, 2208


def _sw(n, d):
    idx = np.arange(n)
    return (idx // d) * d + (idx % d + d // 2) % d


WT_SPECS = []


def _build_specs():
    sp = []
    sp.append(("b_cq", 8, 384))
    sp.append(("b_ckv", 8, 320))
    sp.append(("ukv", 2, 512))
    for h in range(4):
        sp.append((f"uq{h}", 3, 128))
    for c in range(2):
        sp.append((f"a_q{c}", 8, 256))
    for c in range(2):
        sp.append((f"a_k{c}", 8, 256))
    sp.append(("a_v", 8, 256))
    for c in range(2):
        sp.append((f"c_q{c}", 8, 256))
    for c in range(2):
        sp.append((f"c_k{c}", 8, 256))
    sp.append(("c_v", 8, 256))
    sp.append(("d_q", 8, 256))
    sp.append(("d_k", 8, 256))
    sp.append(("d_v", 8, 256))
    for g, m in (("a", 384), ("b", 384), ("c", 256)):
        sp.append((f"wo_{g}", 8, m))
    for c in range(22):
        sp.append((f"up{c}", 8, 256))
    for n in range(8):
        sp.append((f"dn{n}", 22, 128))
    for g, m in (("a", 384), ("b", 384), ("c", 256)):
        sp.append((f"pg_{g}", 8, m))
    sp.append(("pp", 2, 1024))
    return sp


WT_SPECS = _build_specs()
WT_OFF = {}
_o = 0
for _n, _kc, _m in WT_SPECS:
    WT_OFF[_n] = (_o, _kc, _m)
    _o += _kc * _m
WTOT = _o
SLOT = 3072
NSLOT = 4

CV = {}
_c = 0
for _n, _w in (("attn_norm", 8), ("ffn_norm", 8), ("ple_norm", 8), ("q_norm", 3), ("kv_norm", 2), ("subln", 1),
               ("cw0", 44), ("cw1", 44), ("cw2", 44), ("cb", 44), ("laminit", 1), ("omli", 1)):
    CV[_n] = _c
    _c += _w
NCL = _c
NCV = NLAYER * NCL + 8

NTW = 19
TM_F0 = 1408
TM_W = 2944


def _T(mat):
    K, M = mat.shape
    kc = K // 128
    return mat.reshape(kc, 128, M).transpose(1, 0, 2).reshape(128, kc * M)


def _layer_tiles(inp, l):
    w_in = inp["w_in"][l]
    w_uq = inp["w_uq"][l]
    w_ukv = inp["w_ukv"][l]
    t = {}
    ar = np.arange
    t["b_cq"] = w_in[:, B_OFF:B_OFF + 384]
    t["b_ckv"] = w_in[:, np.concatenate([B_OFF + 384 + ar(256), B_OFF + 640 + ar(32), B_OFF + 640 + _sw(32, 32)])]
    kn = np.concatenate([h * 128 + ar(64) for h in range(4)])
    vv = np.concatenate([h * 128 + 64 + ar(64) for h in range(4)])
    t["ukv"] = w_ukv[:, np.concatenate([kn, vv])]
    for h in range(4):
        t[f"uq{h}"] = w_uq[:, np.concatenate([h * 96 + ar(96), h * 96 + 64 + _sw(32, 32)])]
    for c in range(2):
        t[f"a_q{c}"] = w_in[:, np.concatenate([A_OFF + c * 128 + ar(128), A_OFF + c * 128 + _sw(128, 64)])]
        t[f"a_k{c}"] = w_in[:, np.concatenate([A_OFF + 256 + c * 128 + ar(128), A_OFF + 256 + c * 128 + _sw(128, 64)])]
        t[f"c_q{c}"] = w_in[:, np.concatenate([C_OFF + c * 128 + ar(128), C_OFF + c * 128 + _sw(128, 32)])]
        t[f"c_k{c}"] = w_in[:, np.concatenate([C_OFF + 256 + c * 128 + ar(128), C_OFF + 256 + c * 128 + _sw(128, 32)])]
    t["a_v"] = w_in[:, A_OFF + 512:A_OFF + 768]
    t["c_v"] = w_in[:, C_OFF + 512:C_OFF + 768]
    t["d_q"] = w_in[:, D_OFF:D_OFF + 256]
    t["d_k"] = w_in[:, D_OFF + 256:D_OFF + 512]
    t["d_v"] = w_in[:, D_OFF + 512:D_OFF + 768]
    for n in range(8):
        t[f"dn{n}"] = inp["w_down"][l][:, n * 128:(n + 1) * 128]
    for g, (c0, c1) in (("a", (0, 384)), ("b", (384, 768)), ("c", (768, 1024))):
        t[f"wo_{g}"] = inp["w_o"][l][:, c0:c1]
        t[f"pg_{g}"] = inp["w_ple_gate"][l][:, c0:c1]
    w_up = inp["w_up"][l]
    for c in range(22):
        t[f"up{c}"] = w_up[:, np.concatenate([c * 128 + ar(128), DFF + c * 128 + ar(128)])]
    t["pp"] = inp["w_ple_proj"][l]
    return t


def _pack_layer(inp, l):
    t = _layer_tiles(inp, l)
    out = np.empty((128, WTOT), np.float32)
    for n, kc, m in WT_SPECS:
        o = WT_OFF[n][0]
        out[:, o:o + kc * m] = _T(np.asarray(t[n], np.float32))
    return out


def _cvec(inp):
    cv = np.zeros((128, NCV), np.float32)

    def put(col, v):
        v = np.asarray(v, np.float32)
        n = v.shape[0] // 128
        cv[:, col:col + n] = v.reshape(n, 128).T

    for l in range(NLAYER):
        b = l * NCL
        put(b + CV["attn_norm"], inp["attn_norm"][l])
        put(b + CV["ffn_norm"], inp["ffn_norm"][l])
        put(b + CV["ple_norm"], inp["ple_norm"][l])
        put(b + CV["q_norm"], inp["mla_q_norm"][l])
        put(b + CV["kv_norm"], inp["mla_kv_norm"][l])
        put(b + CV["subln"], np.tile(np.asarray(inp["diff_subln"][l]), 2))
        for j in range(3):
            put(b + CV[f"cw{j}"], inp["conv_w"][l][j])
        put(b + CV["cb"], inp["conv_b"][l])
        lam_init = 0.8 - 0.6 * math.exp(-0.3 * l)
        cv[:, b + CV["laminit"]] = lam_init
        cv[:, b + CV["omli"]] = 1.0 - lam_init
    put(NLAYER * NCL, inp["final_norm"])
    return cv


def _rope_tables():
    pos = np.arange(S, dtype=np.float32)
    out = {}
    for name, d in (("ropeA", 64), ("ropeC", 32)):
        inv = np.power(np.float32(10000.0), -np.arange(0, d, 2, dtype=np.float32) / np.float32(d)).astype(np.float32)
        ang = (pos[:, None] * inv[None, :]).astype(np.float32)
        cos = np.cos(ang).astype(np.float32)
        sin = np.sin(ang).astype(np.float32)
        p = np.arange(128)
        j = p % d
        i = j % (d // 2)
        sign = np.where(j < d // 2, -1.0, 1.0).astype(np.float32)
        tab = np.empty((4, 128, 2, 512), np.float32)
        cosT = cos[:, i].T
        sinT = (sin[:, i] * sign[None, :]).T
        for blk in range(4):
            tab[blk, :, 0, :] = cosT[:, blk * 512:(blk + 1) * 512]
            tab[blk, :, 1, :] = sinT[:, blk * 512:(blk + 1) * 512]
        out[name] = tab
    return out


def _na_tile_dr0():
    tiles = []
    for i in range(7):
        tiles.append((-6 + 2 * i, (True, True)))
    for i in range(7):
        tiles.append((-7 + 2 * i, (True, True)))
    for i in range(5):
        dr0 = -5 + 2 * i
        tiles.append((dr0, (i != 0, i != 4)))
    return tiles


def _bias_w(rpb_l):
    tiles = _na_tile_dr0()
    cq = np.arange(64)
    ck = np.arange(64)
    c_start = np.clip(cq - 8, 0, 48)
    ok = (ck[:, None] >= c_start[None, :]) & (ck[:, None] < c_start[None, :] + 16)
    dc = np.clip(ck[:, None] - cq[None, :], -15, 15) + 15
    out = np.full((128, 4, NTW, 64), NEGB, np.float32)
    for h in range(4):
        for t, (dr0, valid) in enumerate(tiles):
            for a in range(2):
                if not valid[a]:
                    continue
                dr = dr0 + a
                g = rpb_l[h, dr + 7][dc]
                out[a * 64:(a + 1) * 64, h, t, :] = np.where(ok, g, np.float32(NEGB))
    return out.reshape(128, 4 * NTW * 64)


def _tmask():
    p = np.arange(128)[:, None]
    f = np.arange(TM_W)[None, :] - TM_F0
    d = f - p
    ad = np.abs(d)
    m = (ad <= 64).astype(np.float32) + ((d % 4 == 0) & (ad <= 256)) + ((d % 16 == 0) & (ad <= 1024))
    return m.astype(np.float32)


def _na_row_plan(rq):
    kr0 = min(max(rq - 4, 0), 24)
    if kr0 % 2 == 0:
        kt0 = kr0 // 2
        nt = 4
        dr0 = kr0 - rq
        if dr0 % 2 == 0:
            t0 = (dr0 + 6) // 2
        else:
            t0 = 7 + (dr0 + 7) // 2
    else:
        kt0 = (kr0 - 1) // 2
        nt = 5
        t0 = 14
    return kt0, nt, t0


class Buf:
    __slots__ = ("w", "r", "name", "excl")

    def __init__(self, name="", excl=False):
        self.w = {}
        self.r = {}
        self.name = name
        self.excl = excl


def _merge(d, s):
    for k, v in s.items():
        if d.get(k, 0) < v:
            d[k] = v


class Sched:
    def __init__(self, nc, es):
        self.nc = nc
        self.es = es
        self.e = dict(pe=nc.tensor, act=nc.scalar, dve=nc.vector, pool=nc.gpsimd, sp=nc.sync)
        self.sems = {}
        self.cnt = {}
        self.waited = {k: {} for k in self.e}
        self.epoch = 0
        self.pend = {k: [] for k in self.e}
        self.nwait = 0
        self.nins = 0

    def sem(self, key):
        if key not in self.sems:
            nm = "s_" + "_".join(str(x) for x in key)
            self.sems[key] = self.es.enter_context(self.nc.semaphore(nm))
            self.cnt[key] = 0
        return self.sems[key]

    def wait(self, eng, deps):
        w = self.waited[eng]
        for k, v in deps.items():
            if w.get(k, 0) < v:
                self.e[eng].wait_ge(self.sem(k), v)
                w[k] = v
                self.nwait += 1

    def deps_of(self, reads, writes):
        d = {}
        for b in reads:
            _merge(d, b.w)
            if b.excl:
                _merge(d, b.r)
        for b in writes:
            _merge(d, b.w)
            _merge(d, b.r)
        return d

    def op(self, eng, fn, reads=(), writes=(), signal=True):
        d = self.deps_of(reads, writes)
        self.wait(eng, d)
        ins = fn(self.e[eng])
        self.nins += 1
        self.pend[eng].append((reads, writes))
        if not signal:
            return None
        key = (eng, self.epoch)
        s = self.sem(key)
        self.cnt[key] += 1
        ins.then_inc(s, 1)
        tok = {key: self.cnt[key]}
        for rs, ws in self.pend[eng]:
            for b in rs:
                if b.excl:
                    b.w = dict(tok)
                    b.r = {}
                else:
                    _merge(b.r, tok)
            for b in ws:
                b.w = dict(tok)
                b.r = {}
        self.pend[eng] = []
        return tok

    def dma(self, q, pairs, reads=(), writes=(), key=None, **kw):
        d = self.deps_of(reads, writes)
        self.wait(q, d)
        s = self.sem(key)
        for out, in_ in pairs:
            ins = self.e[q].dma_start(out=out, in_=in_, **kw)
            self.cnt[key] += 16
            ins.then_inc(s, 16)
            self.nins += 1
        tok = {key: self.cnt[key]}
        for b in reads:
            _merge(b.r, tok)
        for b in writes:
            b.w = dict(tok)
            b.r = {}
        return tok

    def barrier(self):
        d = {k: v for k, v in self.cnt.items() if v > 0}
        for eng in self.e:
            self.wait(eng, d)


class Item:
    def __init__(self, w, fn):
        self.w = w
        self.fn = fn


KB = 256
ARENA_KB = 206


class Builder:
    def __init__(self, layer_ids, src_is_x, do_final):
        self.layer_ids = layer_ids
        self.do_final = do_final
        nl = len(layer_ids)
        nc = bass.Bass("TRN2", target_bir_lowering=False)
        self.nc = nc
        self.hin = nc.dram_tensor("hin", [D, S], F32, kind="ExternalInput").ap()
        self.pT = nc.dram_tensor("pT", [nl, 256, S], F32, kind="ExternalInput").ap()
        self.wpk = nc.dram_tensor("wpk", [nl, 128, WTOT], F32, kind="ExternalInput").ap()
        self.cvec = nc.dram_tensor("cvec", [128, NCV], F32, kind="ExternalInput").ap()
        self.lamv = nc.dram_tensor("lamv", [nl, 128], F32, kind="ExternalInput").ap()
        self.biasw = nc.dram_tensor("biasw", [nl, 128, 4 * NTW * 64], F32, kind="ExternalInput").ap()
        self.tmask = nc.dram_tensor("tmask", [128, TM_W], F32, kind="ExternalInput").ap()
        self.ropeA = nc.dram_tensor("ropeA", [4, 128, 2, 512], F32, kind="ExternalInput").ap()
        self.ropeC = nc.dram_tensor("ropeC", [4, 128, 2, 512], F32, kind="ExternalInput").ap()
        self.hout = nc.dram_tensor("hout", [D, S], F32, kind="ExternalOutput").ap()
        self.hs = nc.dram_tensor("hscr", [D, S], F32, kind="Internal").ap()
        with ExitStack() as es:
            self.es = es
            self.S = Sched(nc, es)
            self.AR = es.enter_context(nc.sbuf_tensor("arena", [128, ARENA_KB * KB], F32))
            self.PS = es.enter_context(nc.psum_tensor("ps", [128, 4096], F32))
            self.PB = [Buf(f"pb{i}", excl=True) for i in range(8)]
            self._layout()
            self._emit()

    def f32v(self, off_kb, n):
        o = int(off_kb * KB)
        return self.AR[:, o:o + n]

    def bfv(self, off_kb, n):
        o = int(off_kb * KB)
        return self.AR[:, o:o + n // 2].bitcast(BF16)

    def _layout(self):
        self.HN = self.bfv(0, 8 * S).rearrange("p (c t) -> p c t", c=8)
        self.RING = [self.bfv(32 + 6 * i, SLOT) for i in range(NSLOT)]
        self.RINGb = [Buf(f"ring{i}") for i in range(NSLOT)]
        self.CVt = self.f32v(56, NCV)
        self.ONES = self.bfv(60, 128)
        self.ONESBD = self.bfv(60.25, 128)
        self.LQ = self.f32v(60.5, 128)
        self.SM = self.f32v(61, 64)
        self.TMP32 = self.f32v(61.25, 64)
        self.PPW = self.bfv(62, 2048).rearrange("p (k n) -> p k n", k=2)
        P0 = 66
        self.QT = self.bfv(P0, 4 * S).rearrange("p (c t) -> p c t", c=4)
        self.KT = self.bfv(P0 + 16, 4 * S).rearrange("p (c t) -> p c t", c=4)
        self.VA = self.bfv(P0 + 32, 16 * 4 * 128).rearrange("p (j h d) -> p j h d", j=16, h=4)
        self.VAflat = self.bfv(P0 + 32, 16 * 4 * 128)
        o = P0 + 48
        self.ROPE = [self.f32v(o + 4 * i, 1024).rearrange("p (a t) -> p a t", a=2) for i in range(2)]
        o += 8
        self.TM = self.bfv(o, TM_W)
        o += 6
        self.W = self.bfv(o, 4 * NTW * 64).rearrange("p (h n) -> p h n", h=4)
        o += 10
        self.PT = [self.bfv(o + i, 512) for i in range(6)]
        self.PTpair = [self.bfv(o + 2 * i, 1024) for i in range(3)]
        self.PTN = [self.bfv(o + 1.25 * i, 640).rearrange("p (j n) -> p j n", j=2) for i in range(3)]
        o += 6
        self.T1 = [self.f32v(o + 2 * i, 512) for i in range(2)]
        o += 4
        self.T2 = [self.f32v(o + 2 * i, 512) for i in range(2)]
        o += 4
        self.REC = [self.f32v(o + 2 * i, 512) for i in range(2)]
        o += 4
        self.RS = [self.f32v(o + 2 * i, 512) for i in range(2)]
        o += 4
        self.OC = [self.f32v(o + 2 * i, 512) for i in range(2)]
        o += 4
        self.BSTG = self.f32v(o, NTW * 64)
        o += 5
        self.SQT = self.bfv(o, 5 * 512).rearrange("p (c t) -> p c t", c=5)
        o += 5
        self.MIXo = o
        self.MIX = self.bfv(o, 8 * S).rearrange("p (c t) -> p c t", c=8)
        o += 32
        assert o <= ARENA_KB, o
        self.H = self.f32v(P0, 8 * S).rearrange("p (c t) -> p c t", c=8)
        ob = P0 + 64
        assert ob <= self.MIXo
        self.ACTB = self.bfv(ob, 22 * 1024).rearrange("p (c t) -> p c t", c=22)
        self.PBt = self.bfv(ob, 2 * S).rearrange("p (k t) -> p k t", k=2)
        self.ET = [self.f32v(ob + 8 + 2 * i, 512) for i in range(2)]
        ob += 44
        self.CA = [self.f32v(ob + 4 * i, 1024) for i in range(3)]
        ob += 12
        self.CG = [self.bfv(ob + 2 * i, 1024) for i in range(2)]
        ob += 4
        self.RSB = [self.f32v(ob + 2 * i, 512) for i in range(2)]
        ob += 4
        assert ob <= ARENA_KB, ob

    def bank(self, b, n=512, off=0):
        return self.PS[:, b * 512 + off:b * 512 + off + n]

    def mm(self, out, obufs, lhsT, rhs, rd, start, stop, signal=None):
        if signal is None:
            signal = stop
        return self.S.op("pe", lambda e: e.matmul(out, lhsT=lhsT, rhs=rhs, start=start, stop=stop),
                         reads=rd, writes=obufs, signal=signal)

    def cv(self, li, name, idx=0):
        c = li * NCL + CV[name] + idx
        return self.CVt[:, c:c + 1]

    def run_items(self, items):
        reqs = []
        for i, it in enumerate(items):
            for w in it.w:
                reqs.append(w)
        state = {"issued": 0}

        def issue_upto(n):
            n = min(n, len(reqs))
            while state["issued"] < n:
                j = state["issued"]
                li, name = reqs[j]
                o, kc, m = WT_OFF[name]
                sl = j % NSLOT
                dst = self.RING[sl][:, 0:kc * m].rearrange("p (k m) -> p k m", k=kc)
                src = self.wpk[li, :, o:o + kc * m].rearrange("p (k m) -> p k m", k=kc)
                self.S.dma("pool", [(dst, src)], writes=[self.RINGb[sl]], key=("ring", sl))
                state["issued"] += 1

        ptr = 0
        for it in items:
            n = len(it.w)
            issue_upto(ptr + n)
            tiles = []
            for j in range(ptr, ptr + n):
                li, name = reqs[j]
                o, kc, m = WT_OFF[name]
                sl = j % NSLOT
                tiles.append((self.RING[sl][:, 0:kc * m].rearrange("p (k m) -> p k m", k=kc), self.RINGb[sl]))
            it.fn(tiles)
            ptr += n
            issue_upto(ptr + NSLOT)

    def _emit(self):
        S_ = self.S
        nc = self.nc
        items = []
        nl = len(self.layer_ids)

        def add(w, fn):
            items.append(Item(w, fn))

        add([], self.prologue)
        for li in range(nl):
            self.add_phase_a(items, li)
            self.add_phase_b(items, li)
        import os
        nit = int(os.environ.get("NITEMS", "0"))
        if nit:
            items = items[:nit]
        self.run_items(items)
        S_.barrier()

    def new_h_bufs(self):
        self.Hb = [[Buf(f"h{k}_{b}") for b in range(4)] for k in range(8)]
        self.HNb = [[Buf(f"hn{k}_{b}") for b in range(4)] for k in range(8)]

    def prologue(self, tiles):
        S_ = self.S
        self.new_h_bufs()
        self.cvb = Buf("cv")
        S_.dma("sp", [(self.CVt, self.cvec)], writes=[self.cvb], key=("cv",))
        self.onesb = Buf("ones")
        S_.op("pool", lambda e: e.memset(self.ONES, 1.0), writes=[self.onesb])
        S_.op("pool", lambda e: e.memset(self.ONESBD, 0.0), writes=[self.onesb])
        S_.op("pool", lambda e: e.memset(self.ONESBD[0:64, 0:64], 1.0), writes=[self.onesb])
        S_.op("pool", lambda e: e.memset(self.ONESBD[64:128, 64:128], 1.0), writes=[self.onesb])
        self.load_h(self.hin)
        self.rmsnorm_main(0, "attn_norm")
        S_.barrier()

    def load_h(self, src):
        for kc in range(8):
            self.S.dma("sp", [(self.H[:, kc, :], src[kc * 128:(kc + 1) * 128, :])],
                       writes=self.Hb[kc], key=("hld", kc))

    def store_h(self, dst):
        for kc in range(8):
            self.S.dma("sp", [(dst[kc * 128:(kc + 1) * 128, :], self.H[:, kc, :])],
                       reads=self.Hb[kc], key=("hst",))

    def _rsbufs(self):
        rsb = getattr(self, "_rsb", None)
        if rsb is None:
            rsb = self._rsb = [Buf("rsb0"), Buf("rsb1")]
        return rsb

    def norm_s1(self, blk):
        ts = slice(blk * 512, (blk + 1) * 512)
        hb = [self.Hb[k][blk] for k in range(8)]
        self.S.op("act", lambda e: e.activation(out=self.HN[:, :, ts], in_=self.H[:, :, ts], func=AF.Square),
                  reads=hb, writes=[self.HNb[k][blk] for k in range(8)])

    def norm_s2(self, li, gname, blk, final=False, gcol=None, rsv=None):
        S_ = self.S
        rsb = self._rsbufs()
        rsv = rsv or self.RSB
        ts = slice(blk * 512, (blk + 1) * 512)
        for kc in range(8):
            self.mm(self.bank(7), [self.PB[7]], self.ONES, self.HN[:, kc, ts],
                    [self.HNb[kc][blk], self.onesb], kc == 0, kc == 7)
        r = blk % 2
        rs = rsv[r]
        S_.op("act", lambda e: e.activation(out=rs, in_=self.bank(7), func=AF.Ln, bias=1e-6, scale=1.0 / D),
              reads=[self.PB[7]], writes=[rsb[r]])
        S_.op("act", lambda e: e.activation(out=rs, in_=rs, func=AF.Exp, scale=-0.5), reads=[rsb[r]], writes=[rsb[r]])
        for kc in range(8):
            if gcol is not None:
                g = self.CVt[:, gcol + kc:gcol + kc + 1]
            else:
                g = self.cv(li, gname, kc)
            if final:
                S_.op("dve", lambda e: e.scalar_tensor_tensor(out=self.H[:, kc, ts], in0=self.H[:, kc, ts], scalar=g,
                                                              in1=rs, op0=ALU.mult, op1=ALU.mult),
                      reads=[rsb[r], self.cvb], writes=[self.Hb[kc][blk]])
            else:
                S_.op("dve", lambda e: e.scalar_tensor_tensor(out=self.HN[:, kc, ts], in0=self.H[:, kc, ts], scalar=g,
                                                              in1=rs, op0=ALU.mult, op1=ALU.mult),
                      reads=[self.Hb[kc][blk], rsb[r], self.cvb], writes=[self.HNb[kc][blk]])

    def rmsnorm_main(self, li, gname, final=False, gcol=None):
        for blk in range(4):
            self.norm_s1(blk)
            self.norm_s2(li, gname, blk, final=final, gcol=gcol)

    def get_rope(self, table, blk):
        i = self._rope_i = (getattr(self, "_rope_i", -1) + 1) % 2
        src = (self.ropeA if table == "A" else self.ropeC)[blk]
        self.S.dma("sp", [(self.ROPE[i], src)], writes=[self.ROPEb[i]], key=("rope", i))
        return self.ROPE[i], self.ROPEb[i]

    def rot(self, name, n):
        v = getattr(self, "_rot_" + name, -1)
        v = (v + 1) % n
        setattr(self, "_rot_" + name, v)
        return v

    def add_phase_a(self, items, li):
        S_ = self.S

        def start(tiles):
            self.QTb = [[Buf() for _ in range(4)] for _ in range(4)]
            self.KTb = [[Buf() for _ in range(4)] for _ in range(4)]
            self.VAb = [Buf() for _ in range(16)]
            self.MIXb = [[Buf() for _ in range(4)] for _ in range(8)]
            self.ROPEb = [Buf(), Buf()]
            self.PTb = [Buf() for _ in range(6)]
            self.T1b = [Buf(), Buf()]
            self.T2b = [Buf(), Buf()]
            self.RECb = [Buf(), Buf()]
            self.RSb = [Buf(), Buf()]
            self.OCb = [Buf(), Buf()]
            self.SQTb = Buf()
            self.TMb = Buf()
            self.Wb = Buf()
            self.BSTGb = Buf()
            self.LQb = Buf()
            self.SMb = Buf()
            for b in self.PB:
                b.w = {}
                b.r = {}
            S_.op("pool", lambda e: e.memset(self.VAflat, 1.0), writes=self.VAb)
            S_.dma("pool", [(self.TM.rearrange("p (a n) -> p a n", a=2),
                             self.tmask.rearrange("p (a n) -> p a n", a=2))], writes=[self.TMb], key=("tm",))
            for h in range(4):
                S_.dma("sp", [(self.BSTG, self.biasw[li, :, h * NTW * 64:(h + 1) * NTW * 64])],
                       writes=[self.BSTGb], key=("bstg",))
                S_.op("act", lambda e: e.activation(out=self.W[:, h, :], in_=self.BSTG, func=AF.Exp),
                      reads=[self.BSTGb], writes=[self.Wb])
            S_.dma("sp", [(self.LQ.rearrange("p (a n) -> p a n", a=1), self.lamv[li:li + 1, :].partition_broadcast(128))], writes=[self.LQb], key=("lq",))
            S_.op("dve", lambda e: e.tensor_tensor(out=self.TMP32[:, 0:32], in0=self.LQ[:, 0:32], in1=self.LQ[:, 32:64],
                                                   op=ALU.mult), reads=[self.LQb], writes=[self.SMb])
            S_.op("dve", lambda e: e.tensor_tensor(out=self.TMP32[:, 32:64], in0=self.LQ[:, 64:96], in1=self.LQ[:, 96:128],
                                                   op=ALU.mult), reads=[self.LQb], writes=[self.SMb])
            S_.op("dve", lambda e: e.reduce_sum(out=self.SM[:, 0:1], in_=self.TMP32[:, 0:32], axis=AX.X),
                  reads=[self.SMb], writes=[self.SMb])
            S_.op("dve", lambda e: e.reduce_sum(out=self.SM[:, 1:2], in_=self.TMP32[:, 32:64], axis=AX.X),
                  reads=[self.SMb], writes=[self.SMb])
            S_.op("act", lambda e: e.activation(out=self.SM[:, 2:4], in_=self.SM[:, 0:2], func=AF.Exp),
                  reads=[self.SMb], writes=[self.SMb])
            S_.op("dve", lambda e: e.tensor_tensor(out=self.SM[:, 4:5], in0=self.SM[:, 3:4], in1=self.SM[:, 2:3],
                                                   op=ALU.subtract), reads=[self.SMb], writes=[self.SMb])
            S_.op("dve", lambda e: e.tensor_tensor(out=self.SM[:, 5:6], in0=self.SM[:, 4:5], in1=self.cv(li, "laminit"),
                                                   op=ALU.subtract), reads=[self.SMb, self.cvb], writes=[self.SMb])
            S_.op("dve", lambda e: e.tensor_tensor(out=self.SM[:, 6:7], in0=self.cv(li, "subln"), in1=self.cv(li, "omli"),
                                                   op=ALU.mult), reads=[self.SMb, self.cvb], writes=[self.SMb])

        items.append(Item([], start))
        self.add_mixer_b(items, li)
        self.add_mixer_a(items, li)
        self.add_mixer_c(items, li)
        self.add_mixer_d(items, li)

    def evac_copy(self, dst, src, rd, wr):
        eng = "act" if self.rot("evac", 2) == 0 else "dve"
        if eng == "act":
            self.S.op("act", lambda e: e.activation(out=dst, in_=src, func=AF.Copy), reads=rd, writes=wr)
        else:
            self.S.op("dve", lambda e: e.tensor_copy(out=dst, in_=src), reads=rd, writes=wr)

    def proj_rope_chunk(self, wt, wb, col0, table, blk, dsts):
        S_ = self.S
        ts = slice(blk * 512, (blk + 1) * 512)
        b1 = self.rot("pbank", 7)
        b2 = self.rot("pbank", 7)
        for kc in range(8):
            self.mm(self.bank(b1), [self.PB[b1]], wt[:, kc, col0:col0 + 128], self.HN[:, kc, ts],
                    [wb, self.HNb[kc][blk]], kc == 0, kc == 7)
        for kc in range(8):
            self.mm(self.bank(b2), [self.PB[b2]], wt[:, kc, col0 + 128:col0 + 256], self.HN[:, kc, ts],
                    [wb, self.HNb[kc][blk]], kc == 0, kc == 7)
        rope, ropeb = self.get_rope(table, blk)
        i = self.rot("t12", 2)
        S_.op("dve", lambda e: e.tensor_tensor(out=self.T1[i], in0=self.bank(b1), in1=rope[:, 0, :], op=ALU.mult),
              reads=[self.PB[b1], ropeb], writes=[self.T1b[i]])
        S_.op("dve", lambda e: e.tensor_tensor(out=self.T2[i], in0=self.bank(b2), in1=rope[:, 1, :], op=ALU.mult),
              reads=[self.PB[b2], ropeb], writes=[self.T2b[i]])
        for lo, hi, dst, db in dsts:
            S_.op("pool", lambda e: e.tensor_tensor(out=dst, in0=self.T1[i][lo:hi, :], in1=self.T2[i][lo:hi, :], op=ALU.add),
                  reads=[self.T1b[i], self.T2b[i]], writes=[db])

    def proj_v(self, wt, wb, col0, nkc, src, srcb_fn):
        for jt in range(16):
            b = self.rot("pbank", 7)
            for kc in range(nkc):
                self.mm(self.bank(b, 256), [self.PB[b]], src[:, kc, jt * 128:(jt + 1) * 128], wt[:, kc, col0:col0 + 256],
                        [wb] + srcb_fn(kc, jt), kc == 0, kc == nkc - 1)
            pv = self.bank(b, 256).rearrange("p (hp two d) -> p hp two d", two=2, d=64)
            va = self.VA[:, jt, :, :].rearrange("p (hp two) d -> p hp two d", two=2)
            self.S.op("act", lambda e: e.activation(out=va[:, :, 0, 0:64], in_=pv[:, :, 0, :], func=AF.Copy),
                      reads=[self.PB[b]], writes=[self.VAb[jt]])
            self.S.op("dve", lambda e: e.tensor_copy(out=va[:, :, 1, 64:128], in_=pv[:, :, 1, :]),
                      reads=[self.PB[b]], writes=[self.VAb[jt]])

    def attn_dense(self, kind, li, chunk0, scale):
        S_ = self.S
        nmaps = 2 if kind == "C" else 1
        groups = []
        for c in range(2):
            for ib in range(4):
                if kind == "A":
                    tl = [jt for jt in range(16) if -TM_F0 <= ib * 512 - jt * 128 <= 1024]
                    for idx, jt in enumerate(tl):
                        groups.append([dict(c=c, ib=ib, j=j, h=2 * c + j, m=0, jt=jt, first=idx == 0,
                                            last=idx == len(tl) - 1) for j in range(2)])
                elif kind == "C":
                    for j in range(2):
                        for jt in range(16):
                            groups.append([dict(c=c, ib=ib, j=j, h=2 * c + j, m=m, jt=jt, first=jt == 0, last=jt == 15)
                                           for m in range(2)])
                else:
                    for j in range(2):
                        for jt in range(16):
                            groups.append([dict(c=c, ib=ib, j=j, h=2 * c + j, m=0, jt=jt, first=jt == 0, last=jt == 15)])
        if kind == "A":
            SB, UB, LAG = [0, 1, 2, 3], [4, 5, 6], 1
        elif kind == "B":
            SB, UB, LAG = [0, 1, 2, 3], [4, 5, 6], 3
        else:
            SB, UB, LAG = [0, 1, 2, 3], [4, 5, 6], 1
        paired = kind != "B"
        ucur = {}
        deferred = []
        un = [0]

        def qk_group(grp):
            if paired:
                s0 = SB[2 * self.rot("sgrp", len(SB) // 2)]
                sbs = [s0, s0 + 1]
            else:
                sbs = [SB[self.rot("sbank", len(SB))]]
            for st, sb in zip(grp, sbs):
                st["sb"] = sb
                h, m, jt, ib = st["h"], st["m"], st["jt"], st["ib"]
                qs = slice(ib * 512, (ib + 1) * 512)
                ks = slice(jt * 128, (jt + 1) * 128)
                kblk = jt // 4
                if kind in ("A",):
                    tile, lo, hi = h // 2, (h % 2) * 64, (h % 2) * 64 + 64
                elif kind == "B":
                    tile, lo, hi = h, 0, 96
                else:
                    tile, lo, hi = h, 32 * m, 32 * m + 32
                self.mm(self.bank(sb), [self.PB[sb]], self.KT[lo:hi, tile, ks], self.QT[lo:hi, tile, qs],
                        [self.KTb[tile][kblk], self.QTb[tile][ib]], True, True)

        for g in range(min(LAG, len(groups))):
            qk_group(groups[g])
        D = 1 if paired else 0
        NPT = len(self.PT)

        def do_exp(g):
            grp = groups[g]
            for i, st in enumerate(grp):
                sb = st["sb"]
                ib, jt, j = st["ib"], st["jt"], st["j"]
                if paired:
                    if i == 0:
                        p0 = 2 * self.rot("ptgrp", NPT // 2)
                        s0 = sb
                        outv = self.PTpair[p0 // 2]
                        S_.op("act", lambda e: e.activation(out=outv, in_=self.PS[:, s0 * 512:s0 * 512 + 1024],
                                                            func=AF.Exp, scale=scale),
                              reads=[self.PB[s0], self.PB[s0 + 1]], writes=[self.PTb[p0], self.PTb[p0 + 1]])
                        self._grp_p0 = p0
                    p = self._grp_p0 + i
                else:
                    p = self.rot("pt", NPT)
                    S_.op("act", lambda e: e.activation(out=self.PT[p], in_=self.bank(sb), func=AF.Exp, scale=scale),
                          reads=[self.PB[sb]], writes=[self.PTb[p]])
                st["p"] = p
                if kind == "A":
                    off = ib * 512 - jt * 128 + TM_F0
                    S_.op("pool" if (j == 1 and jt % 2 == 1) else "dve",
                          lambda e: e.tensor_tensor(out=self.PT[p], in0=self.PT[p], in1=self.TM[:, off:off + 512],
                                                    op=ALU.mult), reads=[self.PTb[p], self.TMb], writes=[self.PTb[p]])

        sched = []
        for g in range(len(groups) + D):
            if g < len(groups):
                sched.append(("front", g, 0, None))
            if g - D >= 0:
                for i, st in enumerate(groups[g - D]):
                    sched.append(("pv", g - D, i, st))
        for what, g, i, st in sched:
            if what == "front":
                if g + LAG < len(groups):
                    qk_group(groups[g + LAG])
                do_exp(g)
                continue
            h, m, jt, ib, c, j, p = st["h"], st["m"], st["jt"], st["ib"], st["c"], st["j"], st["p"]
            if st["first"]:
                ucur[(j, m)] = UB[un[0] % len(UB)]
                un[0] += 1
            ub = ucur[(j, m)]
            self.mm(self.bank(ub), [self.PB[ub]], self.VA[:, jt, h, :], self.PT[p],
                    [self.VAb[jt], self.PTb[p]], st["first"], st["last"],
                    signal=(st["last"] or g + LAG + D + 2 >= len(groups)))
            for _ in range(NFILL.get(kind, 0) if (not paired or i == 1) else 0):
                self.mm(self.bank(7, FILLN), [self.PB[7]], self.ONES, self.TM[:, 0:FILLN], [self.onesb, self.TMb], True, True,
                        signal=False)
            for d in list(deferred):
                d[0] -= 1
                if d[0] <= 0:
                    d[1]()
                    deferred.remove(d)
            if st["last"] and m == nmaps - 1:
                orow = slice(0, 64) if j == 0 else slice(64, 128)
                drow = slice(64, 128) if j == 0 else slice(0, 64)
                qs = slice(ib * 512, (ib + 1) * 512)
                def copy_out(u):
                    r = self.rot("rec", 2)
                    a = self.rot("t12", 2)
                    S_.op("dve", lambda e: e.tensor_copy(out=self.REC[r][orow, :], in_=self.bank(u)[drow, :]),
                          reads=[self.PB[u]], writes=[self.RECb[r]])
                    S_.op("dve", lambda e: e.tensor_copy(out=self.T1[a][orow, :], in_=self.bank(u)[orow, :]),
                          reads=[self.PB[u]], writes=[self.T1b[a]])
                    return r, a

                if kind != "C":
                    r, a = copy_out(ub)
                    S_.op("dve", lambda e: e.reciprocal(out=self.REC[r][orow, :], in_=self.REC[r][orow, :]),
                          reads=[self.RECb[r]], writes=[self.RECb[r]])
                    S_.op("dve", lambda e: e.tensor_tensor(out=self.MIX[orow, chunk0 + c, qs], in0=self.T1[a][orow, :],
                                                           in1=self.REC[r][orow, :], op=ALU.mult),
                          reads=[self.T1b[a], self.RECb[r]], writes=[self.MIXb[chunk0 + c][ib]])
                else:
                    u1, u2 = ucur[(j, 0)], ucur[(j, 1)]
                    oc = (c * 4 + ib) % 2
                    r1, a1 = copy_out(u1)
                    r2, a2 = copy_out(u2)
                    S_.op("dve", lambda e: e.reciprocal(out=self.REC[r1][orow, :], in_=self.REC[r1][orow, :]),
                          reads=[self.RECb[r1]], writes=[self.RECb[r1]])
                    S_.op("dve", lambda e: e.tensor_tensor(out=self.OC[oc][orow, :], in0=self.T1[a1][orow, :],
                                                           in1=self.REC[r1][orow, :], op=ALU.mult),
                          reads=[self.T1b[a1], self.RECb[r1]], writes=[self.OCb[oc]])
                    S_.op("dve", lambda e: e.reciprocal(out=self.REC[r2][orow, :], in_=self.REC[r2][orow, :]),
                          reads=[self.RECb[r2]], writes=[self.RECb[r2]])
                    S_.op("dve", lambda e: e.tensor_tensor(out=self.T1[a2][orow, :], in0=self.T1[a2][orow, :],
                                                           in1=self.REC[r2][orow, :], op=ALU.mult),
                          reads=[self.T1b[a2], self.RECb[r2]], writes=[self.T1b[a2]])
                    S_.op("dve", lambda e: e.scalar_tensor_tensor(out=self.OC[oc][orow, :], in0=self.T1[a2][orow, :],
                                                                  scalar=self.SM[orow, 5:6], in1=self.OC[oc][orow, :],
                                                                  op0=ALU.mult, op1=ALU.add),
                          reads=[self.T1b[a2], self.SMb, self.OCb[oc]], writes=[self.OCb[oc]])
                    if j == 1:
                        def fin(oc=oc, c=c, ib=ib, qs=qs):
                            S_.op("act", lambda e: e.activation(out=self.SQT[:, 0, :], in_=self.OC[oc], func=AF.Square),
                                  reads=[self.OCb[oc]], writes=[self.SQTb])
                            fb = 7
                            self.mm(self.bank(fb), [self.PB[fb]], self.ONESBD, self.SQT[:, 0, :], [self.SQTb, self.onesb], True, True)
                            rr = self.rot("rs", 2)
                            S_.op("act", lambda e: e.activation(out=self.RS[rr], in_=self.bank(fb), func=AF.Ln, bias=1e-5,
                                                                scale=1.0 / 64), reads=[self.PB[fb]], writes=[self.RSb[rr]])
                            S_.op("act", lambda e: e.activation(out=self.RS[rr], in_=self.RS[rr], func=AF.Exp, scale=-0.5),
                                  reads=[self.RSb[rr]], writes=[self.RSb[rr]])
                            S_.op("dve", lambda e: e.scalar_tensor_tensor(out=self.MIX[:, chunk0 + c, qs], in0=self.OC[oc],
                                                                          scalar=self.SM[:, 6:7], in1=self.RS[rr],
                                                                          op0=ALU.mult, op1=ALU.mult),
                                  reads=[self.OCb[oc], self.SMb, self.RSb[rr]], writes=[self.MIXb[chunk0 + c][ib]])
                        deferred.append([6, fin])
        for d in deferred:
            d[1]()

    def add_mixer_b(self, items, li):
        S_ = self.S
        CQN = self.MIX[:, 4:7, :]
        CKVN = self.MIX[:, 0:2, :]

        def b1(tiles):
            (t1, t1b), (t2, t2b) = tiles
            for blk in range(4):
                ts = slice(blk * 512, (blk + 1) * 512)
                for c in range(3):
                    for kc in range(8):
                        self.mm(self.bank(c), [self.PB[c]], t1[:, kc, c * 128:(c + 1) * 128], self.HN[:, kc, ts],
                                [t1b, self.HNb[kc][blk]], kc == 0, kc == 7)
                for c in range(2):
                    for kc in range(8):
                        self.mm(self.bank(3 + c), [self.PB[3 + c]], t2[:, kc, c * 128:(c + 1) * 128], self.HN[:, kc, ts],
                                [t2b, self.HNb[kc][blk]], kc == 0, kc == 7)
                for kc in range(8):
                    self.mm(self.bank(5)[0:64, :], [self.PB[5]], t2[:, kc, 256:320], self.HN[:, kc, ts],
                            [t2b, self.HNb[kc][blk]], kc == 0, kc == 7)
                S_.op("act", lambda e: e.activation(out=self.SQT[:, 0:3, :], in_=self.PS[:, 0:1536].rearrange("p (c t) -> p c t", c=3),
                                                    func=AF.Square), reads=self.PB[0:3], writes=[self.SQTb])
                S_.op("act", lambda e: e.activation(out=self.SQT[:, 3:5, :], in_=self.PS[:, 1536:2560].rearrange("p (c t) -> p c t", c=2),
                                                    func=AF.Square), reads=self.PB[3:5], writes=[self.SQTb])
                for c in range(3):
                    self.mm(self.bank(6), [self.PB[6]], self.ONES, self.SQT[:, c, :], [self.SQTb, self.onesb], c == 0, c == 2)
                for c in range(2):
                    self.mm(self.bank(7), [self.PB[7]], self.ONES, self.SQT[:, 3 + c, :], [self.SQTb, self.onesb], c == 0, c == 1)
                S_.op("act", lambda e: e.activation(out=self.RS[0], in_=self.bank(6), func=AF.Ln, bias=1e-6, scale=1.0 / 384),
                      reads=[self.PB[6]], writes=[self.RSb[0]])
                S_.op("act", lambda e: e.activation(out=self.RS[1], in_=self.bank(7), func=AF.Ln, bias=1e-6, scale=1.0 / 256),
                      reads=[self.PB[7]], writes=[self.RSb[1]])
                S_.op("act", lambda e: e.activation(out=self.RS[0], in_=self.RS[0], func=AF.Exp, scale=-0.5),
                      reads=[self.RSb[0]], writes=[self.RSb[0]])
                S_.op("act", lambda e: e.activation(out=self.RS[1], in_=self.RS[1], func=AF.Exp, scale=-0.5),
                      reads=[self.RSb[1]], writes=[self.RSb[1]])
                for c in range(3):
                    S_.op("dve", lambda e: e.scalar_tensor_tensor(out=CQN[:, c, ts], in0=self.bank(c), scalar=self.cv(li, "q_norm", c),
                                                                  in1=self.RS[0], op0=ALU.mult, op1=ALU.mult),
                          reads=[self.PB[c], self.RSb[0], self.cvb], writes=[self.MIXb[4 + c][blk]])
                for c in range(2):
                    S_.op("dve", lambda e: e.scalar_tensor_tensor(out=CKVN[:, c, ts], in0=self.bank(3 + c),
                                                                  scalar=self.cv(li, "kv_norm", c), in1=self.RS[1],
                                                                  op0=ALU.mult, op1=ALU.mult),
                          reads=[self.PB[3 + c], self.RSb[1], self.cvb], writes=[self.MIXb[c][blk]])
                rope, ropeb = self.get_rope("C", blk)
                i = self.rot("t12", 2)
                S_.op("dve", lambda e: e.tensor_tensor(out=self.T1[i][0:32, :], in0=self.bank(5)[0:32, :], in1=rope[0:32, 0, :],
                                                       op=ALU.mult), reads=[self.PB[5], ropeb], writes=[self.T1b[i]])
                S_.op("dve", lambda e: e.tensor_tensor(out=self.T2[i][0:32, :], in0=self.bank(5)[32:64, :], in1=rope[32:64, 1, :],
                                                       op=ALU.mult), reads=[self.PB[5], ropeb], writes=[self.T2b[i]])
                for h in range(4):
                    S_.op("pool", lambda e: e.tensor_tensor(out=self.KT[64:96, h, ts], in0=self.T1[i][0:32, :],
                                                            in1=self.T2[i][0:32, :], op=ALU.add),
                          reads=[self.T1b[i], self.T2b[i]], writes=[self.KTb[h][blk]])

        items.append(Item([(li, "b_cq"), (li, "b_ckv")], b1))

        def b2(tiles):
            (t, tb), = tiles
            for c in range(2):
                for blk in range(4):
                    ts = slice(blk * 512, (blk + 1) * 512)
                    b = self.rot("pbank", 7)
                    for kc in range(2):
                        self.mm(self.bank(b), [self.PB[b]], t[:, kc, c * 128:(c + 1) * 128], CKVN[:, kc, ts],
                                [tb, self.MIXb[kc][blk]], kc == 0, kc == 1)
                    S_.op("act", lambda e: e.activation(out=self.KT[0:64, 2 * c, ts], in_=self.bank(b)[0:64, :], func=AF.Copy),
                          reads=[self.PB[b]], writes=[self.KTb[2 * c][blk]])
                    S_.op("dve", lambda e: e.tensor_copy(out=self.KT[0:64, 2 * c + 1, ts], in_=self.bank(b)[64:128, :]),
                          reads=[self.PB[b]], writes=[self.KTb[2 * c + 1][blk]])
            self.proj_v(t, tb, 256, 2, CKVN, lambda kc, jt: [self.MIXb[kc][jt // 4]])

        items.append(Item([(li, "ukv")], b2))

        def mk_b3(h):
            def b3(tiles):
                (t, tb), = tiles
                for blk in range(4):
                    ts = slice(blk * 512, (blk + 1) * 512)
                    b1_ = self.rot("pbank", 7)
                    b2_ = self.rot("pbank", 7)
                    for kc in range(3):
                        self.mm(self.bank(b1_)[0:96, :], [self.PB[b1_]], t[:, kc, 0:96], CQN[:, kc, ts],
                                [tb, self.MIXb[4 + kc][blk]], kc == 0, kc == 2)
                    for kc in range(3):
                        self.mm(self.bank(b2_)[0:32, :], [self.PB[b2_]], t[:, kc, 96:128], CQN[:, kc, ts],
                                [tb, self.MIXb[4 + kc][blk]], kc == 0, kc == 2)
                    S_.op("act", lambda e: e.activation(out=self.QT[0:64, h, ts], in_=self.bank(b1_)[0:64, :], func=AF.Copy),
                          reads=[self.PB[b1_]], writes=[self.QTb[h][blk]])
                    rope, ropeb = self.get_rope("C", blk)
                    i = self.rot("t12", 2)
                    S_.op("dve", lambda e: e.tensor_tensor(out=self.T1[i][64:96, :], in0=self.bank(b1_)[64:96, :],
                                                           in1=rope[64:96, 0, :], op=ALU.mult),
                          reads=[self.PB[b1_], ropeb], writes=[self.T1b[i]])
                    S_.op("dve", lambda e: e.tensor_tensor(out=self.T2[i][64:96, :], in0=self.bank(b2_)[0:32, :],
                                                           in1=rope[0:32, 1, :], op=ALU.mult),
                          reads=[self.PB[b2_], ropeb], writes=[self.T2b[i]])
                    S_.op("pool", lambda e: e.tensor_tensor(out=self.QT[64:96, h, ts], in0=self.T1[i][64:96, :],
                                                            in1=self.T2[i][64:96, :], op=ALU.add),
                          reads=[self.T1b[i], self.T2b[i]], writes=[self.QTb[h][blk]])
            return b3

        for h in range(4):
            items.append(Item([(li, f"uq{h}")], mk_b3(h)))
        items.append(Item([], lambda tiles: self.attn_dense("B", li, 2, 96 ** -0.5)))

    def add_mixer_a(self, items, li):
        def mk(which, c):
            def f(tiles):
                (t, tb), = tiles
                dst = self.QT if which == "q" else self.KT
                dstb = self.QTb if which == "q" else self.KTb
                for blk in range(4):
                    ts = slice(blk * 512, (blk + 1) * 512)
                    self.proj_rope_chunk(t, tb, 0, "A", blk, [(0, 128, dst[:, c, ts], dstb[c][blk])])
            return f

        for c in range(2):
            items.append(Item([(li, f"a_q{c}")], mk("q", c)))
        for c in range(2):
            items.append(Item([(li, f"a_k{c}")], mk("k", c)))

        def av(tiles):
            (t, tb), = tiles
            self.proj_v(t, tb, 0, 8, self.HN, lambda kc, jt: [self.HNb[kc][jt // 4]])

        items.append(Item([(li, "a_v")], av))
        items.append(Item([], lambda tiles: self.attn_dense("A", li, 0, 0.125)))

    def add_mixer_c(self, items, li):
        def mk(which, c):
            def f(tiles):
                (t, tb), = tiles
                dst = self.QT if which == "q" else self.KT
                dstb = self.QTb if which == "q" else self.KTb
                for blk in range(4):
                    ts = slice(blk * 512, (blk + 1) * 512)
                    self.proj_rope_chunk(t, tb, 0, "C", blk,
                                         [(0, 64, dst[0:64, 2 * c, ts], dstb[2 * c][blk]),
                                          (64, 128, dst[0:64, 2 * c + 1, ts], dstb[2 * c + 1][blk])])
            return f

        for c in range(2):
            items.append(Item([(li, f"c_q{c}")], mk("q", c)))
        for c in range(2):
            items.append(Item([(li, f"c_k{c}")], mk("k", c)))

        def cvf(tiles):
            (t, tb), = tiles
            self.proj_v(t, tb, 0, 8, self.HN, lambda kc, jt: [self.HNb[kc][jt // 4]])

        items.append(Item([(li, "c_v")], cvf))
        items.append(Item([], lambda tiles: self.attn_dense("C", li, 4, 32 ** -0.5)))

    def add_mixer_d(self, items, li):
        S_ = self.S

        def mk(which):
            def f(tiles):
                (t, tb), = tiles
                dst = self.QT if which == "q" else self.KT
                dstb = self.QTb if which == "q" else self.KTb
                for c in range(2):
                    for blk in range(4):
                        ts = slice(blk * 512, (blk + 1) * 512)
                        b = self.rot("pbank", 7)
                        for kc in range(8):
                            self.mm(self.bank(b), [self.PB[b]], t[:, kc, c * 128:(c + 1) * 128], self.HN[:, kc, ts],
                                    [tb, self.HNb[kc][blk]], kc == 0, kc == 7)
                        self.evac_copy(dst[:, c, ts], self.bank(b), [self.PB[b]], [dstb[c][blk]])
            return f

        items.append(Item([(li, "d_q")], mk("q")))
        items.append(Item([(li, "d_k")], mk("k")))

        def dv(tiles):
            (t, tb), = tiles
            self.proj_v(t, tb, 0, 8, self.HN, lambda kc, jt: [self.HNb[kc][jt // 4]])

        items.append(Item([(li, "d_v")], dv))

        def na(tiles):
            steps = [(rq, hp) for hp in range(2) for rq in range(32)]
            SR = [(0, 1), (2, 3), (4, 5)]
            OB = [6, 7]
            LA = 2
            plans = {}

            def qk(k):
                rq, hp = steps[k]
                kt0, nt, t0 = _na_row_plan(rq)
                plans[k] = (kt0, nt, t0)
                sr = SR[k % 3]
                last = None
                n = 2 * nt
                cnt = 0
                for j in range(2):
                    for m in range(nt):
                        cnt += 1
                        col = j * 512 + m * 64
                        o = self.PS[:, sr[0] * 512 + col: sr[0] * 512 + col + 64]
                        kt = kt0 + m
                        self.mm(o, [self.PB[sr[j]]],
                                self.KT[j * 64:(j + 1) * 64, hp, kt * 128:(kt + 1) * 128],
                                self.QT[j * 64:(j + 1) * 64, hp, rq * 64:(rq + 1) * 64],
                                [self.KTb[hp][kt // 4], self.QTb[hp][rq // 8]], True, True, signal=(cnt == n))

            for k in range(min(LA, len(steps))):
                qk(k)
            for k, (rq, hp) in enumerate(steps):
                if k + LA < len(steps):
                    qk(k + LA)
                kt0, nt, t0 = plans.pop(k)
                sr = SR[k % 3]
                pi = k % 3
                ptv = self.PTN[pi]
                sv = self.PS[:, sr[0] * 512: sr[0] * 512 + 1024].rearrange("p (j n) -> p j n", j=2)[:, :, 0:nt * 64]
                S_.op("act", lambda e: e.activation(out=ptv[:, :, 0:nt * 64], in_=sv, func=AF.Exp, scale=0.125),
                      reads=[self.PB[sr[0]], self.PB[sr[1]]], writes=[self.PTNb[pi]])
                S_.op("pool", lambda e: e.tensor_tensor(out=ptv[:, :, 0:nt * 64], in0=ptv[:, :, 0:nt * 64],
                                                        in1=self.W[:, 2 * hp:2 * hp + 2, t0 * 64:(t0 + nt) * 64], op=ALU.mult),
                      reads=[self.PTNb[pi], self.Wb], writes=[self.PTNb[pi]])
                ob = OB[(k // 4) % 2]
                for j in range(2):
                    for m in range(nt):
                        oc = ((rq % 4) * 2 + j) * 64
                        self.mm(self.bank(ob)[:, oc:oc + 64], [self.PB[ob]], self.VA[:, kt0 + m, 2 * hp + j, :],
                                ptv[:, j, m * 64:(m + 1) * 64], [self.VAb[kt0 + m], self.PTNb[pi]], m == 0, m == nt - 1,
                                signal=(j == 1 and m == nt - 1))
                if k % 4 == 3:
                    r = self.rot("rec", 2)
                    rq0 = rq - 3
                    qs = slice(rq0 * 64, rq0 * 64 + 256)
                    blk = rq0 // 8
                    ov = self.bank(ob).rearrange("p (r j d) -> p r j d", r=4, j=2)
                    rv = self.REC[r].rearrange("p (r j d) -> p r j d", r=4, j=2)
                    S_.op("dve", lambda e: e.reciprocal(out=rv[64:128, :, 0, :], in_=ov[64:128, :, 0, :]),
                          reads=[self.PB[ob]], writes=[self.RECb[r]])
                    S_.op("dve", lambda e: e.reciprocal(out=rv[0:64, :, 1, :], in_=ov[0:64, :, 1, :]),
                          reads=[self.PB[ob]], writes=[self.RECb[r]])
                    S_.op("dve", lambda e: e.tensor_tensor(out=self.MIX[0:64, 6 + hp, qs].rearrange("p (r d) -> p r d", r=4),
                                                           in0=ov[0:64, :, 0, :], in1=rv[64:128, :, 0, :], op=ALU.mult),
                          reads=[self.PB[ob], self.RECb[r]], writes=[self.MIXb[6 + hp][blk]])
                    S_.op("dve", lambda e: e.tensor_tensor(out=self.MIX[64:128, 6 + hp, qs].rearrange("p (r d) -> p r d", r=4),
                                                           in0=ov[64:128, :, 1, :], in1=rv[0:64, :, 1, :], op=ALU.mult),
                          reads=[self.PB[ob], self.RECb[r]], writes=[self.MIXb[6 + hp][blk]])

        def na_wrap(tiles):
            self.PTNb = [Buf(), Buf(), Buf()]
            for i in range(3):
                for bb in self.PTb:
                    _merge(self.PTNb[i].r, bb.w)
                    _merge(self.PTNb[i].r, bb.r)
            na(tiles)

        items.append(Item([], na_wrap))

    def add_phase_b(self, items, li):
        S_ = self.S
        nl = len(self.layer_ids)
        lastl = li == nl - 1

        def start(tiles):
            S_.barrier()
            mixb = self.MIXb
            self.new_h_bufs()
            self.MIXb = mixb
            self._rsb = None
            for b in self.PB:
                b.w = {}
                b.r = {}
            self.load_h(self.hin if li == 0 else self.hs)

        items.append(Item([], start))

        def wo_all(tiles):
            rsw = [self.f32v(130 + 2 * i, 512) for i in range(2)]
            for blk in range(4):
                ts = slice(blk * 512, (blk + 1) * 512)
                for n in range(8):
                    t, tb = tiles[n // 3]
                    c0 = (n % 3) * 128
                    b = self.rot("pbank", 7)
                    for kc in range(8):
                        self.mm(self.bank(b), [self.PB[b]], t[:, kc, c0:c0 + 128], self.MIX[:, kc, ts], [tb, self.MIXb[kc][blk]],
                                kc == 0, kc == 7)
                    S_.op("dve", lambda e: e.tensor_tensor(out=self.H[:, n, ts], in0=self.bank(b), in1=self.H[:, n, ts], op=ALU.add),
                          reads=[self.PB[b], self.Hb[n][blk]], writes=[self.Hb[n][blk]])
                self.norm_s1(blk)
                if blk >= 1:
                    self.norm_s2(li, "ffn_norm", blk - 1, rsv=rsw)
            self.norm_s2(li, "ffn_norm", 3, rsv=rsw)

        items.append(Item([(li, "wo_a"), (li, "wo_b"), (li, "wo_c")], wo_all))

        def ffn_start(tiles):
            S_.barrier()
            for b in self.PB:
                b.w = {}
                b.r = {}
            self.ACTBb = [Buf() for _ in range(22)]
            self.CAb = [Buf() for _ in range(3)]
            self.CGb = [Buf() for _ in range(2)]
            self._rsb = None

        items.append(Item([], ffn_start))

        SLOTS = [(0, 1), (2, 3), (4, 5)]

        def mk_up(hf, c):
            def f(tiles):
                (t, tb), = tiles
                t0 = hf * 1024
                res = {}
                for part in range(2):
                    sl = SLOTS[self.rot("fslot", 3)]
                    for tb_ in range(2):
                        ts = slice(t0 + tb_ * 512, t0 + (tb_ + 1) * 512)
                        blk = (t0 + tb_ * 512) // 512
                        for kc in range(8):
                            self.mm(self.bank(sl[tb_]), [self.PB[sl[tb_]]], t[:, kc, part * 128:(part + 1) * 128],
                                    self.HN[:, kc, ts], [tb, self.HNb[kc][blk]], kc == 0, kc == 7)
                    hi = self.rot("halo", 16)
                    hbk = 6 + hi % 2
                    htok = 1024 if hf == 0 else 1023
                    hcol = self.bank(hbk)[:, hi * 2:hi * 2 + 1]
                    for kc in range(8):
                        self.mm(hcol, [self.PB[hbk]], t[:, kc, part * 128:(part + 1) * 128], self.HN[:, kc, htok:htok + 1],
                                [tb, self.HNb[kc][htok // 512]], kc == 0, kc == 7)
                    u = self.PS[:, sl[0] * 512: sl[0] * 512 + 1024]
                    ub = [self.PB[sl[0]], self.PB[sl[1]]]
                    ci = part * 22 + c
                    ai = self.rot("ca", 3)
                    A = self.CA[ai]
                    Ab = self.CAb[ai]
                    S_.op("act", lambda e: e.activation(out=A, in_=u, func=AF.Identity, bias=self.cv(li, "cb", ci),
                                                        scale=self.cv(li, "cw1", ci)), reads=ub + [self.cvb], writes=[Ab])
                    S_.op("dve", lambda e: e.scalar_tensor_tensor(out=A[:, 1:1024], in0=u[:, 0:1023], scalar=self.cv(li, "cw0", ci),
                                                                  in1=A[:, 1:1024], op0=ALU.mult, op1=ALU.add),
                          reads=ub + [self.cvb, Ab], writes=[Ab])
                    S_.op("dve", lambda e: e.scalar_tensor_tensor(out=A[:, 0:1023], in0=u[:, 1:1024], scalar=self.cv(li, "cw2", ci),
                                                                  in1=A[:, 0:1023], op0=ALU.mult, op1=ALU.add),
                          reads=ub + [self.cvb, Ab], writes=[Ab])
                    if hf == 0:
                        S_.op("dve", lambda e: e.scalar_tensor_tensor(out=A[:, 1023:1024], in0=hcol, scalar=self.cv(li, "cw2", ci),
                                                                      in1=A[:, 1023:1024], op0=ALU.mult, op1=ALU.add),
                              reads=[self.PB[hbk], self.cvb, Ab], writes=[Ab])
                    else:
                        S_.op("dve", lambda e: e.scalar_tensor_tensor(out=A[:, 0:1], in0=hcol, scalar=self.cv(li, "cw0", ci),
                                                                      in1=A[:, 0:1], op0=ALU.mult, op1=ALU.add),
                              reads=[self.PB[hbk], self.cvb, Ab], writes=[Ab])
                    res[part] = (A, Ab)
                    if part == 0:
                        gi = self.rot("cg", 2)
                        S_.op("act", lambda e: e.activation(out=self.CG[gi], in_=A, func=AF.Gelu_apprx_tanh),
                              reads=[Ab], writes=[self.CGb[gi]])
                        res["g"] = (self.CG[gi], self.CGb[gi])
                G, Gb = res["g"]
                Av, Avb = res[1]
                S_.op("pool", lambda e: e.tensor_tensor(out=self.ACTB[:, c, :], in0=G, in1=Av, op=ALU.mult),
                      reads=[Gb, Avb], writes=[self.ACTBb[c]])
            return f

        def mk_dn(hf, n):
            def f(tiles):
                (t, tb), = tiles
                for tb_ in range(2):
                    blk = hf * 2 + tb_
                    ts = slice(blk * 512, (blk + 1) * 512)
                    b = self.rot("dbank", 6)
                    for kc in range(22):
                        self.mm(self.bank(b), [self.PB[b]], t[:, kc, :], self.ACTB[:, kc, tb_ * 512:(tb_ + 1) * 512],
                                [tb, self.ACTBb[kc]], kc == 0, kc == 21)
                    S_.op("dve", lambda e: e.tensor_tensor(out=self.H[:, n, ts], in0=self.bank(b), in1=self.H[:, n, ts], op=ALU.add),
                          reads=[self.PB[b], self.Hb[n][blk]], writes=[self.Hb[n][blk]])
            return f

        for hf in range(2):
            for c in range(22):
                items.append(Item([(li, f"up{c}")], mk_up(hf, c)))
            for n in range(8):
                items.append(Item([(li, f"dn{n}")], mk_dn(hf, n)))

        def ple_start(tiles):
            S_.barrier()
            for b in self.PB:
                b.w = {}
                b.r = {}
            self.PBtb = Buf()
            self.PPWb = Buf()
            self.ETb = [Buf(), Buf()]
            self.CAb = [Buf() for _ in range(3)]
            S_.dma("pool", [(self.PBt[:, k, :], self.pT[li, k * 128:(k + 1) * 128, :]) for k in range(2)],
                   writes=[self.PBtb], key=("pbt",))
            o, kc, m = WT_OFF["pp"]
            S_.dma("pool", [(self.PPW, self.wpk[li, :, o:o + 2048].rearrange("p (k n) -> p k n", k=2))],
                   writes=[self.PPWb], key=("ppw",))
            self.rmsnorm_main(li, "ple_norm")

        items.append(Item([], ple_start))

        def ple_all(tiles):
            hs_v = (self.hout if lastl else self.hs).rearrange("(k p) t -> p k t", p=128)

            def post(blk):
                ts = slice(blk * 512, (blk + 1) * 512)
                if lastl and not self.do_final:
                    pass
                elif lastl:
                    self.norm_s2(li, None, blk, final=True, gcol=NLAYER * NCL)
                else:
                    self.norm_s2(li + 1, "attn_norm", blk)
                if lastl:
                    S_.dma("sp", [(hs_v[:, :, ts], self.H[:, :, ts])], reads=[self.Hb[k][blk] for k in range(8)], key=("hst",))

            for blk in range(4):
                ts = slice(blk * 512, (blk + 1) * 512)
                for n in range(8):
                    t, tb = tiles[n // 3]
                    c0 = (n % 3) * 128
                    bg = self.rot("pbank", 7)
                    be = self.rot("pbank", 7)
                    for kc in range(8):
                        self.mm(self.bank(bg), [self.PB[bg]], t[:, kc, c0:c0 + 128], self.HN[:, kc, ts], [tb, self.HNb[kc][blk]],
                                kc == 0, kc == 7)
                    for kc in range(2):
                        self.mm(self.bank(be), [self.PB[be]], self.PPW[:, kc, n * 128:(n + 1) * 128], self.PBt[:, kc, ts],
                                [self.PPWb, self.PBtb], kc == 0, kc == 1)
                    ei = self.rot("et", 2)
                    S_.op("act", lambda e: e.activation(out=self.ET[ei], in_=self.bank(bg), func=AF.Sigmoid),
                          reads=[self.PB[bg]], writes=[self.ETb[ei]])
                    S_.op("dve", lambda e: e.tensor_tensor(out=self.ET[ei], in0=self.bank(be), in1=self.ET[ei], op=ALU.mult),
                          reads=[self.PB[be], self.ETb[ei]], writes=[self.ETb[ei]])
                    S_.op("pool", lambda e: e.tensor_tensor(out=self.H[:, n, ts], in0=self.H[:, n, ts], in1=self.ET[ei], op=ALU.add),
                          reads=[self.ETb[ei], self.Hb[n][blk]], writes=[self.Hb[n][blk]])
                if not lastl:
                    S_.dma("sp", [(hs_v[:, :, ts], self.H[:, :, ts])], reads=[self.Hb[k][blk] for k in range(8)], key=("hst",))
                if not (lastl and not self.do_final):
                    self.norm_s1(blk)
                if blk >= 1:
                    post(blk - 1)
            post(3)

        items.append(Item([(li, "pg_a"), (li, "pg_b"), (li, "pg_c")], ple_all))

        def finish(tiles):
            S_.barrier()
            S_.epoch += 1

        items.append(Item([], finish))


_CACHE = {}


def _get_prog(layer_ids, src_is_x, do_final):
    key = (tuple(range(len(layer_ids))), do_final)
    if key not in _CACHE:
        _CACHE[key] = Builder(list(range(len(layer_ids))), src_is_x, do_final).nc
    return _CACHE[key]


def kernel(**inputs):
    inp = {k: np.asarray(v) for k, v in inputs.items()}
    x = inp["x"].astype(np.float32, copy=False)
    p = inp["p"].astype(np.float32, copy=False)
    B = x.shape[0]
    ncores = 8
    cv = _cvec(inp)
    ropes = _rope_tables()
    tm = _tmask()
    wpk_all = np.stack([_pack_layer(inp, l) for l in range(NLAYER)])
    bw_all = np.stack([_bias_w(np.asarray(inp["na_rpb"][l], np.float32)) for l in range(NLAYER)])
    lam_all = np.stack([np.concatenate([inp["lam_q1"][l], inp["lam_k1"][l], inp["lam_q2"][l], inp["lam_k2"][l]])
                        for l in range(NLAYER)]).astype(np.float32)

    def run(hT_list, layers, do_final):
        nc = _get_prog(layers, False, do_final)
        cvl = np.zeros_like(cv)
        for i, l in enumerate(layers):
            cvl[:, i * NCL:(i + 1) * NCL] = cv[:, l * NCL:(l + 1) * NCL]
        cvl[:, NLAYER * NCL:] = cv[:, NLAYER * NCL:]
        wl = np.ascontiguousarray(wpk_all[layers])
        bl = np.ascontiguousarray(bw_all[layers])
        ll = np.ascontiguousarray(lam_all[layers])
        in_maps = []
        for b in range(ncores):
            in_maps.append({
                "hin": hT_list[b],
                "pT": np.ascontiguousarray(p[layers, b].transpose(0, 2, 1)),
                "wpk": wl, "cvec": cvl, "lamv": ll, "biasw": bl, "tmask": tm,
                "ropeA": ropes["ropeA"], "ropeC": ropes["ropeC"],
            })
        res = run_bass_kernel_spmd(nc, in_maps, core_ids=list(range(ncores)))
        return [np.asarray(r["hout"]) for r in res.results]

    hT = [np.ascontiguousarray(x[b].T) for b in range(B)]
    if FUSED:
        hT = run(hT, list(range(NLAYER)), True)
    else:
        for l in range(NLAYER):
            hT = run(hT, [l], l == NLAYER - 1)
    out = np.stack([h.T for h in hT]).astype(np.float32)
    return out
```
